# Optimizing a Trainium2 kernel written in Bass

```python
import math
import jax, jax.numpy as jnp
from jax import lax
import numpy as np

D_MODEL = 1024
BATCH = 8
SEQ = 2048
DEPTH = 1
DEC_BATCH = 128
DEC_SEQ = 4
PAST_LEN = 16384
PAGE_SIZE = 128

D_SSM = D_MODEL // 2
SSM_GROUP = 16
N_SSM_GROUPS = D_SSM // SSM_GROUP
SSM_STATE = 64
DT_MIN = 1e-3
DT_MAX = 1e-1
D_POOL = D_MODEL - D_SSM
POOL_WINDOWS = (2, 4, 8, 16)
N_POOL_GROUPS = len(POOL_WINDOWS)
POOL_GROUP = D_POOL // N_POOL_GROUPS
POOL_OUT = D_MODEL // N_POOL_GROUPS
POOL_HIST = max(POOL_WINDOWS) - 1
D_FF = -(-8 * D_MODEL // (3 * 256)) * 256
N_MOD = 6
EPS = 1e-6

kernel_name = "s5_pool_gated_hybrid_decode_step"


def rmsnorm(x, g):
    xf = x.astype(jnp.float32)
    y = xf * lax.rsqrt(jnp.mean(xf * xf, axis=-1, keepdims=True) + EPS)
    return (y * g.astype(jnp.float32)).astype(x.dtype)


def modulate(h, shift, scale):
    return h * (1.0 + scale[:, None, :]) + shift[:, None, :]


def cmul(ar, ai, br, bi):
    return ar * br - ai * bi, ar * bi + ai * br


def ssm_combine(e1, e2):
    a1r, a1i, b1r, b1i = e1
    a2r, a2i, b2r, b2i = e2
    ar, ai = cmul(a1r, a1i, a2r, a2i)
    br, bi = cmul(a2r, a2i, b1r, b1i)
    return ar, ai, br + b2r, bi + b2i


def s5_mixer(u, h0_re, h0_im, lam_re, lam_im, log_dt, b_re, b_im, c_re, c_im, d_skip):
    bsz, t_len, _ = u.shape
    uf = u.astype(jnp.float32)
    ug = uf.reshape(bsz, t_len, N_SSM_GROUPS, SSM_GROUP)
    lr = lam_re.astype(jnp.float32)
    li = lam_im.astype(jnp.float32)
    dt = jnp.exp(log_dt.astype(jnp.float32))[:, None]
    mag = jnp.exp(lr * dt)
    abar_r = mag * jnp.cos(li * dt)
    abar_i = mag * jnp.sin(li * dt)
    den = lr * lr + li * li
    nr = abar_r - 1.0
    ni = abar_i
    f_r = (nr * lr + ni * li) / den
    f_i = (ni * lr - nr * li) / den
    bb_r, bb_i = cmul(f_r[..., None], f_i[..., None],
                      b_re.astype(jnp.float32), b_im.astype(jnp.float32))
    bu_r = jnp.einsum('btgc,gpc->btgp', ug, bb_r)
    bu_i = jnp.einsum('btgc,gpc->btgp', ug, bb_i)
    h0r, h0i = cmul(abar_r, abar_i, h0_re.astype(jnp.float32), h0_im.astype(jnp.float32))
    bu_r = bu_r.at[:, 0].add(h0r)
    bu_i = bu_i.at[:, 0].add(h0i)
    a_r = jnp.broadcast_to(abar_r, (1, t_len, N_SSM_GROUPS, SSM_STATE))
    a_i = jnp.broadcast_to(abar_i, (1, t_len, N_SSM_GROUPS, SSM_STATE))
    _, _, s_r, s_i = lax.associative_scan(ssm_combine, (a_r, a_i, bu_r, bu_i), axis=1)
    y = (jnp.einsum('btgp,gcp->btgc', s_r, c_re.astype(jnp.float32))
         - jnp.einsum('btgp,gcp->btgc', s_i, c_im.astype(jnp.float32)))
    y = y.reshape(bsz, t_len, D_SSM) + d_skip.astype(jnp.float32) * uf
    return y, s_r[:, -1], s_i[:, -1]


def pool_mixer(u, buf, pos, w_pool, pool_scale):
    bsz, t_len, _ = u.shape
    uf = u.astype(jnp.float32)
    ext = jnp.concatenate([buf.astype(jnp.float32), uf], axis=1)
    cs = jnp.cumsum(jnp.pad(ext, ((0, 0), (1, 0), (0, 0))), axis=1)
    top = cs[:, POOL_HIST + 1:]
    groups = []
    for k, w in enumerate(POOL_WINDOWS):
        sl = slice(k * POOL_GROUP, (k + 1) * POOL_GROUP)
        lo = cs[:, POOL_HIST + 1 - w:POOL_HIST + 1 - w + t_len, sl]
        count = jnp.minimum(pos + 1, w).astype(jnp.float32)[None, :, None]
        groups.append((top[..., sl] - lo) / count)
    pooled = jnp.concatenate(groups, axis=-1) - uf
    z = jnp.einsum('btgc,gco->btgo',
                   pooled.reshape(bsz, t_len, N_POOL_GROUPS, POOL_GROUP),
                   w_pool.astype(jnp.float32)).reshape(bsz, t_len, D_MODEL)
    z = z * pool_scale.astype(jnp.float32)
    new_buf = ext[:, -POOL_HIST:].astype(u.dtype)
    return z, new_buf


def hybrid_layer(x, c, pos, h0_re, h0_im, pool_buf,
                 norm1_g, norm2_g, w_ada, b_ada, w_in,
                 ssm_lam_re, ssm_lam_im, ssm_log_dt, ssm_b_re, ssm_b_im, ssm_c_re, ssm_c_im, ssm_d,
                 w_glu, w_pool, pool_scale, w_out, w_ffn_in, w_ffn_out):
    bsz = c.shape[0]
    mod = (jax.nn.silu(c.astype(jnp.float32)) @ w_ada + b_ada).reshape(bsz, N_MOD, D_MODEL)
    shift1, scale1, gate1 = mod[:, 0], mod[:, 1], mod[:, 2]
    shift2, scale2, gate2 = mod[:, 3], mod[:, 4], mod[:, 5]

    h = modulate(rmsnorm(x, norm1_g), shift1, scale1)
    proj = h @ w_in
    u_ssm, u_pool, g_ssm, g_pool = jnp.split(
        proj, [D_SSM, D_SSM + D_POOL, D_SSM + D_POOL + D_MODEL], axis=-1)
    y_ssm, h_re, h_im = s5_mixer(u_ssm, h0_re, h0_im, ssm_lam_re, ssm_lam_im, ssm_log_dt,
                                 ssm_b_re, ssm_b_im, ssm_c_re, ssm_c_im, ssm_d)
    glu_a, glu_b = jnp.split(jax.nn.gelu(y_ssm) @ w_glu, 2, axis=-1)
    br_ssm = glu_a * jax.nn.sigmoid(glu_b)
    br_pool, new_buf = pool_mixer(u_pool, pool_buf, pos, w_pool, pool_scale)
    merged = jax.nn.sigmoid(g_ssm) * br_ssm + jax.nn.sigmoid(g_pool) * br_pool
    x = x + gate1[:, None, :] * (merged @ w_out)

    h2 = modulate(rmsnorm(x, norm2_g), shift2, scale2)
    f_a, f_b = jnp.split(h2 @ w_ffn_in, 2, axis=-1)
    x = x + gate2[:, None, :] * ((jax.nn.silu(f_a) * f_b) @ w_ffn_out)
    return x, h_re, h_im, new_buf


def trunk(x, c, pos, st_re, st_im, st_pool, normf_g, params):
    (norm1_g, norm2_g, w_ada, b_ada, w_in, ssm_lam_re, ssm_lam_im, ssm_log_dt, ssm_b_re, ssm_b_im,
     ssm_c_re, ssm_c_im, ssm_d, w_glu, w_pool, pool_scale, w_out, w_ffn_in, w_ffn_out) = params
    out_re, out_im, out_pool = [], [], []
    for l in range(DEPTH):
        x, h_re, h_im, nb = hybrid_layer(
            x, c, pos, st_re[l], st_im[l], st_pool[l],
            norm1_g[l], norm2_g[l], w_ada[l], b_ada[l], w_in[l],
            ssm_lam_re[l], ssm_lam_im[l], ssm_log_dt[l], ssm_b_re[l], ssm_b_im[l],
            ssm_c_re[l], ssm_c_im[l], ssm_d[l], w_glu[l], w_pool[l], pool_scale[l],
            w_out[l], w_ffn_in[l], w_ffn_out[l])
        out_re.append(h_re)
        out_im.append(h_im)
        out_pool.append(nb)
    y = rmsnorm(x, normf_g)
    return y, jnp.stack(out_re), jnp.stack(out_im), jnp.stack(out_pool)


def setup_inputs(seed: int = 0) -> dict:
    key = jax.random.key(seed)
    ks = jax.random.split(key, 32)
    f32 = jnp.float32
    G, P, C = N_SSM_GROUPS, SSM_STATE, SSM_GROUP
    nrm = lambda k, shape, s: jax.random.normal(k, shape, f32) * s
    n_idx = jnp.arange(P, dtype=f32)
    return {
        "x_prompt": nrm(ks[0], (BATCH, SEQ, D_MODEL), 1.0),
        "x_sample": nrm(ks[1], (DEC_BATCH, DEC_SEQ, D_MODEL), 1.0),
        "c_prompt": nrm(ks[2], (BATCH, D_MODEL), 1.0),
        "c_sample": nrm(ks[3], (DEC_BATCH, D_MODEL), 1.0),
        "state_ssm_re": nrm(ks[4], (DEPTH, DEC_BATCH, G, P), 0.3),
        "state_ssm_im": nrm(ks[5], (DEPTH, DEC_BATCH, G, P), 0.3),
        "state_pool": nrm(ks[6], (DEPTH, DEC_BATCH, POOL_HIST, D_POOL), 1.0),
        "norm1_g": 1.0 + nrm(ks[7], (DEPTH, D_MODEL), 0.05),
        "norm2_g": 1.0 + nrm(ks[8], (DEPTH, D_MODEL), 0.05),
        "normf_g": 1.0 + nrm(ks[9], (D_MODEL,), 0.05),
        "w_ada": nrm(ks[10], (DEPTH, D_MODEL, N_MOD * D_MODEL), 0.5 * D_MODEL ** -0.5),
        "b_ada": nrm(ks[11], (DEPTH, N_MOD * D_MODEL), 0.02),
        "w_in": nrm(ks[12], (DEPTH, D_MODEL, D_SSM + D_POOL + 2 * D_MODEL), D_MODEL ** -0.5),
        "ssm_lam_re": -0.5 + nrm(ks[13], (DEPTH, G, P), 0.01),
        "ssm_lam_im": jnp.pi * n_idx + nrm(ks[14], (DEPTH, G, P), 0.01),
        "ssm_log_dt": jax.random.uniform(ks[15], (DEPTH, G), f32,
                                         minval=math.log(DT_MIN), maxval=math.log(DT_MAX)),
        "ssm_b_re": nrm(ks[16], (DEPTH, G, P, C), (2 * C) ** -0.5),
        "ssm_b_im": nrm(ks[17], (DEPTH, G, P, C), (2 * C) ** -0.5),
        "ssm_c_re": nrm(ks[18], (DEPTH, G, C, P), P ** -0.5),
        "ssm_c_im": nrm(ks[19], (DEPTH, G, C, P), P ** -0.5),
        "ssm_d": nrm(ks[20], (DEPTH, D_SSM), 1.0),
        "w_glu": nrm(ks[21], (DEPTH, D_SSM, 2 * D_MODEL), D_SSM ** -0.5),
        "w_pool": nrm(ks[22], (DEPTH, N_POOL_GROUPS, POOL_GROUP, POOL_OUT), POOL_GROUP ** -0.5),
        "pool_scale": 1.0 + nrm(ks[23], (DEPTH, D_MODEL), 0.1),
        "w_out": nrm(ks[24], (DEPTH, D_MODEL, D_MODEL), D_MODEL ** -0.5),
        "w_ffn_in": nrm(ks[25], (DEPTH, D_MODEL, 2 * D_FF), D_MODEL ** -0.5),
        "w_ffn_out": nrm(ks[26], (DEPTH, D_FF, D_MODEL), D_FF ** -0.5),
    }


def reference(x_prompt, x_sample, c_prompt, c_sample, state_ssm_re, state_ssm_im, state_pool,
              norm1_g, norm2_g, normf_g, w_ada, b_ada, w_in,
              ssm_lam_re, ssm_lam_im, ssm_log_dt, ssm_b_re, ssm_b_im, ssm_c_re, ssm_c_im, ssm_d,
              w_glu, w_pool, pool_scale, w_out, w_ffn_in, w_ffn_out):
    params = (norm1_g, norm2_g, w_ada, b_ada, w_in, ssm_lam_re, ssm_lam_im, ssm_log_dt,
              ssm_b_re, ssm_b_im, ssm_c_re, ssm_c_im, ssm_d, w_glu, w_pool, pool_scale,
              w_out, w_ffn_in, w_ffn_out)
    n_prompt, t_prompt = x_prompt.shape[0], x_prompt.shape[1]
    zero_ssm = jnp.zeros((DEPTH, n_prompt, N_SSM_GROUPS, SSM_STATE), jnp.float32)
    zero_pool = jnp.zeros((DEPTH, n_prompt, POOL_HIST, D_POOL), x_prompt.dtype)
    pos_prompt = jnp.arange(t_prompt, dtype=jnp.int32)
    y_prompt, p_ssm_re, p_ssm_im, p_pool = trunk(
        x_prompt, c_prompt, pos_prompt, zero_ssm, zero_ssm, zero_pool, normf_g, params)
    pos_sample = PAST_LEN + jnp.arange(x_sample.shape[1], dtype=jnp.int32)
    y_sample, s_ssm_re, s_ssm_im, s_pool = trunk(
        x_sample, c_sample, pos_sample, state_ssm_re, state_ssm_im, state_pool, normf_g, params)
    return (y_prompt, y_sample, p_ssm_re, p_ssm_im, p_pool, s_ssm_re, s_ssm_im, s_pool)
```

```python
import math
import numpy as np
from contextlib import ExitStack
import concourse.bass as bass
import concourse.mybir as mybir
from concourse.bass_utils import run_bass_kernel_spmd

F32 = mybir.dt.float32
BF16 = mybir.dt.bfloat16
AF = mybir.ActivationFunctionType
ALU = mybir.AluOpType
D = 1024; DFF = 2816; G = 32; NPST = 64; EPS = 1e-6
SELFSYNC = True
RAW_ONLY = True
DBG = []


class _Rec:
    def __getattr__(self, name):
        return lambda *a, **k: (name, a, k)


C = _Rec()


class Sem:
    def __init__(self, h):
        self.h = h; self.v = 0


_DEFW = [None]


class T:
    def __init__(self, name=""):
        self.w = _DEFW[0]; self.r = []; self.name = name


class Prog:
    EMAP = dict(pe="tensor", act="scalar", dve="vector", pool="gpsimd", sp="sync")

    def __init__(self, nc, es):
        self.nc = nc; self.es = es
        self.q = {e: [] for e in self.EMAP}
        self.cnt = {e: 0 for e in self.EMAP}
        self.sem = {e: Sem(es.enter_context(nc.semaphore("s_" + e))) for e in self.EMAP}
        self.waited = {e: {} for e in self.EMAP}
        self.nsb = 0; self.es2 = None; self.stopped = False

    def sb(self, shape, dt, name=None):
        self.nsb += 1
        es = self.es2 if (self.es2 is not None) else self.es
        return es.enter_context(self.nc.sbuf_tensor(name or ("sb%d" % self.nsb), list(shape), dt))

    def ps(self, shape, dt, name):
        return self.es.enter_context(self.nc.psum_tensor(name, list(shape), dt))

    def newsem(self, name):
        return Sem(self.es.enter_context(self.nc.semaphore(name)))

    def op(self, eng, fn, reads=(), writes=(), dma=None, sig=True):
        if self.stopped:
            return (self.sem['dve'], 0)
        deps = set()
        own = self.sem[eng]
        if dma is not None and getattr(dma, 'guard', 0) > 0:
            deps.add((dma, dma.guard))
        for t in reads:
            if t.w is not None: deps.add(t.w)
        for t in writes:
            if t.w is not None and not (RAW_ONLY and t.w[0] is own and dma is None): deps.add(t.w)
            for ev_ in t.r:
                if not (RAW_ONLY and ev_[0] is own and dma is None): deps.add(ev_)
        if dma is not None:
            dma.v += 16; ev = (dma, dma.v); inc = (dma, 16)
        elif sig:
            self.cnt[eng] += 1; ev = (own, self.cnt[eng]); inc = (own, 1)
        else:
            ev = (own, self.cnt[eng] + 1); inc = None
        waits = []
        for (s, v) in sorted(deps, key=lambda sv: id(sv[0])):
            if s is own and (eng == "pe" or not SELFSYNC or dma is not None and False):
                continue
            if s is own and dma is None and v >= ev[1]:
                continue
            if self.waited[eng].get(s, 0) >= v: continue
            self.waited[eng][s] = v; waits.append((s, v))
        self.q[eng].append((fn, waits, inc))
        for t in reads: t.r.append(ev)
        for t in writes:
            t.w = ev; t.r = []
        return ev

    def emit(self, final_waits):
        with self.nc.Block() as block:
            for eng, ename in self.EMAP.items():
                ops = self.q[eng]
                extra = final_waits if eng == "sp" else []
                if not ops and not extra: continue
                def body(e, ops=ops, extra=extra):
                    for fn, waits, inc in ops:
                        for (s, v) in waits:
                            e.wait_ge(s.h, v)
                        ins = getattr(e, fn[0])(*fn[1], **fn[2])
                        if inc is not None:
                            ins.then_inc(inc[0].h, inc[1])
                    for (s, v) in extra:
                        e.wait_ge(s.h, v)
                getattr(block, ename)(body)


def build_nc(stop_at=None, dbg=()):
    nc = bass.Bass("TRN2", target_bir_lowering=False)
    def din(name, shape): return nc.dram_tensor(name, list(shape), F32, kind="ExternalInput").ap()
    def dout(name, shape): return nc.dram_tensor(name, list(shape), F32, kind="ExternalOutput").ap()
    xp = din("xp", [2048, D]); xs = din("xs", [64, D]); cc = din("cc", [17, D])
    st_re = din("st_re", [16, 2048]); st_im = din("st_im", [16, 2048]); st_pool = din("st_pool", [240, 512])
    norm1_g = din("norm1_g", [D]); norm2_g = din("norm2_g", [D]); normf_g = din("normf_g", [D])
    w_ada = din("w_ada", [D, 6 * D]); b_ada = din("b_ada", [6 * D]); w_in = din("w_in", [D, 3072])
    lam_re = din("lam_re", [G, 64]); lam_im = din("lam_im", [G, 64]); log_dt = din("log_dt", [1, G])
    b_re = din("b_re", [G, 64, 16]); b_im = din("b_im", [G, 64, 16]); c_re = din("c_re", [512, 64]); c_im = din("c_im", [512, 64])
    ssm_d = din("ssm_d", [512]); w_glu = din("w_glu", [512, 2048]); w_pool = din("w_pool", [4, 128, 256])
    pool_scale = din("pool_scale", [D]); w_out = din("w_out", [D, D]); w_ffn_in = din("w_ffn_in", [D, 2 * DFF]); w_ffn_out = din("w_ffn_out", [DFF, D])
    yp = dout("yp", [2048, D]); ys = dout("ys", [64, D]); p_re = dout("p_re", [2048]); p_im = dout("p_im", [2048])
    p_pool = dout("p_pool", [15, 512]); s_re = dout("s_re", [16, 2048]); s_im = dout("s_im", [16, 2048]); s_pool = dout("s_pool", [240, 512])
    Ud = nc.dram_tensor("Ud", [512, 8, 64], BF16, kind="Internal").ap()
    Yd = nc.dram_tensor("Yd", [512, 8, 64], BF16, kind="Internal").ap()
    dbg_outs = {}

    with ExitStack() as es:
        es.enter_context(nc.allow_non_contiguous_dma(reason="small strided parameter loads"))
        P = Prog(nc, es)
        out_events = []
        def ckpt(name):
            if stop_at == name:
                P.stopped = True
        def tap(name, ap, t):
            if name in dbg and not P.stopped:
                d = nc.dram_tensor("dbg_" + name, list(ap.shape), ap.dtype, kind="ExternalOutput").ap()
                out_events.append(store("pool", d, ap, t))
        banks = [(P.ps([128, 512], F32, "bank%d" % i), T("bank%d" % i)) for i in range(8)]
        bank_i = [0]
        NBM = [8]
        def nb():
            b = banks[bank_i[0] % NBM[0]]; bank_i[0] += 1
            return b
        banke_i = [0]
        def nb_e():
            b = banks[6 + banke_i[0] % 2]; banke_i[0] += 1
            return b
        dpool = [P.newsem("dp%d" % i) for i in range(20)]
        lsem = [0]
        def psem():
            lsem[0] += 1
            sm = dpool[lsem[0] % len(dpool)]
            sm.guard = sm.v
            return sm
        dpool2 = [P.newsem("dq%d" % i) for i in range(8)]
        lsem2 = [0]
        def psem2():
            lsem2[0] += 1
            sm = dpool2[lsem2[0] % len(dpool2)]
            sm.guard = sm.v
            return sm
        def psem_q(q):
            return psem2() if q == "sp" else psem()
        def load(eng, out_ap, in_ap, t, reads=()):
            s = psem_q(eng)
            return P.op(eng, C.dma_start(out=out_ap, in_=in_ap), reads=reads, writes=[t], dma=s)
        stsem = [0]
        def store(eng, out_ap, in_ap, t):
            s = psem()
            ev = P.op(eng, C.dma_start(out=out_ap, in_=in_ap), reads=[t], dma=s)
            return ev
        ident = P.sb([128, 128], F32, "ident"); t_ident = T()
        onesf = P.sb([128, 128], F32, "onesf"); t_ones = T()
        onesb = P.sb([128, 128], BF16, "onesb")
        mask = P.sb([128, 128], F32, "mask"); t_mask = T()
        P.op("dve", C.memset(onesf[:], 1.0), writes=[t_ones])
        P.op("dve", C.memset(onesb[:], 1.0), writes=[t_ones])
        P.op("pool", C.affine_select(out=ident[:], in_=onesf[:], pattern=[[-1, 128]], compare_op=ALU.is_equal, fill=0.0, base=0, channel_multiplier=1), reads=[t_ones], writes=[t_ident])
        P.op("pool", C.affine_select(out=mask[:].rearrange("p (i c) -> p i c", i=8), in_=onesf[:].rearrange("p (i c) -> p i c", i=8), pattern=[[16, 8], [0, 16]], compare_op=ALU.is_ge, fill=0.0, base=15, channel_multiplier=-1), reads=[t_ones], writes=[t_mask])

        NSLOT = 3
        wslots = [(P.sb([128, 8, 256], BF16, "wslot%d" % i), T(), P.newsem("wsem%d" % i)) for i in range(NSLOT)]
        ws_i = [0]
        wsems_sw = [P.newsem("wsemsw%d" % i) for i in range(NSLOT)]
        def wload_direct(mk):
            sl, t, s_ = wslots[ws_i[0] % NSLOT]; s_ = wsems_sw[ws_i[0] % NSLOT]; ws_i[0] += 1
            for (o, i_) in mk(sl):
                P.op("pool", C.dma_start(out=o, in_=i_), writes=[t], dma=s_)
            return sl, t
        w2slots = [(P.sb([128, 11, 128], BF16, "w2slot%d" % i), T(), P.newsem("w2sem%d" % i)) for i in range(2)]
        w2_i = [0]
        NSL = 46
        Wd = nc.dram_tensor("Wd", [NSL, 128, 2048], BF16, kind="Internal").ap()
        Wd2 = nc.dram_tensor("Wd2", [16, 128, 1408], BF16, kind="Internal").ap()
        wreg = {}
        def precast(key, mk, kind="w"):
            idx = sum(1 for v_ in wreg.values() if v_[0] == kind)
            tw = T()
            view = Wd[idx].rearrange("p (k c) -> p k c", k=8) if kind == "w" else Wd2[idx].rearrange("p (k c) -> p k c", k=11)
            sm = psem()
            for (o, i_) in mk(view):
                P.op("pool", C.dma_start(out=o, in_=i_), writes=[tw], dma=sm)
            wreg[key] = (kind, idx, tw)
        def wload(key, mk, kind="w"):
            if kind == "w":
                sl, t, s_ = wslots[ws_i[0] % NSLOT]; ws_i[0] += 1
            else:
                sl, t, s_ = w2slots[w2_i[0] % 2]; w2_i[0] += 1
            flat = sl[:].rearrange("p k c -> p (k c)")
            if key not in wreg:
                idx = sum(1 for v_ in wreg.values() if v_[0] == kind)
                tw = T()
                dst = Wd[idx] if kind == "w" else Wd2[idx]
                for (o, i_) in mk(sl):
                    P.op("pool", C.dma_start(out=o, in_=i_), writes=[t], dma=psem())
                P.op("sp", C.dma_start(out=dst, in_=flat), reads=[t], writes=[tw], dma=psem2())
                wreg[key] = (kind, idx, tw)
            else:
                kind_, idx, tw = wreg[key]
                src = Wd[idx] if kind == "w" else Wd2[idx]
                if key[0] == "glu":
                    P.op("sp", C.dma_start(out=flat[:, 0:1024], in_=src[:, 0:1024]), reads=[tw], writes=[t], dma=s_)
                else:
                    P.op("sp", C.dma_start(out=flat, in_=src), reads=[tw], writes=[t], dma=s_)
            return sl, t
        PW = P.sb([64, 2, G, 9], F32, "PW"); NPW = P.sb([64, 2, G, 8], F32, "NPW")
        Wend = P.sb([128, G, 128], BF16, "Wend"); Kloc = P.sb([128, G, 128], BF16, "Kloc")
        Wst = P.sb([128, G, 128], BF16, "Wst")
        AR = P.sb([64, 2, G], F32, "AR"); AIs = P.sb([64, 2, G], F32, "AIs")
        RHO = P.sb([64, G], F32, "RHO"); ROT = P.sb([64, 2, G, 64], F32, "ROT")
        D8 = P.sb([128, G], F32, "D8")
        mod = P.sb([128, 48, 17], F32, "mod")
        A1 = P.sb([128, 8, 17], F32, "A1"); A2 = P.sb([128, 8, 17], F32, "A2")
        siluT = P.sb([128, 8, 17], BF16, "siluT")
        bada = P.sb([128, 48], F32, "bada")
        g1 = P.sb([128, 8], F32, "g1"); g2 = P.sb([128, 8], F32, "g2"); gf = P.sb([128, 8], F32, "gf")
        psc = P.sb([128, 8], F32, "psc"); dsk = P.sb([128, 4], F32, "dsk")
        NTT = 576
        SBLK = 1
        hB = P.sb([128, 8, NTT], BF16, "hB"); t_hB = T()
        tqe_bufs = [(P.sb([128, NTT], F32, "tqe%d" % i), T()) for i in range(2)]
        tqe_i = [0]
        def tq_e():
            b = tqe_bufs[tqe_i[0] % 2]; tqe_i[0] += 1
            return b
        usm = P.sb([128, 4, NTT], BF16, "usm"); t_usm = T()
        pooled = P.sb([128, 4, NTT], BF16, "pooled"); t_pooled = T()
        U8 = P.sb([128, G, 64], BF16, "U8"); t_U8 = T()
        U8s = P.sb([128, G, 16], BF16, "U8s"); t_U8s = T()
        stage = [(P.sb([128, D], F32, "stage%d" % i), T()) for i in range(2)]
        st_i = [0]
        s1 = P.sb([128, 8, 66], F32, "s1"); s2 = P.sb([128, 8, 66], F32, "s2"); t_s = T()
        P.op("dve", C.memset(s1[:].rearrange("p a b -> p (a b)"), 0.0), writes=[t_s])
        P.op("dve", C.memset(s2[:].rearrange("p a b -> p (a b)"), 0.0), writes=[t_s])
        E1 = P.sb([128, 8, 66], F32, "E1"); t_E1 = T()
        Ehist = P.sb([128, 4, 8, 2], F32, "Ehist"); t_Eh = T()
        P.op("dve", C.memset(Ehist[:].rearrange("p a b c -> p (a b c)"), 0.0), writes=[t_Eh])
        PPt = P.sb([128, 64], F32, "PPt"); t_PPt = T()
        ssb = [(P.sb([128, 4], F32, "ssb%d" % i), T()) for i in range(2)]
        ss_i = [0]
        t_Ud = T(); t_Yd = T(); t_Uds = T(); t_Yds = T()
        Uds = nc.dram_tensor("Uds", [512, 8, 16], BF16, kind="Internal").ap()
        Yds = nc.dram_tensor("Yds", [512, 8, 16], BF16, kind="Internal").ap()
        Es = P.sb([128, 4, 16, 19], F32, "Es"); t_Es = T()
        epsc = P.sb([128, 1], F32, "epsc"); t_eps = T()
        P.op("dve", C.memset(epsc[:], EPS), writes=[t_eps])
        xpv = xp.rearrange("(n j) f -> j n f", j=8)
        xsv = xs.rearrange("(s t) f -> t s f", t=4)
        ypv = yp.rearrange("(n j) f -> j n f", j=8)
        ysv = ys.rearrange("(s t) f -> t s f", t=4)
        win_v = w_in.rearrange("(k p) n -> p k n", p=128)
        wglu_v = w_glu.rearrange("(k p) n -> p k n", p=128)
        wout_v = w_out.rearrange("(k p) n -> p k n", p=128)
        wfi_v = w_ffn_in.rearrange("(k p) n -> p k n", p=128)
        wfo_v = w_ffn_out.rearrange("(k p) n -> p k n", p=128)
        es2 = ExitStack(); P.es2 = es2
        t_mod = T(); t_A = T(); t_silu = T(); t_bada = T(); t_g = T(); t_c = T(); t_sg = T()
        csb = P.sb([128, D], F32, "csb")
        P.op("dve", C.memset(csb[:], 0.0), writes=[t_c])
        load("sp", csb[0:17, :], cc[:, :], t_c)
        load("sp", bada[:], b_ada.rearrange("(m p) -> p m", p=128), t_bada)
        load("sp", g1[:], norm1_g.rearrange("(m p) -> p m", p=128), t_g)
        load("sp", g2[:], norm2_g.rearrange("(m p) -> p m", p=128), t_g)
        load("sp", gf[:], normf_g.rearrange("(m p) -> p m", p=128), t_g)
        load("sp", psc[:], pool_scale.rearrange("(m p) -> p m", p=128), t_g)
        sg = P.sb([128, 8, 17], F32, "sg")
        cT = P.sb([128, 8, 17], F32, "cT")
        for half in range(2):
            bk, tb = nb()
            for f4 in range(4):
                fc = half * 4 + f4
                P.op("pe", C.transpose(out=bk[:, f4 * 128:(f4 + 1) * 128], in_=csb[:, fc * 128:(fc + 1) * 128], identity=ident[:]), reads=[t_c, t_ident], writes=[tb], sig=(f4 == 3))
            P.op("act", C.activation(out=sg[:, half * 4:half * 4 + 4, :], in_=bk[:, :].rearrange("p (a b) -> p a b", a=4)[:, :, 0:17], func=AF.Sigmoid), reads=[tb], writes=[t_sg])
            P.op("dve", C.tensor_copy(out=cT[:, half * 4:half * 4 + 4, :], in_=bk[:, :].rearrange("p (a b) -> p a b", a=4)[:, :, 0:17]), reads=[tb], writes=[t_sg])
        P.op("dve", C.tensor_tensor(out=siluT[:], in0=sg[:], in1=cT[:], op=ALU.mult), reads=[t_sg], writes=[t_silu])
        wada_v = w_ada.rearrange("(k p) n -> p k n", p=128)
        def adaln_slices(lo, hi):
            for sl_i in range(lo, hi):
                sl, tsl = wload_direct(lambda sl, sl_i=sl_i: [(sl[:, :, :], wada_v[:, :, sl_i * 256:(sl_i + 1) * 256])])
                bk, tb = nb()
                for sub in range(2):
                    for k in range(8):
                        P.op("pe", C.matmul(bk[:, sub * 17:(sub + 1) * 17], lhsT=sl[:, k, sub * 128:(sub + 1) * 128], rhs=siluT[:, k, :], start=(k == 0), stop=(k == 7)),
                             reads=[tsl, t_silu], writes=[tb], sig=(k == 7))
                for sub in range(2):
                    mc = sl_i * 2 + sub
                    P.op("act", C.activation(out=mod[:, mc, :], in_=bk[:, sub * 17:(sub + 1) * 17], func=AF.Identity, bias=bada[:, mc:mc + 1]), reads=[tb, t_bada], writes=[t_mod])
        def adaln_A(A, g_, kind):
            P.op("dve", C.tensor_scalar(out=A[:], in0=mod[:, kind * 8:(kind + 1) * 8, :], scalar1=1.0, scalar2=None, op0=ALU.add), reads=[t_mod], writes=[t_A])
            P.op("dve", C.tensor_tensor(out=A[:], in0=A[:], in1=g_[:].unsqueeze(2).to_broadcast([128, 8, 17]), op=ALU.mult), reads=[t_g, t_A], writes=[t_A])
        adaln_slices(0, 24)
        t_ssm = T()
        lr = P.sb([64, G], F32, "lr"); li = P.sb([64, G], F32, "li"); dtr = P.sb([128, G], F32, "dtr"); dtb = P.sb([64, G], F32, "dtb")
        load("sp", lr[:], lam_re.rearrange("g p -> p g"), t_ssm)
        load("sp", li[:], lam_im.rearrange("g p -> p g"), t_ssm)
        P.op("dve", C.memset(dtr[:], 0.0), writes=[t_ssm])
        load("sp", dtr[0:1, :], log_dt[:, :], t_ssm)
        load("sp", dsk[:], ssm_d.rearrange("(m p) -> p m", p=128), t_g)
        P.op("act", C.activation(out=dtr[0:1, :], in_=dtr[0:1, :], func=AF.Exp), reads=[t_ssm], writes=[t_ssm])
        bk, tb = nb()
        P.op("pe", C.matmul(bk[0:64, 0:G], lhsT=onesf[:, 0:64], rhs=dtr[:, :], start=True, stop=True), reads=[t_ssm, t_ones], writes=[tb])
        P.op("dve", C.tensor_copy(out=dtb[:], in_=bk[0:64, 0:G]), reads=[tb], writes=[t_ssm])
        def sbt(shape, name): return P.sb(shape, F32, name)
        lrdt = sbt([64, G], "lrdt"); ang = sbt([64, G], "ang"); mag = sbt([64, G], "mag"); cs = sbt([64, G], "cs"); sn = sbt([64, G], "sn")
        tA = sbt([64, G], "tA"); tB = sbt([64, G], "tB"); tC = sbt([64, G], "tC")
        def dv(fn): return P.op("dve", fn, reads=[t_ssm], writes=[t_ssm])
        def ac(fn): return P.op("act", fn, reads=[t_ssm], writes=[t_ssm])
        dv(C.tensor_tensor(out=lrdt[:], in0=lr[:], in1=dtb[:], op=ALU.mult))
        dv(C.tensor_tensor(out=ang[:], in0=li[:], in1=dtb[:], op=ALU.mult))
        ac(C.activation(out=mag[:], in_=lrdt[:], func=AF.Exp))
        halfpi = sbt([64, 1], "halfpi")
        dv(C.memset(halfpi[:], math.pi / 2))
        ac(C.activation(out=sn[:], in_=ang[:], func=AF.Sin, scale=1.0 / 32))
        ac(C.activation(out=cs[:], in_=ang[:], func=AF.Sin, scale=1.0 / 32, bias=halfpi[:, 0:1]))
        for _ in range(5):
            dv(C.tensor_tensor(out=tA[:], in0=cs[:], in1=cs[:], op=ALU.mult))
            dv(C.tensor_tensor(out=tB[:], in0=sn[:], in1=sn[:], op=ALU.mult))
            dv(C.tensor_tensor(out=tC[:], in0=cs[:], in1=sn[:], op=ALU.mult))
            dv(C.tensor_tensor(out=cs[:], in0=tA[:], in1=tB[:], op=ALU.subtract))
            dv(C.tensor_scalar(out=sn[:], in0=tC[:], scalar1=2.0, scalar2=None, op0=ALU.mult))

        dv(C.memset(PW[:, 0, :, 0], 1.0)); dv(C.memset(PW[:, 1, :, 0], 0.0))
        dv(C.memset(NPW[:, 0, :, 0], 1.0)); dv(C.memset(NPW[:, 1, :, 0], 0.0))
        dv(C.tensor_tensor(out=PW[:, 0, :, 1], in0=mag[:], in1=cs[:], op=ALU.mult))
        dv(C.tensor_tensor(out=PW[:, 1, :, 1], in0=mag[:], in1=sn[:], op=ALU.mult))
        m2i = sbt([64, G], "m2i")
        ac(C.activation(out=m2i[:], in_=lrdt[:], func=AF.Exp, scale=-2.0))
        dv(C.tensor_tensor(out=NPW[:, 0, :, 1], in0=PW[:, 0, :, 1], in1=m2i[:], op=ALU.mult))
        dv(C.tensor_tensor(out=tA[:], in0=PW[:, 1, :, 1], in1=m2i[:], op=ALU.mult))
        dv(C.tensor_scalar(out=NPW[:, 1, :, 1], in0=tA[:], scalar1=-1.0, scalar2=None, op0=ALU.mult))
        def cmul_small(outr, outi, ar_, ai_, br_, bi_):
            dv(C.tensor_tensor(out=tA[:], in0=ar_, in1=br_, op=ALU.mult))
            dv(C.tensor_tensor(out=tB[:], in0=ai_, in1=bi_, op=ALU.mult))
            dv(C.tensor_tensor(out=outr, in0=tA[:], in1=tB[:], op=ALU.subtract))
            dv(C.tensor_tensor(out=tA[:], in0=ar_, in1=bi_, op=ALU.mult))
            dv(C.tensor_tensor(out=tB[:], in0=ai_, in1=br_, op=ALU.mult))
            dv(C.tensor_tensor(out=outi, in0=tA[:], in1=tB[:], op=ALU.add))
        for k in range(2, 9):
            cmul_small(PW[:, 0, :, k], PW[:, 1, :, k], PW[:, 0, :, k - 1], PW[:, 1, :, k - 1], PW[:, 0, :, 1], PW[:, 1, :, 1])
        for k in range(2, 8):
            cmul_small(NPW[:, 0, :, k], NPW[:, 1, :, k], NPW[:, 0, :, k - 1], NPW[:, 1, :, k - 1], NPW[:, 0, :, 1], NPW[:, 1, :, 1])
        fr = sbt([64, G], "fr"); fi = sbt([64, G], "fi"); den = sbt([64, G], "den"); nr = sbt([64, G], "nr")
        dv(C.tensor_tensor(out=tA[:], in0=lr[:], in1=lr[:], op=ALU.mult))
        dv(C.tensor_tensor(out=tB[:], in0=li[:], in1=li[:], op=ALU.mult))
        dv(C.tensor_tensor(out=den[:], in0=tA[:], in1=tB[:], op=ALU.add))
        dv(C.reciprocal(out=den[:], in_=den[:]))
        dv(C.tensor_scalar(out=nr[:], in0=PW[:, 0, :, 1], scalar1=-1.0, scalar2=None, op0=ALU.add))
        dv(C.tensor_tensor(out=tA[:], in0=nr[:], in1=lr[:], op=ALU.mult))
        dv(C.tensor_tensor(out=tB[:], in0=PW[:, 1, :, 1], in1=li[:], op=ALU.mult))
        dv(C.tensor_tensor(out=tA[:], in0=tA[:], in1=tB[:], op=ALU.add))
        dv(C.tensor_tensor(out=fr[:], in0=tA[:], in1=den[:], op=ALU.mult))
        dv(C.tensor_tensor(out=tA[:], in0=PW[:, 1, :, 1], in1=lr[:], op=ALU.mult))
        dv(C.tensor_tensor(out=tB[:], in0=nr[:], in1=li[:], op=ALU.mult))
        dv(C.tensor_tensor(out=tA[:], in0=tA[:], in1=tB[:], op=ALU.subtract))
        dv(C.tensor_tensor(out=fi[:], in0=tA[:], in1=den[:], op=ALU.mult))
        Br = sbt([64, G, 16], "Br"); Bi = sbt([64, G, 16], "Bi"); Bbr = sbt([64, G, 16], "Bbr"); Bbi = sbt([64, G, 16], "Bbi")
        u1_ = sbt([64, 8, 16], "u1"); u2_ = sbt([64, 8, 16], "u2"); u1f = sbt([64, G, 16], "u1f"); u2f = sbt([64, G, 16], "u2f")
        load("sp", Br[:], b_re.rearrange("g p c -> p g c"), t_ssm)
        load("sp", Bi[:], b_im.rearrange("g p c -> p g c"), t_ssm)
        def bc_g(ap2):
            return ap2.unsqueeze(2).to_broadcast([64, ap2.shape[1], 16])
        def cmul_gc(outr, outi, xr, xi, yr2, yi2, u1=None, u2=None):
            u1 = u1_ if u1 is None else u1; u2 = u2_ if u2 is None else u2
            dv(C.tensor_tensor(out=u1[:], in0=xr, in1=bc_g(yr2), op=ALU.mult))
            dv(C.tensor_tensor(out=u2[:], in0=xi, in1=bc_g(yi2), op=ALU.mult))
            dv(C.tensor_tensor(out=outr, in0=u1[:], in1=u2[:], op=ALU.subtract))
            dv(C.tensor_tensor(out=u1[:], in0=xr, in1=bc_g(yi2), op=ALU.mult))
            dv(C.tensor_tensor(out=u2[:], in0=xi, in1=bc_g(yr2), op=ALU.mult))
            dv(C.tensor_tensor(out=outi, in0=u1[:], in1=u2[:], op=ALU.add))
        cmul_gc(Bbr[:], Bbi[:], Br[:], Bi[:], fr[:], fi[:], u1f, u2f)
        Cr = sbt([64, G, 16], "Cr"); Ci = sbt([64, G, 16], "Ci")
        cst = sbt([128, 4, 64], "cst"); cst2 = sbt([128, 4, 64], "cst2")
        load("sp", cst[:], c_re.rearrange("(a q) p -> q a p", q=128), t_ssm)
        load("sp", cst2[:], c_im.rearrange("(a q) p -> q a p", q=128), t_ssm)
        win_v0 = w_in.rearrange("(k p) n -> p k n", p=128); wglu_v0 = w_glu.rearrange("(k p) n -> p k n", p=128)
        wout_v0 = w_out.rearrange("(k p) n -> p k n", p=128)
        wfi_v0 = w_ffn_in.rearrange("(k p) n -> p k n", p=128); wfo_v0 = w_ffn_out.rearrange("(k p) n -> p k n", p=128)
        def mk_cols0(wv, K, cols):
            return lambda view: [(view[:, 0:K, ci * 128:(ci + 1) * 128], wv[:, :, c0_:c0_ + 128]) for ci, c0_ in enumerate(cols)]
        for i0 in range(0, 8, 2):
            precast(("u", i0), mk_cols0(win_v0, 8, [128 * i0, 128 * i0 + 128]), "w")
        for m in range(8):
            precast(("g", m), mk_cols0(win_v0, 8, [1024 + 128 * m, 2048 + 128 * m]), "w")
            precast(("glu", m), mk_cols0(wglu_v0, 4, [128 * m, 1024 + 128 * m]), "w")
        for i0 in range(0, 8, 2):
            precast(("out", i0), mk_cols0(wout_v0, 8, [128 * i0, 128 * i0 + 128]), "w")
        for fh in range(2):
            for fl in range(11):
                ff = fh * 11 + fl
                precast(("ffi", ff), mk_cols0(wfi_v0, 8, [128 * ff, DFF + 128 * ff]), "w")
            for m in range(8):
                precast(("ffo", fh, m), (lambda view, fh=fh, m=m: [(view[:, :, :], wfo_v0[:, fh * 11:(fh + 1) * 11, 128 * m:128 * m + 128])]), "w2")
        for (src, dst) in ((cst, Cr), (cst2, Ci)):
            bk, tb = nb()
            for a in range(4):
                P.op("pe", C.transpose(out=bk[0:64, a * 128:(a + 1) * 128], in_=src[:, a, :], identity=ident[:]), reads=[t_ssm, t_ident], writes=[tb], sig=(a == 3))
            P.op("dve", C.tensor_copy(out=dst[:].rearrange("p g c -> p (g c)"), in_=bk[0:64, :]), reads=[tb], writes=[t_ssm])
        t_W = T()
        Er = sbt([128, 8, 8, 16], "Er"); Ei = sbt([128, 8, 8, 16], "Ei")
        Rr = sbt([128, 8, 8, 16], "Rr"); Ri = sbt([128, 8, 8, 16], "Ri")
        Qr = sbt([128, 8, 9, 16], "Qr"); Qi = sbt([128, 8, 9, 16], "Qi"); nQi = sbt([128, 8, 9, 16], "nQi")
        t_D8 = T()
        for i in range(8):
            load("sp", D8[16 * i:16 * i + 16, :], ssm_d.rearrange("(g c) -> c g", c=16), t_D8)
        kt = sbt([128, 128], "kt"); t_kt = T()
        v1 = sbt([64, 8, 9, 16], "v1"); v2 = sbt([64, 8, 9, 16], "v2")
        PWrev = sbt([64, 2, G, 8], "PWrev")
        for j in range(8):
            dv(C.tensor_copy(out=PWrev[:, :, :, j], in_=PW[:, :, :, 7 - j]))
        for tt in (Er, Ei, Rr, Ri, Qr, Qi, nQi):
            dv(C.memset(tt[:].rearrange("p a b c -> p (a b c)"), 0.0))
        adaln_A(A1, g1, 1)
        DBLK = int(dbg[0][1:]) if (dbg and dbg[0].startswith('@')) else -1
        def mk_cols(wv, K, cols):
            return lambda view: [(view[:, 0:K, ci * 128:(ci + 1) * 128], wv[:, :, c0_:c0_ + 128]) for ci, c0_ in enumerate(cols)]
        specs = []
        for i0 in range(0, 8, 2):
            specs.append((("u", i0), mk_cols(win_v, 8, [128 * i0, 128 * i0 + 128]), "w"))
        for m in range(8):
            specs.append((("g", m), mk_cols(win_v, 8, [1024 + 128 * m, 2048 + 128 * m]), "w"))
            specs.append((("glu", m), mk_cols(wglu_v, 4, [128 * m, 1024 + 128 * m]), "w"))
        for i0 in range(0, 8, 2):
            specs.append((("out", i0), mk_cols(wout_v, 8, [128 * i0, 128 * i0 + 128]), "w"))
        for fh in range(2):
            for fl in range(11):
                ff = fh * 11 + fl
                specs.append((("ffi", ff), mk_cols(wfi_v, 8, [128 * ff, DFF + 128 * ff]), "w"))
            for m in range(8):
                specs.append((("ffo", fh, m), (lambda view, fh=fh, m=m: [(view[:, :, :], wfo_v[:, fh * 11:(fh + 1) * 11, 128 * m:128 * m + 128])]), "w2"))
        spec = {key: (mk, kind) for (key, mk, kind) in specs}
        def wl(key):
            mk, kind = spec[key]
            return wload(key, mk, kind)
        def early(blk):
            dq = "sp" if blk == 0 else "pool"
            has_s = (blk == SBLK)
            parts = [(0, 512, False)] + ([(512, 64, True)] if has_s else [])
            def V(ap, smp):
                return ap.rearrange("p (t s) -> p t s", t=4) if smp else ap
            def MB(tab, idx, smp):
                if smp:
                    return tab[:, idx, 1:17].unsqueeze(1).to_broadcast([128, 4, 16])
                return tab[:, idx, 0:1].to_broadcast([128, 512])
            tiles = [(jj, False) for jj in range(4)] + ([(0, True)] if has_s else [])
            def linear(kname, K, evac, src, t_src, gen=False):
                for i0 in range(0, 8, 2):
                    sl, tsl = wl((kname, i0))
                    for ci in range(2):
                        for (c0, NT, smp) in parts:
                            bk, tb = nb_e()
                            for k in range(K):
                                P.op("pe", C.matmul(bk[:, 0:NT], lhsT=sl[:, k, ci * 128:(ci + 1) * 128], rhs=src[:, k, c0:c0 + NT], start=(k == 0), stop=(k == K - 1)),
                                     reads=[tsl, t_src], writes=[tb], sig=(k == K - 1))
                            evac(i0 + ci, bk, tb, c0, NT, smp)
                    if gen:
                        yield

            def linear_now(kname, K, evac, src, t_src):
                for _ in linear(kname, K, evac, src, t_src, False):
                    pass
            for (jj, smp) in tiles:
                stg, tstg = stage[st_i[0] % 2]; st_i[0] += 1
                rows = 64 if smp else 128
                c0 = 512 if smp else jj * 128
                if smp:
                    for t in range(4):
                        load(dq, stg[16 * t:16 * t + 16, :], xsv[t, :, :], tstg)
                else:
                    for j2 in range(2):
                        load(dq, stg[64 * j2:64 * j2 + 64, :], xpv[2 * jj + j2, 64 * blk:64 * blk + 64, :], tstg)
                yield
                yield
                sb_, tsb = ssb[ss_i[0] % 2]; ss_i[0] += 1
                junk = usm[:].rearrange("p a b -> p (a b)")[:, 0:1024]
                P.op("act", C.activation(out=junk, in_=stg[:, :], func=AF.Square, accum_out=sb_[:, 0:1]), reads=[tstg], writes=[t_usm, tsb])
                P.op("act", C.activation(out=sb_[:, 1:2], in_=sb_[:, 0:1], func=AF.Sqrt, scale=1.0 / D, bias=epsc[:, 0:1]), reads=[tsb, t_eps], writes=[tsb])
                P.op("dve", C.reciprocal(out=sb_[:, 2:3], in_=sb_[:, 1:2]), reads=[tsb], writes=[tsb])
                P.op("act", C.activation(out=stg[:, :], in_=stg[:, :], func=AF.Identity, scale=sb_[:, 2:3]), reads=[tsb, tstg], writes=[tstg])
                yield
                bks = []
                for half in range(2):
                    bk, tb = nb_e()
                    bks.append((bk, tb))
                    for f4 in range(4):
                        fc = half * 4 + f4
                        P.op("pe", C.transpose(out=bk[:, f4 * 128:(f4 + 1) * 128], in_=stg[:, fc * 128:(fc + 1) * 128], identity=ident[:]),
                             reads=[tstg, t_ident], writes=[tb], sig=(f4 == 3))
                yield
                for half in range(2):
                    bk, tb = bks[half]
                    for f4 in range(4):
                        fc = half * 4 + f4
                        if not smp:
                            P.op("act", C.activation(out=hB[:, fc, c0:c0 + 128], in_=bk[:, f4 * 128:(f4 + 1) * 128], func=AF.Identity, scale=A1[:, fc, 0:1], bias=mod[:, fc, 0:1]),
                                 reads=[tb, t_A, t_mod], writes=[t_hB])
                        else:
                            tm, ttm = tq_e()
                            P.op("dve", C.tensor_tensor(out=V(tm[:, 0:64], True), in0=V(bk[:, f4 * 128:f4 * 128 + 64], True), in1=MB(A1, fc, True), op=ALU.mult), reads=[tb, t_A], writes=[ttm])
                            P.op("dve", C.tensor_tensor(out=V(hB[:, fc, 512:576], True), in0=V(tm[:, 0:64], True), in1=MB(mod, fc, True), op=ALU.add), reads=[ttm, t_mod], writes=[t_hB])
                yield
            if blk == DBLK: tap('h1', hB[:], t_hB)
            def pool_pc(pc):
                lv = pc + 1
                w = float(1 << lv)
                if has_s:
                    cur = Es[:, pc, :, :]
                    bufs = [s1[:].rearrange("p a b -> p (a b)")[:, 0:304].rearrange("p (s r) -> p s r", r=19), s2[:].rearrange("p a b -> p (a b)")[:, 0:304].rearrange("p (s r) -> p s r", r=19)]
                    for l in range(lv):
                        d = 1 << l
                        dst = bufs[l % 2]
                        P.op("dve", C.tensor_tensor(out=dst[:, :, d:19], in0=cur[:, :, d:19], in1=cur[:, :, 0:19 - d], op=ALU.add), reads=[t_Es, t_s], writes=[t_s])
                        cur = dst
                    P.op("dve", C.scalar_tensor_tensor(out=pooled[:, pc, 512:576].rearrange("p (t s) -> p s t", t=4), in0=cur[:, :, 15:19], scalar=1.0 / w, in1=Es[:, pc, :, 15:19], op0=ALU.mult, op1=ALU.subtract), reads=[t_s, t_Es], writes=[t_pooled])
                cur = E1[:, :, :]
                bufs = [s1, s2]
                for l in range(lv):
                    d = 1 << l
                    dst = bufs[l % 2]
                    if d < 8:
                        P.op("dve", C.tensor_tensor(out=dst[:, d:8, :], in0=cur[:, d:8, :], in1=cur[:, 0:8 - d, :], op=ALU.add), reads=[t_E1, t_s], writes=[t_s])
                        P.op("dve", C.tensor_tensor(out=dst[:, 0:d, 1:66], in0=cur[:, 0:d, 1:66], in1=cur[:, 8 - d:8, 0:65], op=ALU.add), reads=[t_E1, t_s], writes=[t_s])
                    else:
                        P.op("dve", C.tensor_tensor(out=dst[:, :, 1:66], in0=cur[:, :, 1:66], in1=cur[:, :, 0:65], op=ALU.add), reads=[t_E1, t_s], writes=[t_s])
                    cur = dst
                P.op("dve", C.scalar_tensor_tensor(out=pooled[:, pc, 0:512].rearrange("p (j n) -> p j n", j=8), in0=cur[:, :, 2:66], scalar=1.0 / w, in1=E1[:, :, 2:66], op0=ALU.mult, op1=ALU.subtract), reads=[t_s, t_E1], writes=[t_pooled])
                if blk == 0:
                    for t in range(int(w) - 1):
                        j, n = t % 8, t // 8
                        P.op("dve", C.scalar_tensor_tensor(out=pooled[:, pc, j * 64 + n:j * 64 + n + 1], in0=cur[:, j, 2 + n:3 + n], scalar=1.0 / (t + 1), in1=E1[:, j, 2 + n:3 + n], op0=ALU.mult, op1=ALU.subtract), reads=[t_s, t_E1], writes=[t_pooled])
                if blk == 3:
                    P.op("dve", C.tensor_copy(out=PPt[:, pc * 16:(pc + 1) * 16].rearrange("p (n j) -> p n j", n=2), in_=E1[:, :, 64:66].rearrange("p j n -> p n j")), reads=[t_E1], writes=[t_PPt])
                P.op("dve", C.tensor_copy(out=Ehist[:, pc, :, :], in_=E1[:, :, 64:66]), reads=[t_E1, t_pooled], writes=[t_Eh])
            def evac_u(mc, bk, tb, c0, NT, smp):
                if mc < 4:
                    P.op("act", C.activation(out=usm[:, mc, c0:c0 + NT], in_=bk[:, 0:NT], func=AF.Copy), reads=[tb], writes=[t_usm])
                else:
                    pc = mc - 4
                    if smp:
                        P.op("act", C.activation(out=Es[:, pc, :, 15:19], in_=bk[:, 0:64].rearrange("p (t s) -> p s t", t=4), func=AF.Copy), reads=[tb], writes=[t_Es])
                    else:
                        P.op("act", C.activation(out=E1[:, :, 2:66], in_=bk[:, 0:512].rearrange("p (j n) -> p j n", j=8), func=AF.Copy), reads=[tb], writes=[t_E1])
                        P.op("dve", C.tensor_copy(out=E1[:, :, 0:2], in_=Ehist[:, pc, :, :]), reads=[t_Eh], writes=[t_E1])
                    if smp or not has_s:
                        pool_pc(pc)
            if has_s:
                for half in range(2):
                    stg, tstg = stage[st_i[0] % 2]; st_i[0] += 1
                    load(dq, stg[0:120, 0:512], st_pool[120 * half:120 * half + 120, :], tstg)
                    bk, tb = nb_e()
                    for pc in range(4):
                        P.op("pe", C.transpose(out=bk[:, pc * 128:(pc + 1) * 128], in_=stg[:, pc * 128:(pc + 1) * 128], identity=ident[:]), reads=[tstg, t_ident], writes=[tb], sig=(pc == 3))
                    P.op("act", C.activation(out=Es[:, :, 8 * half:8 * half + 8, 0:15], in_=bk[:, :].rearrange("p (a b) -> p a b", a=4)[:, :, 0:120].rearrange("p a (s r) -> p a s r", s=8), func=AF.Copy), reads=[tb], writes=[t_Es])
            yield from linear("u", 8, evac_u, hB, t_hB, True)
            if blk == DBLK: tap('usm', usm[:], t_usm)
            ckpt('u%d' % blk)
            P.op(dq, C.dma_start(out=Ud.rearrange("(a p) j n -> p a (j n)", p=128), in_=usm[:, :, 0:512]), reads=[t_usm, t_U8], writes=[t_Ud], dma=psem_q(dq))
            Udv = Ud.rearrange("(g c) j n -> j c g n", c=16)
            us = psem_q(dq)
            for j in range(8):
                P.op(dq, C.dma_start(out=U8[16 * j:16 * j + 16, :, :], in_=Udv[j, :, :, :]), reads=[t_Ud], writes=[t_U8], dma=us)
            if has_s:
                uds = psem_q(dq)
                for a in range(4):
                    P.op(dq, C.dma_start(out=Uds.rearrange("(a p) j n -> p a j n", p=128)[:, a, 4:8, :], in_=usm[:, a, 512:576].rearrange("p (t s) -> p t s", t=4)), reads=[t_usm, t_U8s], writes=[t_Uds], dma=uds)
                P.op("dve", C.memset(U8s[:].rearrange("p a b -> p (a b)"), 0.0), reads=[], writes=[t_U8s])
                Udsv = Uds.rearrange("(g c) j n -> j c g n", c=16)
                us2 = psem_q(dq)
                for j in range(4, 8):
                    P.op(dq, C.dma_start(out=U8s[16 * j:16 * j + 16, :, :], in_=Udsv[j, :, :, :]), reads=[t_Uds], writes=[t_U8s], dma=us2)
            if blk == DBLK: tap('U8', U8[:], t_U8)
            ckpt('im%d' % blk)

            if blk == DBLK: tap('usm', usm[:], t_usm)
            if blk == 3:
                tm, ttm = PPt, t_PPt
                tm2, ttm2 = tq_e()
                for pc in range(4):
                    bk, tb = nb_e()
                    P.op("pe", C.transpose(out=bk[0:16, 0:128], in_=tm[:, pc * 16:(pc + 1) * 16], identity=ident[:]), reads=[ttm, t_ident], writes=[tb])
                    P.op("dve", C.tensor_copy(out=tm2[0:16, pc * 128:(pc + 1) * 128], in_=bk[0:16, 0:128]), reads=[tb], writes=[ttm2])
                out_events.append(store("pool", p_pool[:, :], tm2[1:16, 0:512], ttm2))
            if has_s:
                for half in range(2):
                    stg, tstg = stage[st_i[0] % 2]; st_i[0] += 1
                    tm, ttm = tq_e()
                    P.op("dve", C.tensor_copy(out=tm[:, 0:480].rearrange("p (a s r) -> p a s r", a=4, s=8), in_=Es[:, :, 8 * half:8 * half + 8, 4:19]), reads=[t_Es], writes=[ttm])
                    for pc in range(4):
                        bk, tb = nb_e()
                        P.op("pe", C.transpose(out=bk[0:120, 0:128], in_=tm[:, pc * 120:(pc + 1) * 120], identity=ident[:]), reads=[ttm, t_ident], writes=[tb])
                        P.op("dve", C.tensor_copy(out=stg[0:120, pc * 128:(pc + 1) * 128], in_=bk[0:120, 0:128]), reads=[tb], writes=[tstg])
                    out_events.append(store("pool", s_pool[120 * half:120 * half + 120, :], stg[0:120, 0:512], tstg))
            if blk == DBLK: tap('pooled', pooled[:], t_pooled)
            ckpt('pool%d' % blk)

            yield "ssm"
            P.op("dve", C.tensor_tensor(out=S1[:], in0=S[:], in1=AR[:], op=ALU.mult), reads=[t_S, t_ssm], writes=[t_ssc])
            P.op("dve", C.tensor_tensor(out=S2[:, 0, :], in0=S[:, 1, :], in1=AIs[:, 0, :], op=ALU.mult), reads=[t_S], writes=[t_ssc])
            P.op("dve", C.tensor_tensor(out=S2[:, 1, :], in0=S[:, 0, :], in1=AIs[:, 1, :], op=ALU.mult), reads=[t_S], writes=[t_ssc])
            P.op("dve", C.tensor_tensor(out=Zm1[:], in0=S1[:], in1=S2[:], op=ALU.add), reads=[t_ssc], writes=[t_Zm1])
            for o in range(4):
                g0 = 8 * o
                zb = ZB[0]
                zT1, tzT1 = tq_e(); zT2, tzT2 = tq_e()
                zb['T1'] = zT1[0:64, 0:512].rearrange("p (a b) -> p a b", a=8); zb['T2'] = zT2[0:64, 0:512].rearrange("p (a b) -> p a b", a=8)
                bka, tba = nb_e(); bkb, tbb = nb_e()
                for gg in range(8):
                    g = g0 + gg
                    P.op("pe", C.matmul(bka[0:64, gg * 64:(gg + 1) * 64], lhsT=Wend[:, g, 0:64], rhs=U8[:, g, :], start=True, stop=True), reads=[t_W, t_U8], writes=[tba], sig=False)
                    P.op("pe", C.matmul(bkb[0:64, gg * 64:(gg + 1) * 64], lhsT=Wend[:, g, 64:128], rhs=U8[:, g, :], start=True, stop=True), reads=[t_W, t_U8], writes=[tbb], sig=(gg == 7))
                XA = bka[0:64, :].rearrange("p (a b) -> p a b", a=8); XB = bkb[0:64, :].rearrange("p (a b) -> p a b", a=8)
                CO = ROT[:, 0, g0:g0 + 8, :]; SI = ROT[:, 1, g0:g0 + 8, :]
                tzall = [zb['tz'][r][gl] for r in range(2) for gl in range(8)]
                yield
                P.op("dve", C.memset(zT1[0:64, 0:1], 0.0), reads=[zb['t']], writes=[tzT1, tzT2, zb['t']])
                P.op("dve", C.tensor_tensor(out=zb['T1'], in0=XA, in1=CO, op=ALU.mult), reads=[tba, t_ssm], writes=[zb['t']])
                P.op("dve", C.tensor_tensor(out=zb['T2'], in0=XB, in1=SI, op=ALU.mult), reads=[tbb], writes=[zb['t']])
                P.op("dve", C.tensor_tensor(out=zb['Zin'][:, 0], in0=zb['T1'], in1=zb['T2'], op=ALU.add), reads=[zb['t']], writes=[zb['t']])
                P.op("dve", C.tensor_tensor(out=zb['T1'], in0=XB, in1=CO, op=ALU.mult), reads=[tbb, zb['t']], writes=[zb['t']])
                P.op("dve", C.tensor_tensor(out=zb['T2'], in0=XA, in1=SI, op=ALU.mult), reads=[tba], writes=[zb['t']])
                P.op("dve", C.tensor_tensor(out=zb['Zin'][:, 1], in0=zb['T1'], in1=zb['T2'], op=ALU.subtract), reads=[zb['t']], writes=[zb['t']])
                for r in range(2):
                    for gl in range(8):
                        g = g0 + gl
                        P.op("dve", C.tensor_tensor_scan(out=zb['Z'][:, r, gl, :], data0=RHO[:, g:g + 1].to_broadcast([64, 64]), data1=zb['Zin'][:, r, gl, :], initial=Zm1[:, r, g:g + 1], op0=ALU.mult, op1=ALU.add),
                             reads=[zb['t'], t_Zm1], writes=[zb['tz'][r][gl]])
                P.op("dve", C.tensor_tensor(out=zb['T1'], in0=zb['Z'][:, 0], in1=CO, op=ALU.mult), reads=tzall + [zb['t']], writes=[zb['t']])
                P.op("dve", C.tensor_tensor(out=zb['T2'], in0=zb['Z'][:, 1], in1=SI, op=ALU.mult), reads=[zb['t']], writes=[zb['t']])
                P.op("dve", C.tensor_tensor(out=zb['Sf'][:, 0], in0=zb['T1'], in1=zb['T2'], op=ALU.subtract), reads=[zb['t']], writes=[zb['t']])
                P.op("dve", C.tensor_tensor(out=zb['T1'], in0=zb['Z'][:, 1], in1=CO, op=ALU.mult), reads=[zb['t']], writes=[zb['t']])
                P.op("dve", C.tensor_tensor(out=zb['T2'], in0=zb['Z'][:, 0], in1=SI, op=ALU.mult), reads=[zb['t']], writes=[zb['t']])
                P.op("dve", C.tensor_tensor(out=zb['Sf'][:, 1], in0=zb['T1'], in1=zb['T2'], op=ALU.add), reads=[zb['t']], writes=[zb['t']] + tzall)
                for r in range(2):
                    P.op("dve", C.tensor_copy(out=Spv[64 * r:64 * r + 64, g0:g0 + 8, 1:64], in_=zb['Sf'][:, r, :, 0:63]), reads=[zb['t']], writes=[t_Spv[o]])
                    P.op("dve", C.tensor_copy(out=Spv[64 * r:64 * r + 64, g0:g0 + 8, 0], in_=S[:, r, g0:g0 + 8]), reads=[t_S], writes=[t_Spv[o]])
                P.op("dve", C.tensor_copy(out=S[:, :, g0:g0 + 8], in_=zb['Sf'][:, :, :, 63]), reads=[zb['t'], t_Zm1], writes=[t_S, zb['t'], tzT1, tzT2])
                yield
            if blk == 3:
                out_events.append(store("pool", p_re.rearrange("(g p) -> p g", p=64), S[:, 0, :], t_S))
                out_events.append(store("pool", p_im.rearrange("(g p) -> p g", p=64), S[:, 1, :], t_S))
            if has_s:
                for g0 in range(0, G, 16):
                    bka, tba = nb_e(); bkb, tbb = nb_e()
                    for gg in range(16):
                        g = g0 + gg
                        P.op("pe", C.matmul(bka[0:64, gg * 16:(gg + 1) * 16], lhsT=Wend[:, g, 0:64], rhs=U8s[:, g, :], start=True, stop=True), reads=[t_W, t_U8s], writes=[tba], sig=False)
                        P.op("pe", C.matmul(bkb[0:64, gg * 16:(gg + 1) * 16], lhsT=Wend[:, g, 64:128], rhs=U8s[:, g, :], start=True, stop=True), reads=[t_W, t_U8s], writes=[tbb], sig=(gg == 15))
                    P.op("act", C.activation(out=Xs[:, 0, g0:g0 + 16, :], in_=bka[0:64, 0:256].rearrange("p (a b) -> p a b", a=16), func=AF.Copy), reads=[tba], writes=[t_Xs])
                    P.op("act", C.activation(out=Xs[:, 1, g0:g0 + 16, :], in_=bkb[0:64, 0:256].rearrange("p (a b) -> p a b", a=16), func=AF.Copy), reads=[tbb], writes=[t_Xs])
                for ri, src in enumerate((st_re, st_im)):
                    srcv = src.rearrange("s (g p) -> (s g) p", p=64)
                    for q in range(4):
                        stg, tstg = stage[st_i[0] % 2]; st_i[0] += 1
                        load(dq, stg[:, 0:64], srcv[128 * q:128 * q + 128, :], tstg)
                        bk, tb = nb_e()
                        P.op("pe", C.transpose(out=bk[0:64, 0:128], in_=stg[:, 0:64], identity=ident[:]), reads=[tstg, t_ident], writes=[tb])
                        P.op("dve", C.tensor_copy(out=H0[:, ri, 4 * q:4 * q + 4, :].rearrange("p s g -> p (s g)"), in_=bk[0:64, 0:128]), reads=[tb], writes=[t_H0])
                def bc_s(ap2): return ap2.unsqueeze(1).to_broadcast([64, 16, G])
                t_hh = ZB[0]['t']
                def cm(outt, pr, pi_):
                    P.op("dve", C.tensor_tensor(out=Ht[:, 0, :, :], in0=H0[:, 0, :, :], in1=bc_s(pr), op=ALU.mult), reads=[t_H0, t_ssm, t_hh], writes=[t_hh])
                    P.op("dve", C.tensor_tensor(out=Ht[:, 1, :, :], in0=H0[:, 1, :, :], in1=bc_s(pi_), op=ALU.mult), reads=[t_H0], writes=[t_hh])
                    P.op("dve", C.tensor_tensor(out=outt[:, 0, :, :], in0=Ht[:, 0, :, :], in1=Ht[:, 1, :, :], op=ALU.subtract), reads=[t_hh], writes=[t_hh])
                    P.op("dve", C.tensor_tensor(out=Ht[:, 0, :, :], in0=H0[:, 0, :, :], in1=bc_s(pi_), op=ALU.mult), reads=[t_H0, t_hh], writes=[t_hh])
                    P.op("dve", C.tensor_tensor(out=Ht[:, 1, :, :], in0=H0[:, 1, :, :], in1=bc_s(pr), op=ALU.mult), reads=[t_H0], writes=[t_hh])
                    P.op("dve", C.tensor_tensor(out=outt[:, 1, :, :], in0=Ht[:, 0, :, :], in1=Ht[:, 1, :, :], op=ALU.add), reads=[t_hh], writes=[t_hh])
                cm(Hu[0:64], NPW[:, 0, :, 4], NPW[:, 1, :, 4])
                for r in range(2):
                    P.op("dve", C.tensor_copy(out=Spvs[64 * r:64 * r + 64, :, :], in_=Hu[0:64, r].rearrange("p s g -> p g s")), reads=[t_hh], writes=[t_Spvs])
                cm(Hu[0:64], PW[:, 0, :, 4], PW[:, 1, :, 4])
                P.op("dve", C.tensor_tensor(out=Hu[0:64], in0=Hu[0:64], in1=Xs[:, :, :, :].rearrange("p r g s -> p r s g"), op=ALU.add), reads=[t_hh, t_Xs], writes=[t_hh])
                for ri, dst in enumerate((s_re, s_im)):
                    dstv = dst.rearrange("s (g p) -> (s g) p", p=64)
                    for q in range(4):
                        stg, tstg = stage[st_i[0] % 2]; st_i[0] += 1
                        bk, tb = nb_e()
                        P.op("pe", C.transpose(out=bk[:, 0:128], in_=Hu[:, ri, 4 * q:4 * q + 4, :].rearrange("p s g -> p (s g)"), identity=ident[:]), reads=[t_hh, t_ident], writes=[tb])
                        P.op("dve", C.tensor_copy(out=stg[:, 0:64], in_=bk[:, 0:64]), reads=[tb], writes=[tstg])
                        out_events.append(store("pool", dstv[128 * q:128 * q + 128, :], stg[:, 0:64], tstg))
            ckpt('rec%d' % blk)
            yield
            def ssm_out(NC, U8_, tU8_, Spv_, tSpv_fn, yg_, tyg_, tYd_):
                for o in range(4):
                    yield
                    g0 = 8 * o
                    bk, tb = nb_e()
                    for gg in range(8):
                        g = g0 + gg
                        P.op("pe", C.matmul(bk[:, gg * NC:(gg + 1) * NC], lhsT=Wst[:, g, :], rhs=Spv_[:, g, :], start=True, stop=False), reads=[t_W, tSpv_fn(o)], writes=[tb], sig=False)
                        P.op("pe", C.matmul(bk[:, gg * NC:(gg + 1) * NC], lhsT=Kloc[:, g, :], rhs=U8_[:, g, :], start=False, stop=True), reads=[t_W, tU8_], writes=[tb], sig=(gg == 7))
                    yield
                    W_ = 8 * NC
                    ta, tta = tq_e(); tb2, ttb2 = tq_e()
                    def v3(ap): return ap[:, 0:W_].rearrange("p (a b) -> p a b", a=8)
                    P.op("act", C.activation(out=ta[:, 0:W_], in_=bk[:, 0:W_], func=AF.Copy), reads=[tb], writes=[tta])
                    P.op("act", C.activation(out=tb2[:, 0:W_], in_=bk[:, 0:W_], func=AF.Square), reads=[tb], writes=[ttb2])
                    P.op("act", C.activation(out=tb2[:, 0:W_], in_=tb2[:, 0:W_], func=AF.Identity, scale=0.044715, bias=onesf[:, 0:1]), reads=[ttb2, t_ones], writes=[ttb2])
                    yield
                    P.op("dve", C.tensor_tensor(out=tb2[:, 0:W_], in0=tb2[:, 0:W_], in1=ta[:, 0:W_], op=ALU.mult), reads=[tta, ttb2], writes=[ttb2])
                    yield
                    P.op("act", C.activation(out=tb2[:, 0:W_], in_=tb2[:, 0:W_], func=AF.Sigmoid, scale=2.0 * math.sqrt(2.0 / math.pi)), reads=[ttb2], writes=[ttb2])
                    yield
                    P.op("dve", C.tensor_tensor(out=yg_[:, g0:g0 + 8, :], in0=v3(ta), in1=v3(tb2), op=ALU.mult), reads=[tta, ttb2, tYd_], writes=[tyg_])
            yield from ssm_out(64, U8, t_U8, Spv, lambda o: t_Spv[o], yg8, t_yg8, t_Yd)
            yield
            Ydv = Yd.rearrange("(g c) i n -> i c g n", c=16)
            ysm = psem_q(dq)
            for i in range(8):
                P.op(dq, C.dma_start(out=Ydv[i, :, :, :], in_=yg8[16 * i:16 * i + 16, :, :]), reads=[t_yg8, t_yfe], writes=[t_Yd], dma=ysm)
            P.op(dq, C.dma_start(out=yfe[:, :, 0:512], in_=Yd.rearrange("(a p) i n -> p a (i n)", p=128)), reads=[t_Yd], writes=[t_yfe], dma=psem_q(dq))
            if has_s:
                yield from ssm_out(16, U8s, t_U8s, Spvs, lambda o: t_Spvs, yg8s, t_yg8s, t_Yds)
                Ydsv = Yds.rearrange("(g c) i n -> i c g n", c=16)
                ysm2 = psem_q(dq)
                for i in range(4, 8):
                    P.op(dq, C.dma_start(out=Ydsv[i, :, :, :], in_=yg8s[16 * i:16 * i + 16, :, :]), reads=[t_yg8s, t_yfe], writes=[t_Yds], dma=ysm2)
                yfs = psem_q(dq)
                for a in range(4):
                    P.op(dq, C.dma_start(out=yfe[:, a, 512:576].rearrange("p (t s) -> p t s", t=4), in_=Yds.rearrange("(a p) i n -> p a i n", p=128)[:, a, 4:8, :]), reads=[t_Yds], writes=[t_yfe], dma=yfs)
            if blk == DBLK: tap('yg8', yg8[:], t_yg8); tap('yfe', yfe[:], t_yfe)
            ckpt('ssm%d' % blk)

            yield
        def run_main(blk, nxt, prev_out):
            dq = "sp" if blk == 0 else "pool"
            def step_next():
                if nxt is not None:
                    next(nxt, None)
                    next(nxt, None)
            has_s = (blk == SBLK)
            parts = [(0, 512, False)] + ([(512, 64, True)] if has_s else [])
            def V(ap, smp):
                return ap.rearrange("p (t s) -> p t s", t=4) if smp else ap
            def MB(tab, idx, smp):
                if smp:
                    return tab[:, idx, 1:17].unsqueeze(1).to_broadcast([128, 4, 16])
                return tab[:, idx, 0:1].to_broadcast([128, 512])
            tiles = [(jj, False) for jj in range(4)] + ([(0, True)] if has_s else [])
            def rmsnorm(Atab, kind_shift, final=False):
                for (c0, NT, smp) in parts:
                    cs_ = slice(c0, c0 + NT)
                    for fc in range(8):
                        P.op("act", C.activation(out=sq[:, fc, cs_], in_=xT[:, fc, cs_], func=AF.Square), reads=[t_xT[fc]], writes=[t_sq])
                    bk, tb = nb()
                    for fc in range(8):
                        P.op("pe", C.matmul(bk[:, 0:NT], lhsT=onesb[:], rhs=sq[:, fc, cs_], start=(fc == 0), stop=(fc == 7)), reads=[t_sq, t_ones], writes=[tb], sig=(fc == 7))
                    tm, ttm = tq()
                    P.op("act", C.activation(out=tm[:, 0:NT], in_=bk[:, 0:NT], func=AF.Sqrt, scale=1.0 / D, bias=epsc[:, 0:1]), reads=[tb, t_eps], writes=[ttm])
                    P.op("dve", C.reciprocal(out=rstd[:, cs_], in_=tm[:, 0:NT]), reads=[ttm], writes=[t_rstd])
                    for fc in range(8):
                        tm, ttm = tq()
                        P.op("dve", C.tensor_tensor(out=tm[:, 0:NT], in0=xT[:, fc, cs_], in1=rstd[:, cs_], op=ALU.mult), reads=[t_xT[fc], t_rstd], writes=[ttm])
                        if not final:
                            P.op("dve", C.tensor_tensor(out=V(tm[:, 0:NT], smp), in0=V(tm[:, 0:NT], smp), in1=MB(Atab, fc, smp), op=ALU.mult), reads=[t_A, ttm], writes=[ttm])
                            P.op("dve", C.tensor_tensor(out=V(h[:, fc, cs_], smp), in0=V(tm[:, 0:NT], smp), in1=MB(mod, kind_shift * 8 + fc, smp), op=ALU.add), reads=[t_mod, ttm], writes=[t_h])
                        else:
                            P.op("dve", C.tensor_scalar(out=xT[:, fc, cs_], in0=tm[:, 0:NT], scalar1=gf[:, fc:fc + 1], scalar2=None, op0=ALU.mult), reads=[ttm, t_g], writes=[t_xT[fc]])

            def linear(kname, K, evac, src, t_src, gen=False):
                for i0 in range(0, 8, 2):
                    sl, tsl = wl((kname, i0))
                    for ci in range(2):
                        for (c0, NT, smp) in parts:
                            bk, tb = nb()
                            for k in range(K):
                                P.op("pe", C.matmul(bk[:, 0:NT], lhsT=sl[:, k, ci * 128:(ci + 1) * 128], rhs=src[:, k, c0:c0 + NT], start=(k == 0), stop=(k == K - 1)),
                                     reads=[tsl, t_src], writes=[tb], sig=(k == K - 1))
                            evac(i0 + ci, bk, tb, c0, NT, smp)
                    if gen:
                        yield

            def linear_now(kname, K, evac, src, t_src, after=None):
                for _ in linear(kname, K, evac, src, t_src, True):
                    if after is not None:
                        after()
            for m in range(8):
                slg, tslg = wl(("g", m))
                sll, tsll = wl(("glu", m))
                for (c0, NT, smp) in parts:
                    cs_ = slice(c0, c0 + NT)
                    bgs, tgs = nb(); bgp, tgp = nb(); bla, tla = nb(); blb, tlb = nb(); bz, tz = nb()
                    for k in range(8):
                        P.op("pe", C.matmul(bgs[:, 0:NT], lhsT=slg[:, k, 0:128], rhs=hB[:, k, cs_], start=(k == 0), stop=(k == 7)), reads=[tslg, t_hB], writes=[tgs], sig=(k == 7))
                    for k in range(8):
                        P.op("pe", C.matmul(bgp[:, 0:NT], lhsT=slg[:, k, 128:256], rhs=hB[:, k, cs_], start=(k == 0), stop=(k == 7)), reads=[tslg, t_hB], writes=[tgp], sig=(k == 7))
                    for k in range(4):
                        P.op("pe", C.matmul(bla[:, 0:NT], lhsT=sll[:, k, 0:128], rhs=yfe[:, k, cs_], start=(k == 0), stop=(k == 3)), reads=[tsll, t_yfe], writes=[tla], sig=(k == 3))
                    for k in range(4):
                        P.op("pe", C.matmul(blb[:, 0:NT], lhsT=sll[:, k, 128:256], rhs=yfe[:, k, cs_], start=(k == 0), stop=(k == 3)), reads=[tsll, t_yfe], writes=[tlb], sig=(k == 3))
                    P.op("pe", C.matmul(bz[:, 0:NT], lhsT=wpl[:, m // 2, (m % 2) * 128:(m % 2) * 128 + 128], rhs=pooled[:, m // 2, cs_], start=True, stop=True), reads=[t_wpl, t_pooled], writes=[tz])
                    t1, tt1 = tq(); t2, tt2 = tq(); t3, tt3 = tq()
                    P.op("act", C.activation(out=t1[:, 0:NT], in_=bgs[:, 0:NT], func=AF.Sigmoid), reads=[tgs], writes=[tt1])
                    P.op("act", C.activation(out=t2[:, 0:NT], in_=bgp[:, 0:NT], func=AF.Sigmoid), reads=[tgp], writes=[tt2])
                    P.op("act", C.activation(out=t3[:, 0:NT], in_=blb[:, 0:NT], func=AF.Sigmoid), reads=[tlb], writes=[tt3])
                    P.op("dve", C.tensor_tensor(out=t3[:, 0:NT], in0=t3[:, 0:NT], in1=bla[:, 0:NT], op=ALU.mult), reads=[tla, tt3], writes=[tt3])
                    P.op("dve", C.tensor_tensor(out=t3[:, 0:NT], in0=t3[:, 0:NT], in1=t1[:, 0:NT], op=ALU.mult), reads=[tt1, tt3], writes=[tt3])
                    P.op("dve", C.scalar_tensor_tensor(out=t2[:, 0:NT], in0=bz[:, 0:NT], scalar=psc[:, m:m + 1], in1=t2[:, 0:NT], op0=ALU.mult, op1=ALU.mult), reads=[tz, tt2, t_g], writes=[tt2])
                    P.op("dve", C.tensor_tensor(out=merged[:, m, cs_], in0=t3[:, 0:NT], in1=t2[:, 0:NT], op=ALU.add), reads=[tt2, tt3], writes=[t_merged])
            if blk == DBLK: tap('merged', merged[:], t_merged)
            ckpt('mrg%d' % blk)

            if prev_out is not None:
                prev_out()
            tiles = [(jj, False) for jj in range(4)] + ([(0, True)] if has_s else [])
            for (jj, smp) in tiles:
                stg, tstg = stage[st_i[0] % 2]; st_i[0] += 1
                rows = 64 if smp else 128
                c0 = 512 if smp else jj * 128
                if smp:
                    for t in range(4):
                        load(dq, stg[16 * t:16 * t + 16, :], xsv[t, :, :], tstg)
                else:
                    for j2 in range(2):
                        load(dq, stg[64 * j2:64 * j2 + 64, :], xpv[2 * jj + j2, 64 * blk:64 * blk + 64, :], tstg)
                for half in range(2):
                    bk, tb = nb()
                    for f4 in range(4):
                        fc = half * 4 + f4
                        P.op("pe", C.transpose(out=bk[:, f4 * 128:(f4 + 1) * 128], in_=stg[:, fc * 128:(fc + 1) * 128], identity=ident[:]),
                             reads=[tstg, t_ident], writes=[tb], sig=(f4 == 3))
                    P.op("act", C.activation(out=xT[:, half * 4:half * 4 + 4, c0:c0 + rows], in_=bk[:, :].rearrange("p (a b) -> p a b", a=4)[:, :, 0:rows], func=AF.Copy),
                         reads=[tb], writes=t_xT[half * 4:half * 4 + 4])
            def evac_res(kind_gate):
                def f(mc, bk, tb, c0, NT, smp):
                    if not smp:
                        P.op("dve", C.scalar_tensor_tensor(out=xT[:, mc, c0:c0 + NT], in0=bk[:, 0:NT], scalar=mod[:, kind_gate * 8 + mc, 0:1], in1=xT[:, mc, c0:c0 + NT], op0=ALU.mult, op1=ALU.add), reads=[tb, t_mod, t_xT[mc]], writes=[t_xT[mc]])
                    else:
                        tm, ttm = tq()
                        P.op("dve", C.tensor_tensor(out=V(tm[:, 0:NT], smp), in0=V(bk[:, 0:NT], smp), in1=MB(mod, kind_gate * 8 + mc, smp), op=ALU.mult), reads=[tb, t_mod], writes=[ttm])
                        P.op("dve", C.tensor_tensor(out=xT[:, mc, c0:c0 + NT], in0=xT[:, mc, c0:c0 + NT], in1=tm[:, 0:NT], op=ALU.add), reads=[ttm, t_xT[mc]], writes=[t_xT[mc]])
                return f
            linear_now("out", 8, evac_res(2), merged, t_merged, step_next)
            if blk == DBLK: tap('x1', xT[:], t_xT[7])
            ckpt('wout%d' % blk)

            rmsnorm(A2, 3)

            for fh in range(2):
                for fl in range(11):
                    ff = fh * 11 + fl
                    sl, tsl = wl(("ffi", ff))
                    for (c0, NT, smp) in parts:
                        cs_ = slice(c0, c0 + NT)
                        ba, ta = nb(); bb, tbb_ = nb()
                        for k in range(8):
                            P.op("pe", C.matmul(ba[:, 0:NT], lhsT=sl[:, k, 0:128], rhs=h[:, k, cs_], start=(k == 0), stop=(k == 7)), reads=[tsl, t_h], writes=[ta], sig=(k == 7))
                        for k in range(8):
                            P.op("pe", C.matmul(bb[:, 0:NT], lhsT=sl[:, k, 128:256], rhs=h[:, k, cs_], start=(k == 0), stop=(k == 7)), reads=[tsl, t_h], writes=[tbb_], sig=(k == 7))
                        tm, ttm = tq()
                        P.op("act", C.activation(out=tm[:, 0:NT], in_=ba[:, 0:NT], func=AF.Silu), reads=[ta], writes=[ttm])
                        P.op("dve", C.tensor_tensor(out=actb[:, fl, cs_], in0=tm[:, 0:NT], in1=bb[:, 0:NT], op=ALU.mult), reads=[tbb_, ttm], writes=[t_act])
                    step_next()
                if blk == DBLK and fh == 0: tap('act', actb[:], t_act)
                for m in range(8):
                    sl, tsl = wl(("ffo", fh, m))
                    for (c0, NT, smp) in parts:
                        bk, tb = nb()
                        for k in range(11):
                            P.op("pe", C.matmul(bk[:, 0:NT], lhsT=sl[:, k, :], rhs=actb[:, k, c0:c0 + NT], start=(k == 0), stop=(k == 10)), reads=[tsl, t_act], writes=[tb], sig=(k == 10))
                        evac_res(5)(m, bk, tb, c0, NT, smp)
                    step_next()
            if nxt is not None:
                for _ in nxt:
                    pass
            if blk == DBLK: tap('x2', xT[:], t_xT[7])
            ckpt('ffo%d' % blk)

            rmsnorm(None, 0, final=True)
            ckpt('fn%d' % blk)
            def out_fn():
                for (jj, smp) in tiles:
                    rows = 64 if smp else 128
                    c0 = 512 if smp else jj * 128
                    stg, tstg = stage[st_i[0] % 2]; st_i[0] += 1
                    for half in range(2):
                        bk, tb = nb()
                        for f4 in range(4):
                            fc = half * 4 + f4
                            cw = 64 if smp else 128
                            P.op("pe", C.transpose(out=bk[0:cw, f4 * 128:(f4 + 1) * 128], in_=xT[:, fc, c0:c0 + cw], identity=ident[:]), reads=[t_xT[fc], t_ident], writes=[tb], sig=(f4 == 3))
                        P.op("act", C.activation(out=stg[0:rows, half * 512:(half + 1) * 512], in_=bk[0:rows, :], func=AF.Copy), reads=[tb], writes=[tstg])
                    if smp:
                        for t in range(4):
                            out_events.append(store("pool", ysv[t, :, :], stg[16 * t:16 * t + 16, :], tstg))
                    else:
                        for j2 in range(2):
                            out_events.append(store("pool", ypv[2 * jj + j2, 64 * blk:64 * blk + 64, :], stg[64 * j2:64 * j2 + 64, :], tstg))

            return out_fn
        NBM[0] = 6
        g0 = early(0)
        g0_state = [None]
        def step_g0(n):
            for _ in range(n):
                if g0_state[0] in ("ssm", "end"):
                    return
                g0_state[0] = next(g0, "end")
        for go in range(4):
            gs = slice(8 * go, 8 * go + 8)
            def cmul4(outr, outi, xr, xi, yr, yi, nk):
                def bx(a): return a.unsqueeze(2).to_broadcast([64, 8, nk, 16])
                def by(a): return a.unsqueeze(3).to_broadcast([64, 8, nk, 16])
                dv(C.tensor_tensor(out=v1[:, :, 0:nk, :], in0=bx(xr), in1=by(yr), op=ALU.mult))
                dv(C.tensor_tensor(out=v2[:, :, 0:nk, :], in0=bx(xi), in1=by(yi), op=ALU.mult))
                dv(C.tensor_tensor(out=outr, in0=v1[:, :, 0:nk, :], in1=v2[:, :, 0:nk, :], op=ALU.subtract))
                dv(C.tensor_tensor(out=v1[:, :, 0:nk, :], in0=bx(xr), in1=by(yi), op=ALU.mult))
                dv(C.tensor_tensor(out=v2[:, :, 0:nk, :], in0=bx(xi), in1=by(yr), op=ALU.mult))
                dv(C.tensor_tensor(out=outi, in0=v1[:, :, 0:nk, :], in1=v2[:, :, 0:nk, :], op=ALU.add))
            cmul4(Er[0:64], Ei[0:64], Bbr[:, gs, :], Bbi[:, gs, :], PWrev[:, 0, gs, :], PWrev[:, 1, gs, :], 8)
            cmul4(Rr[0:64], Ri[0:64], Bbr[:, gs, :], Bbi[:, gs, :], NPW[:, 0, gs, :], NPW[:, 1, gs, :], 8)
            cmul4(Qr[0:64], Qi[0:64], Cr[:, gs, :], Ci[:, gs, :], PW[:, 0, gs, :], PW[:, 1, gs, :], 9)
            dv(C.tensor_scalar(out=nQi[0:64].rearrange("p g k c -> p (g k c)"), in0=Qi[0:64].rearrange("p g k c -> p (g k c)"), scalar1=-1.0, scalar2=None, op0=ALU.mult))
            step_g0(10)
            for gl in range(8):
                g = 8 * go + gl
                P.op("dve", C.tensor_copy(out=Wst[0:64, g, :].rearrange("p (k c) -> p k c", k=8), in_=Qr[0:64, gl, 1:9, :]), reads=[t_ssm], writes=[t_W])
                P.op("act", C.activation(out=Wst[64:128, g, :].rearrange("p (k c) -> p k c", k=8), in_=nQi[0:64, gl, 1:9, :], func=AF.Copy), reads=[t_ssm], writes=[t_W])
            for gl in range(8):
                g = 8 * go + gl
                bk, tb = nb()
                P.op("pe", C.transpose(out=bk[:, 0:128], in_=Er[:, gl, :, :].rearrange("p j c -> p (j c)"), identity=ident[:]), reads=[t_ssm, t_ident], writes=[tb], sig=False)
                P.op("pe", C.transpose(out=bk[:, 128:256], in_=Ei[:, gl, :, :].rearrange("p j c -> p (j c)"), identity=ident[:]), reads=[t_ssm, t_ident], writes=[tb], sig=False)
                P.op("pe", C.matmul(bk[:, 256:384], lhsT=Rr[:, gl, :, :].rearrange("p j c -> p (j c)"), rhs=Qr[:, gl, 0:8, :].rearrange("p k c -> p (k c)"), start=True, stop=False), reads=[t_ssm], writes=[tb], sig=False)
                P.op("pe", C.matmul(bk[:, 256:384], lhsT=Ri[:, gl, :, :].rearrange("p j c -> p (j c)"), rhs=nQi[:, gl, 0:8, :].rearrange("p k c -> p (k c)"), start=False, stop=True), reads=[t_ssm], writes=[tb], sig=True)
                P.op("act", C.activation(out=Wend[:, g, :].rearrange("p (a b) -> p a b", a=2), in_=bk[:, 0:256].rearrange("p (a b) -> p a b", a=2)[:, :, 0:64], func=AF.Copy), reads=[tb], writes=[t_W])
                P.op("dve", C.tensor_tensor(out=kt[:], in0=bk[:, 256:384], in1=mask[:], op=ALU.mult), reads=[tb, t_mask], writes=[t_kt])
                P.op("dve", C.scalar_tensor_tensor(out=Kloc[:, g, :], in0=ident[:], scalar=D8[:, g:g + 1], in1=kt[:], op0=ALU.mult, op1=ALU.add), reads=[t_kt, t_ident, t_D8], writes=[t_W])
        dv(C.tensor_tensor(out=tA[:], in0=PW[:, 0, :, 8], in1=PW[:, 0, :, 8], op=ALU.mult))
        dv(C.tensor_tensor(out=tB[:], in0=PW[:, 1, :, 8], in1=PW[:, 1, :, 8], op=ALU.mult))
        dv(C.tensor_tensor(out=tA[:], in0=tA[:], in1=tB[:], op=ALU.add))
        ac(C.activation(out=RHO[:], in_=tA[:], func=AF.Sqrt))
        dv(C.reciprocal(out=tC[:], in_=RHO[:]))
        ur = sbt([64, G], "ur"); ui = sbt([64, G], "ui")
        dv(C.tensor_tensor(out=ur[:], in0=PW[:, 0, :, 8], in1=tC[:], op=ALU.mult))
        dv(C.tensor_tensor(out=ui[:], in0=PW[:, 1, :, 8], in1=tC[:], op=ALU.mult))
        dv(C.memset(ROT[:, 0, :, 0], 1.0)); dv(C.memset(ROT[:, 1, :, 0], 0.0))
        w1 = sbt([64, G, 32], "w1"); w2 = sbt([64, G, 32], "w2")
        for lvl in range(6):
            n0 = 1 << lvl
            def bu(a): return a.unsqueeze(2).to_broadcast([64, G, n0])
            src_r = ROT[:, 0, :, 0:n0]; src_i = ROT[:, 1, :, 0:n0]
            dv(C.tensor_tensor(out=w1[:, :, 0:n0], in0=src_r, in1=bu(ur[:]), op=ALU.mult))
            dv(C.tensor_tensor(out=w2[:, :, 0:n0], in0=src_i, in1=bu(ui[:]), op=ALU.mult))
            dv(C.tensor_tensor(out=ROT[:, 0, :, n0:2 * n0], in0=w1[:, :, 0:n0], in1=w2[:, :, 0:n0], op=ALU.subtract))
            dv(C.tensor_tensor(out=w1[:, :, 0:n0], in0=src_r, in1=bu(ui[:]), op=ALU.mult))
            dv(C.tensor_tensor(out=w2[:, :, 0:n0], in0=src_i, in1=bu(ur[:]), op=ALU.mult))
            dv(C.tensor_tensor(out=ROT[:, 1, :, n0:2 * n0], in0=w1[:, :, 0:n0], in1=w2[:, :, 0:n0], op=ALU.add))
            if lvl < 5:
                dv(C.tensor_tensor(out=tA[:], in0=ur[:], in1=ur[:], op=ALU.mult))
                dv(C.tensor_tensor(out=tB[:], in0=ui[:], in1=ui[:], op=ALU.mult))
                dv(C.tensor_tensor(out=tC[:], in0=ur[:], in1=ui[:], op=ALU.mult))
                dv(C.tensor_tensor(out=ur[:], in0=tA[:], in1=tB[:], op=ALU.subtract))
                dv(C.tensor_scalar(out=ui[:], in0=tC[:], scalar1=2.0, scalar2=None, op0=ALU.mult))
        dv(C.tensor_copy(out=AR[:, 0, :], in_=ROT[:, 0, :, 1])); dv(C.tensor_copy(out=AR[:, 1, :], in_=ROT[:, 0, :, 1]))
        dv(C.tensor_scalar(out=AIs[:, 0, :], in0=ROT[:, 1, :, 1], scalar1=-1.0, scalar2=None, op0=ALU.mult)); dv(C.tensor_copy(out=AIs[:, 1, :], in_=ROT[:, 1, :, 1]))

        adaln_A(A2, g2, 4)
        tap('mod', mod[:], t_mod); tap('A1', A1[:], t_A)
        ckpt('adaln')
        tap('PW', PW[:], t_ssm); tap('NPW', NPW[:], t_ssm); tap('Wend', Wend[:], t_W); tap('Kloc', Kloc[:], t_W); tap('Wst', Wst[:], t_W); tap('AR', AR[:], t_ssm); tap('D8', D8[:], t_D8)
        ckpt('ssmpre')
        step_g0(1000)
        bar = P.op("dve", C.memset(tA[:], 0.0), reads=[], writes=[t_ssm, t_c, t_sg, t_silu, t_W])
        bar2 = P.op("dve", C.memset(tB[:], 0.0), reads=[], writes=[t_ssm])
        P.es2 = None
        es2.close()
        _DEFW[0] = bar2
        S = sbt([64, 2, G], "S"); t_S = T()
        P.op("dve", C.memset(S[:], 0.0), writes=[t_S])
        wpl = P.sb([128, 4, 256], BF16, "wpl"); t_wpl = T()
        wps = P.newsem("wpsem")
        P.op("pool", C.dma_start(out=wpl[:], in_=w_pool.rearrange("g c o -> c g o")), writes=[t_wpl], dma=wps)

        xT = sbt([128, 8, NTT], "xT"); t_xT = [T() for _ in range(8)]
        h = P.sb([128, 8, NTT], BF16, "h"); t_h = T()
        rstd = sbt([128, NTT], "rstd"); t_rstd = T()
        tq_bufs = [(sbt([128, NTT], "tq%d" % i), T()) for i in range(3)]
        tq_i = [0]
        def tq():
            b = tq_bufs[tq_i[0] % len(tq_bufs)]; tq_i[0] += 1
            return b
        Spv = P.sb([128, G, 64], BF16, "Spv"); t_Spv = [T() for _ in range(4)]
        Spvs = P.sb([128, G, 16], BF16, "Spvs"); t_Spvs = T()
        yg8 = P.sb([128, G, 64], BF16, "yg8"); t_yg8 = T()
        yg8s = P.sb([128, G, 16], BF16, "yg8s"); t_yg8s = T()
        yfe = usm; t_yfe = t_usm
        merged = P.sb([128, 8, NTT], BF16, "merged"); t_merged = T()
        sq = merged; t_sq = t_merged
        actb = P.sb([128, 11, NTT], BF16, "actb"); t_act = T()
        ZB = [dict(Zin=sbt([64, 2, 8, 64], "zin%d" % i),
                   Z=sbt([64, 2, 8, 64], "zz%d" % i), Sf=sbt([64, 2, 8, 64], "zsf%d" % i),
                   t=T(), tz=[[T() for _ in range(8)] for _ in range(2)]) for i in range(1)]
        Zm1 = sbt([64, 2, G], "Zm1"); t_Zm1 = T()
        S1 = sbt([64, 2, G], "S1"); S2 = sbt([64, 2, G], "S2"); t_ssc = T()
        Xs = ZB[0]["Zin"][:].rearrange("p a b c -> p (a b c)").rearrange("p (r g s) -> p r g s", r=2, g=G); t_Xs = ZB[0]["t"]
        H0 = ZB[0]["Z"][:].rearrange("p a b c -> p (a b c)").rearrange("p (r s g) -> p r s g", r=2, g=G); t_H0 = ZB[0]["t"]
        Ht = ZB[0]["Sf"][:].rearrange("p a b c -> p (a b c)").rearrange("p (r s g) -> p r s g", r=2, g=G); Hu = sbt([128, 2, 16, G], "Hu"); t_ssc0 = T()
        P.op("dve", C.memset(Hu[:].rearrange("p a b c -> p (a b c)"), 0.0), writes=[t_ssc0])


        for _ in g0:
            pass
        prev_out = None
        for blk in range(4):
            nxt = early(blk + 1) if blk < 3 else None
            prev_out = run_main(blk, nxt, prev_out)
        prev_out()
        fw = {}
        for (s, v) in out_events:
            fw[s] = max(fw.get(s, 0), v)
        P.emit([(s, v) for s, v in fw.items()])
    return nc


_NC = None


def kernel(**inputs):
    global _NC
    f = lambda k: np.ascontiguousarray(inputs[k], dtype=np.float32)
    x_prompt = f("x_prompt"); x_sample = f("x_sample"); c_prompt = f("c_prompt"); c_sample = f("c_sample")
    sre = f("state_ssm_re")[0].reshape(128, 2048); sim = f("state_ssm_im")[0].reshape(128, 2048); spool = f("state_pool")[0].reshape(128, 15 * 512)
    shared = dict(
        norm1_g=f("norm1_g")[0], norm2_g=f("norm2_g")[0], normf_g=f("normf_g"), w_ada=f("w_ada")[0], b_ada=f("b_ada")[0], w_in=f("w_in")[0],
        lam_re=f("ssm_lam_re")[0], lam_im=f("ssm_lam_im")[0], log_dt=f("ssm_log_dt"), b_re=f("ssm_b_re")[0], b_im=f("ssm_b_im")[0],
        c_re=f("ssm_c_re")[0].reshape(512, 64), c_im=f("ssm_c_im")[0].reshape(512, 64), ssm_d=f("ssm_d")[0], w_glu=f("w_glu")[0],
        w_pool=f("w_pool")[0], pool_scale=f("pool_scale")[0], w_out=f("w_out")[0], w_ffn_in=f("w_ffn_in")[0], w_ffn_out=f("w_ffn_out")[0])
    in_maps = []
    for i in range(8):
        m = dict(shared)
        m["xp"] = x_prompt[i]
        m["xs"] = x_sample[16 * i:16 * i + 16].reshape(64, D)
        m["cc"] = np.concatenate([c_prompt[i:i + 1], c_sample[16 * i:16 * i + 16]], 0)
        m["st_re"] = sre[16 * i:16 * i + 16]; m["st_im"] = sim[16 * i:16 * i + 16]
        m["st_pool"] = spool[16 * i:16 * i + 16].reshape(240, 512)
        in_maps.append(m)
    if _NC is None:
        _NC = build_nc()
    res = run_bass_kernel_spmd(_NC, in_maps, core_ids=list(range(8)))
    R = res.results
    y_prompt = np.stack([r["yp"] for r in R], 0)
    y_sample = np.concatenate([r["ys"].reshape(16, 4, D) for r in R], 0)
    p_re = np.stack([r["p_re"].reshape(G, 64) for r in R], 0)[None]
    p_im = np.stack([r["p_im"].reshape(G, 64) for r in R], 0)[None]
    p_pool = np.stack([r["p_pool"] for r in R], 0)[None]
    s_re = np.concatenate([r["s_re"].reshape(16, G, 64) for r in R], 0)[None]
    s_im = np.concatenate([r["s_im"].reshape(16, G, 64) for r in R], 0)[None]
    s_pool = np.concatenate([r["s_pool"].reshape(16, 15, 512) for r in R], 0)[None]
    return (y_prompt.astype(np.float32), y_sample.astype(np.float32), p_re.astype(np.float32), p_im.astype(np.float32),
            p_pool.astype(np.float32), s_re.astype(np.float32), s_im.astype(np.float32), s_pool.astype(np.float32))
```

```python
import math
import numpy as np
from contextlib import ExitStack
import concourse.bass as bass
import concourse.mybir as mybir
from concourse.bass_utils import run_bass_kernel_spmd

F32 = mybir.dt.float32
BF16 = mybir.dt.bfloat16
AF = mybir.ActivationFunctionType
ALU = mybir.AluOpType
D = 1024; DFF = 2816; G = 32; NPST = 64; EPS = 1e-6
SELFSYNC = True
RAW_ONLY = True
DBG = []


class _Rec:
    def __getattr__(self, name):
        return lambda *a, **k: (name, a, k)


C = _Rec()


class Sem:
    def __init__(self, h):
        self.h = h; self.v = 0


_DEFW = [None]


class T:
    def __init__(self, name=""):
        self.w = _DEFW[0]; self.r = []; self.name = name


class Prog:
    EMAP = dict(pe="tensor", act="scalar", dve="vector", pool="gpsimd", sp="sync")

    def __init__(self, nc, es):
        self.nc = nc; self.es = es
        self.q = {e: [] for e in self.EMAP}
        self.cnt = {e: 0 for e in self.EMAP}
        self.sem = {e: Sem(es.enter_context(nc.semaphore("s_" + e))) for e in self.EMAP}
        self.waited = {e: {} for e in self.EMAP}
        self.nsb = 0; self.es2 = None; self.stopped = False

    def sb(self, shape, dt, name=None):
        self.nsb += 1
        es = self.es2 if (self.es2 is not None) else self.es
        return es.enter_context(self.nc.sbuf_tensor(name or ("sb%d" % self.nsb), list(shape), dt))

    def ps(self, shape, dt, name):
        return self.es.enter_context(self.nc.psum_tensor(name, list(shape), dt))

    def newsem(self, name):
        return Sem(self.es.enter_context(self.nc.semaphore(name)))

    def op(self, eng, fn, reads=(), writes=(), dma=None, sig=True):
        if self.stopped:
            return (self.sem['dve'], 0)
        deps = set()
        own = self.sem[eng]
        if dma is not None and getattr(dma, 'guard', 0) > 0:
            deps.add((dma, dma.guard))
        for t in reads:
            if t.w is not None: deps.add(t.w)
        for t in writes:
            if t.w is not None and not (RAW_ONLY and t.w[0] is own and dma is None): deps.add(t.w)
            for ev_ in t.r:
                if not (RAW_ONLY and ev_[0] is own and dma is None): deps.add(ev_)
        if dma is not None:
            dma.v += 16; ev = (dma, dma.v); inc = (dma, 16)
        elif sig:
            self.cnt[eng] += 1; ev = (own, self.cnt[eng]); inc = (own, 1)
        else:
            ev = (own, self.cnt[eng] + 1); inc = None
        waits = []
        for (s, v) in sorted(deps, key=lambda sv: id(sv[0])):
            if s is own and (eng == "pe" or not SELFSYNC or dma is not None and False):
                continue
            if s is own and dma is None and v >= ev[1]:
                continue
            if self.waited[eng].get(s, 0) >= v: continue
            self.waited[eng][s] = v; waits.append((s, v))
        self.q[eng].append((fn, waits, inc))
        for t in reads: t.r.append(ev)
        for t in writes:
            t.w = ev; t.r = []
        return ev

    def emit(self, final_waits):
        with self.nc.Block() as block:
            for eng, ename in self.EMAP.items():
                ops = self.q[eng]
                extra = final_waits if eng == "sp" else []
                if not ops and not extra: continue
                def body(e, ops=ops, extra=extra):
                    for fn, waits, inc in ops:
                        for (s, v) in waits:
                            e.wait_ge(s.h, v)
                        ins = getattr(e, fn[0])(*fn[1], **fn[2])
                        if inc is not None:
                            ins.then_inc(inc[0].h, inc[1])
                    for (s, v) in extra:
                        e.wait_ge(s.h, v)
                getattr(block, ename)(body)


def build_nc(stop_at=None, dbg=()):
    nc = bass.Bass("TRN2", target_bir_lowering=False)
    def din(name, shape): return nc.dram_tensor(name, list(shape), F32, kind="ExternalInput").ap()
    def dout(name, shape): return nc.dram_tensor(name, list(shape), F32, kind="ExternalOutput").ap()
    xp = din("xp", [2048, D]); xs = din("xs", [64, D]); cc = din("cc", [17, D])
    st_re = din("st_re", [16, 2048]); st_im = din("st_im", [16, 2048]); st_pool = din("st_pool", [240, 512])
    norm1_g = din("norm1_g", [D]); norm2_g = din("norm2_g", [D]); normf_g = din("normf_g", [D])
    w_ada = din("w_ada", [D, 6 * D]); b_ada = din("b_ada", [6 * D]); w_in = din("w_in", [D, 3072])
    lam_re = din("lam_re", [G, 64]); lam_im = din("lam_im", [G, 64]); log_dt = din("log_dt", [1, G])
    b_re = din("b_re", [G, 64, 16]); b_im = din("b_im", [G, 64, 16]); c_re = din("c_re", [512, 64]); c_im = din("c_im", [512, 64])
    ssm_d = din("ssm_d", [512]); w_glu = din("w_glu", [512, 2048]); w_pool = din("w_pool", [4, 128, 256])
    pool_scale = din("pool_scale", [D]); w_out = din("w_out", [D, D]); w_ffn_in = din("w_ffn_in", [D, 2 * DFF]); w_ffn_out = din("w_ffn_out", [DFF, D])
    yp = dout("yp", [2048, D]); ys = dout("ys", [64, D]); p_re = dout("p_re", [2048]); p_im = dout("p_im", [2048])
    p_pool = dout("p_pool", [15, 512]); s_re = dout("s_re", [16, 2048]); s_im = dout("s_im", [16, 2048]); s_pool = dout("s_pool", [240, 512])
    Ud = nc.dram_tensor("Ud", [512, 8, 64], BF16, kind="Internal").ap()
    Yd = nc.dram_tensor("Yd", [512, 8, 64], BF16, kind="Internal").ap()
    dbg_outs = {}

    with ExitStack() as es:
        es.enter_context(nc.allow_non_contiguous_dma(reason="small strided parameter loads"))
        P = Prog(nc, es)
        out_events = []
        def ckpt(name):
            if stop_at == name:
                P.stopped = True
        def tap(name, ap, t):
            if name in dbg and not P.stopped:
                d = nc.dram_tensor("dbg_" + name, list(ap.shape), ap.dtype, kind="ExternalOutput").ap()
                out_events.append(store("pool", d, ap, t))
        banks = [(P.ps([128, 512], F32, "bank%d" % i), T("bank%d" % i)) for i in range(8)]
        bank_i = [0]
        NBM = [8]
        def nb():
            b = banks[bank_i[0] % NBM[0]]; bank_i[0] += 1
            return b
        banke_i = [0]
        def nb_e():
            b = banks[6 + banke_i[0] % 2]; banke_i[0] += 1
            return b
        dpool = [P.newsem("dp%d" % i) for i in range(20)]
        lsem = [0]
        def psem():
            lsem[0] += 1
            sm = dpool[lsem[0] % len(dpool)]
            sm.guard = sm.v
            return sm
        dpool2 = [P.newsem("dq%d" % i) for i in range(8)]
        lsem2 = [0]
        def psem2():
            lsem2[0] += 1
            sm = dpool2[lsem2[0] % len(dpool2)]
            sm.guard = sm.v
            return sm
        def psem_q(q):
            return psem2() if q == "sp" else psem()
        def load(eng, out_ap, in_ap, t, reads=()):
            s = psem_q(eng)
            return P.op(eng, C.dma_start(out=out_ap, in_=in_ap), reads=reads, writes=[t], dma=s)
        stsem = [0]
        def store(eng, out_ap, in_ap, t):
            s = psem()
            ev = P.op(eng, C.dma_start(out=out_ap, in_=in_ap), reads=[t], dma=s)
            return ev
        ident = P.sb([128, 128], F32, "ident"); t_ident = T()
        onesf = P.sb([128, 128], F32, "onesf"); t_ones = T()
        onesb = P.sb([128, 128], BF16, "onesb")
        mask = P.sb([128, 128], F32, "mask"); t_mask = T()
        P.op("dve", C.memset(onesf[:], 1.0), writes=[t_ones])
        P.op("dve", C.memset(onesb[:], 1.0), writes=[t_ones])
        P.op("pool", C.affine_select(out=ident[:], in_=onesf[:], pattern=[[-1, 128]], compare_op=ALU.is_equal, fill=0.0, base=0, channel_multiplier=1), reads=[t_ones], writes=[t_ident])
        P.op("pool", C.affine_select(out=mask[:].rearrange("p (i c) -> p i c", i=8), in_=onesf[:].rearrange("p (i c) -> p i c", i=8), pattern=[[16, 8], [0, 16]], compare_op=ALU.is_ge, fill=0.0, base=15, channel_multiplier=-1), reads=[t_ones], writes=[t_mask])

        NSLOT = 3
        wslots = [(P.sb([128, 8, 256], BF16, "wslot%d" % i), T(), P.newsem("wsem%d" % i)) for i in range(NSLOT)]
        ws_i = [0]
        wsems_sw = [P.newsem("wsemsw%d" % i) for i in range(NSLOT)]
        def wload_direct(mk):
            sl, t, s_ = wslots[ws_i[0] % NSLOT]; s_ = wsems_sw[ws_i[0] % NSLOT]; ws_i[0] += 1
            for (o, i_) in mk(sl):
                P.op("pool", C.dma_start(out=o, in_=i_), writes=[t], dma=s_)
            return sl, t
        w2slots = [(P.sb([128, 11, 128], BF16, "w2slot%d" % i), T(), P.newsem("w2sem%d" % i)) for i in range(2)]
        w2_i = [0]
        NSL = 46
        Wd = nc.dram_tensor("Wd", [NSL, 128, 2048], BF16, kind="Internal").ap()
        Wd2 = nc.dram_tensor("Wd2", [16, 128, 1408], BF16, kind="Internal").ap()
        wreg = {}
        def precast(key, mk, kind="w"):
            idx = sum(1 for v_ in wreg.values() if v_[0] == kind)
            tw = T()
            view = Wd[idx].rearrange("p (k c) -> p k c", k=8) if kind == "w" else Wd2[idx].rearrange("p (k c) -> p k c", k=11)
            sm = psem()
            for (o, i_) in mk(view):
                P.op("pool", C.dma_start(out=o, in_=i_), writes=[tw], dma=sm)
            wreg[key] = (kind, idx, tw)
        def wload(key, mk, kind="w"):
            if kind == "w":
                sl, t, s_ = wslots[ws_i[0] % NSLOT]; ws_i[0] += 1
            else:
                sl, t, s_ = w2slots[w2_i[0] % 2]; w2_i[0] += 1
            flat = sl[:].rearrange("p k c -> p (k c)")
            if key not in wreg:
                idx = sum(1 for v_ in wreg.values() if v_[0] == kind)
                tw = T()
                dst = Wd[idx] if kind == "w" else Wd2[idx]
                for (o, i_) in mk(sl):
                    P.op("pool", C.dma_start(out=o, in_=i_), writes=[t], dma=psem())
                P.op("sp", C.dma_start(out=dst, in_=flat), reads=[t], writes=[tw], dma=psem2())
                wreg[key] = (kind, idx, tw)
            else:
                kind_, idx, tw = wreg[key]
                src = Wd[idx] if kind == "w" else Wd2[idx]
                if key[0] == "glu":
                    P.op("sp", C.dma_start(out=flat[:, 0:1024], in_=src[:, 0:1024]), reads=[tw], writes=[t], dma=s_)
                else:
                    P.op("sp", C.dma_start(out=flat, in_=src), reads=[tw], writes=[t], dma=s_)
            return sl, t
        PW = P.sb([64, 2, G, 9], F32, "PW"); NPW = P.sb([64, 2, G, 8], F32, "NPW")
        Wend = P.sb([128, G, 128], BF16, "Wend"); Kloc = P.sb([128, G, 128], BF16, "Kloc")
        Wst = P.sb([128, G, 128], BF16, "Wst")
        AR = P.sb([64, 2, G], F32, "AR"); AIs = P.sb([64, 2, G], F32, "AIs")
        RHO = P.sb([64, G], F32, "RHO"); ROT = P.sb([64, 2, G, 64], F32, "ROT")
        D8 = P.sb([128, G], F32, "D8")
        mod = P.sb([128, 48, 17], F32, "mod")
        A1 = P.sb([128, 8, 17], F32, "A1"); A2 = P.sb([128, 8, 17], F32, "A2")
        siluT = P.sb([128, 8, 17], BF16, "siluT")
        bada = P.sb([128, 48], F32, "bada")
        g1 = P.sb([128, 8], F32, "g1"); g2 = P.sb([128, 8], F32, "g2"); gf = P.sb([128, 8], F32, "gf")
        psc = P.sb([128, 8], F32, "psc"); dsk = P.sb([128, 4], F32, "dsk")
        NTT = 576
        SBLK = 1
        hB = P.sb([128, 8, NTT], BF16, "hB"); t_hB = T()
        tqe_bufs = [(P.sb([128, NTT], F32, "tqe%d" % i), T()) for i in range(2)]
        tqe_i = [0]
        def tq_e():
            b = tqe_bufs[tqe_i[0] % 2]; tqe_i[0] += 1
            return b
        usm = P.sb([128, 4, NTT], BF16, "usm"); t_usm = T()
        pooled = P.sb([128, 4, NTT], BF16, "pooled"); t_pooled = T()
        U8 = P.sb([128, G, 64], BF16, "U8"); t_U8 = T()
        U8s = P.sb([128, G, 16], BF16, "U8s"); t_U8s = T()
        stage = [(P.sb([128, D], F32, "stage%d" % i), T()) for i in range(2)]
        st_i = [0]
        s1 = P.sb([128, 8, 66], F32, "s1"); s2 = P.sb([128, 8, 66], F32, "s2"); t_s = T()
        P.op("dve", C.memset(s1[:].rearrange("p a b -> p (a b)"), 0.0), writes=[t_s])
        P.op("dve", C.memset(s2[:].rearrange("p a b -> p (a b)"), 0.0), writes=[t_s])
        E1 = P.sb([128, 8, 66], F32, "E1"); t_E1 = T()
        Ehist = P.sb([128, 4, 8, 2], F32, "Ehist"); t_Eh = T()
        P.op("dve", C.memset(Ehist[:].rearrange("p a b c -> p (a b c)"), 0.0), writes=[t_Eh])
        PPt = P.sb([128, 64], F32, "PPt"); t_PPt = T()
        ssb = [(P.sb([128, 4], F32, "ssb%d" % i), T()) for i in range(2)]
        ss_i = [0]
        t_Ud = T(); t_Yd = T(); t_Uds = T(); t_Yds = T()
        Uds = nc.dram_tensor("Uds", [512, 8, 16], BF16, kind="Internal").ap()
        Yds = nc.dram_tensor("Yds", [512, 8, 16], BF16, kind="Internal").ap()
        Es = P.sb([128, 4, 16, 19], F32, "Es"); t_Es = T()
        epsc = P.sb([128, 1], F32, "epsc"); t_eps = T()
        P.op("dve", C.memset(epsc[:], EPS), writes=[t_eps])
        xpv = xp.rearrange("(n j) f -> j n f", j=8)
        xsv = xs.rearrange("(s t) f -> t s f", t=4)
        ypv = yp.rearrange("(n j) f -> j n f", j=8)
        ysv = ys.rearrange("(s t) f -> t s f", t=4)
        win_v = w_in.rearrange("(k p) n -> p k n", p=128)
        wglu_v = w_glu.rearrange("(k p) n -> p k n", p=128)
        wout_v = w_out.rearrange("(k p) n -> p k n", p=128)
        wfi_v = w_ffn_in.rearrange("(k p) n -> p k n", p=128)
        wfo_v = w_ffn_out.rearrange("(k p) n -> p k n", p=128)
        es2 = ExitStack(); P.es2 = es2
        t_mod = T(); t_A = T(); t_silu = T(); t_bada = T(); t_g = T(); t_c = T(); t_sg = T()
        csb = P.sb([128, D], F32, "csb")
        P.op("dve", C.memset(csb[:], 0.0), writes=[t_c])
        load("sp", csb[0:17, :], cc[:, :], t_c)
        load("sp", bada[:], b_ada.rearrange("(m p) -> p m", p=128), t_bada)
        load("sp", g1[:], norm1_g.rearrange("(m p) -> p m", p=128), t_g)
        load("sp", g2[:], norm2_g.rearrange("(m p) -> p m", p=128), t_g)
        load("sp", gf[:], normf_g.rearrange("(m p) -> p m", p=128), t_g)
        load("sp", psc[:], pool_scale.rearrange("(m p) -> p m", p=128), t_g)
        sg = P.sb([128, 8, 17], F32, "sg")
        cT = P.sb([128, 8, 17], F32, "cT")
        for half in range(2):
            bk, tb = nb()
            for f4 in range(4):
                fc = half * 4 + f4
                P.op("pe", C.transpose(out=bk[:, f4 * 128:(f4 + 1) * 128], in_=csb[:, fc * 128:(fc + 1) * 128], identity=ident[:]), reads=[t_c, t_ident], writes=[tb], sig=(f4 == 3))
            P.op("act", C.activation(out=sg[:, half * 4:half * 4 + 4, :], in_=bk[:, :].rearrange("p (a b) -> p a b", a=4)[:, :, 0:17], func=AF.Sigmoid), reads=[tb], writes=[t_sg])
            P.op("dve", C.tensor_copy(out=cT[:, half * 4:half * 4 + 4, :], in_=bk[:, :].rearrange("p (a b) -> p a b", a=4)[:, :, 0:17]), reads=[tb], writes=[t_sg])
        P.op("dve", C.tensor_tensor(out=siluT[:], in0=sg[:], in1=cT[:], op=ALU.mult), reads=[t_sg], writes=[t_silu])
        wada_v = w_ada.rearrange("(k p) n -> p k n", p=128)
        def adaln_slices(lo, hi):
            for sl_i in range(lo, hi):
                sl, tsl = wload_direct(lambda sl, sl_i=sl_i: [(sl[:, :, :], wada_v[:, :, sl_i * 256:(sl_i + 1) * 256])])
                bk, tb = nb()
                for sub in range(2):
                    for k in range(8):
                        P.op("pe", C.matmul(bk[:, sub * 17:(sub + 1) * 17], lhsT=sl[:, k, sub * 128:(sub + 1) * 128], rhs=siluT[:, k, :], start=(k == 0), stop=(k == 7)),
                             reads=[tsl, t_silu], writes=[tb], sig=(k == 7))
                for sub in range(2):
                    mc = sl_i * 2 + sub
                    P.op("act", C.activation(out=mod[:, mc, :], in_=bk[:, sub * 17:(sub + 1) * 17], func=AF.Identity, bias=bada[:, mc:mc + 1]), reads=[tb, t_bada], writes=[t_mod])
        def adaln_A(A, g_, kind):
            P.op("dve", C.tensor_scalar(out=A[:], in0=mod[:, kind * 8:(kind + 1) * 8, :], scalar1=1.0, scalar2=None, op0=ALU.add), reads=[t_mod], writes=[t_A])
            P.op("dve", C.tensor_tensor(out=A[:], in0=A[:], in1=g_[:].unsqueeze(2).to_broadcast([128, 8, 17]), op=ALU.mult), reads=[t_g, t_A], writes=[t_A])
        adaln_slices(0, 24)
        t_ssm = T()
        lr = P.sb([64, G], F32, "lr"); li = P.sb([64, G], F32, "li"); dtr = P.sb([128, G], F32, "dtr"); dtb = P.sb([64, G], F32, "dtb")
        load("sp", lr[:], lam_re.rearrange("g p -> p g"), t_ssm)
        load("sp", li[:], lam_im.rearrange("g p -> p g"), t_ssm)
        P.op("dve", C.memset(dtr[:], 0.0), writes=[t_ssm])
        load("sp", dtr[0:1, :], log_dt[:, :], t_ssm)
        load("sp", dsk[:], ssm_d.rearrange("(m p) -> p m", p=128), t_g)
        P.op("act", C.activation(out=dtr[0:1, :], in_=dtr[0:1, :], func=AF.Exp), reads=[t_ssm], writes=[t_ssm])
        bk, tb = nb()
        P.op("pe", C.matmul(bk[0:64, 0:G], lhsT=onesf[:, 0:64], rhs=dtr[:, :], start=True, stop=True), reads=[t_ssm, t_ones], writes=[tb])
        P.op("dve", C.tensor_copy(out=dtb[:], in_=bk[0:64, 0:G]), reads=[tb], writes=[t_ssm])
        def sbt(shape, name): return P.sb(shape, F32, name)
        lrdt = sbt([64, G], "lrdt"); ang = sbt([64, G], "ang"); mag = sbt([64, G], "mag"); cs = sbt([64, G], "cs"); sn = sbt([64, G], "sn")
        tA = sbt([64, G], "tA"); tB = sbt([64, G], "tB"); tC = sbt([64, G], "tC")
        def dv(fn): return P.op("dve", fn, reads=[t_ssm], writes=[t_ssm])
        def ac(fn): return P.op("act", fn, reads=[t_ssm], writes=[t_ssm])
        dv(C.tensor_tensor(out=lrdt[:], in0=lr[:], in1=dtb[:], op=ALU.mult))
        dv(C.tensor_tensor(out=ang[:], in0=li[:], in1=dtb[:], op=ALU.mult))
        ac(C.activation(out=mag[:], in_=lrdt[:], func=AF.Exp))
        halfpi = sbt([64, 1], "halfpi")
        dv(C.memset(halfpi[:], math.pi / 2))
        ac(C.activation(out=sn[:], in_=ang[:], func=AF.Sin, scale=1.0 / 32))
        ac(C.activation(out=cs[:], in_=ang[:], func=AF.Sin, scale=1.0 / 32, bias=halfpi[:, 0:1]))
        for _ in range(5):
            dv(C.tensor_tensor(out=tA[:], in0=cs[:], in1=cs[:], op=ALU.mult))
            dv(C.tensor_tensor(out=tB[:], in0=sn[:], in1=sn[:], op=ALU.mult))
            dv(C.tensor_tensor(out=tC[:], in0=cs[:], in1=sn[:], op=ALU.mult))
            dv(C.tensor_tensor(out=cs[:], in0=tA[:], in1=tB[:], op=ALU.subtract))
            dv(C.tensor_scalar(out=sn[:], in0=tC[:], scalar1=2.0, scalar2=None, op0=ALU.mult))

        dv(C.memset(PW[:, 0, :, 0], 1.0)); dv(C.memset(PW[:, 1, :, 0], 0.0))
        dv(C.memset(NPW[:, 0, :, 0], 1.0)); dv(C.memset(NPW[:, 1, :, 0], 0.0))
        dv(C.tensor_tensor(out=PW[:, 0, :, 1], in0=mag[:], in1=cs[:], op=ALU.mult))
        dv(C.tensor_tensor(out=PW[:, 1, :, 1], in0=mag[:], in1=sn[:], op=ALU.mult))
        m2i = sbt([64, G], "m2i")
        ac(C.activation(out=m2i[:], in_=lrdt[:], func=AF.Exp, scale=-2.0))
        dv(C.tensor_tensor(out=NPW[:, 0, :, 1], in0=PW[:, 0, :, 1], in1=m2i[:], op=ALU.mult))
        dv(C.tensor_tensor(out=tA[:], in0=PW[:, 1, :, 1], in1=m2i[:], op=ALU.mult))
        dv(C.tensor_scalar(out=NPW[:, 1, :, 1], in0=tA[:], scalar1=-1.0, scalar2=None, op0=ALU.mult))
        def cmul_small(outr, outi, ar_, ai_, br_, bi_):
            dv(C.tensor_tensor(out=tA[:], in0=ar_, in1=br_, op=ALU.mult))
            dv(C.tensor_tensor(out=tB[:], in0=ai_, in1=bi_, op=ALU.mult))
            dv(C.tensor_tensor(out=outr, in0=tA[:], in1=tB[:], op=ALU.subtract))
            dv(C.tensor_tensor(out=tA[:], in0=ar_, in1=bi_, op=ALU.mult))
            dv(C.tensor_tensor(out=tB[:], in0=ai_, in1=br_, op=ALU.mult))
            dv(C.tensor_tensor(out=outi, in0=tA[:], in1=tB[:], op=ALU.add))
        for k in range(2, 9):
            cmul_small(PW[:, 0, :, k], PW[:, 1, :, k], PW[:, 0, :, k - 1], PW[:, 1, :, k - 1], PW[:, 0, :, 1], PW[:, 1, :, 1])
        for k in range(2, 8):
            cmul_small(NPW[:, 0, :, k], NPW[:, 1, :, k], NPW[:, 0, :, k - 1], NPW[:, 1, :, k - 1], NPW[:, 0, :, 1], NPW[:, 1, :, 1])
        fr = sbt([64, G], "fr"); fi = sbt([64, G], "fi"); den = sbt([64, G], "den"); nr = sbt([64, G], "nr")
        dv(C.tensor_tensor(out=tA[:], in0=lr[:], in1=lr[:], op=ALU.mult))
        dv(C.tensor_tensor(out=tB[:], in0=li[:], in1=li[:], op=ALU.mult))
        dv(C.tensor_tensor(out=den[:], in0=tA[:], in1=tB[:], op=ALU.add))
        dv(C.reciprocal(out=den[:], in_=den[:]))
        dv(C.tensor_scalar(out=nr[:], in0=PW[:, 0, :, 1], scalar1=-1.0, scalar2=None, op0=ALU.add))
        dv(C.tensor_tensor(out=tA[:], in0=nr[:], in1=lr[:], op=ALU.mult))
        dv(C.tensor_tensor(out=tB[:], in0=PW[:, 1, :, 1], in1=li[:], op=ALU.mult))
        dv(C.tensor_tensor(out=tA[:], in0=tA[:], in1=tB[:], op=ALU.add))
        dv(C.tensor_tensor(out=fr[:], in0=tA[:], in1=den[:], op=ALU.mult))
        dv(C.tensor_tensor(out=tA[:], in0=PW[:, 1, :, 1], in1=lr[:], op=ALU.mult))
        dv(C.tensor_tensor(out=tB[:], in0=nr[:], in1=li[:], op=ALU.mult))
        dv(C.tensor_tensor(out=tA[:], in0=tA[:], in1=tB[:], op=ALU.subtract))
        dv(C.tensor_tensor(out=fi[:], in0=tA[:], in1=den[:], op=ALU.mult))
        Br = sbt([64, G, 16], "Br"); Bi = sbt([64, G, 16], "Bi"); Bbr = sbt([64, G, 16], "Bbr"); Bbi = sbt([64, G, 16], "Bbi")
        u1_ = sbt([64, 8, 16], "u1"); u2_ = sbt([64, 8, 16], "u2"); u1f = sbt([64, G, 16], "u1f"); u2f = sbt([64, G, 16], "u2f")
        load("sp", Br[:], b_re.rearrange("g p c -> p g c"), t_ssm)
        load("sp", Bi[:], b_im.rearrange("g p c -> p g c"), t_ssm)
        def bc_g(ap2):
            return ap2.unsqueeze(2).to_broadcast([64, ap2.shape[1], 16])
        def cmul_gc(outr, outi, xr, xi, yr2, yi2, u1=None, u2=None):
            u1 = u1_ if u1 is None else u1; u2 = u2_ if u2 is None else u2
            dv(C.tensor_tensor(out=u1[:], in0=xr, in1=bc_g(yr2), op=ALU.mult))
            dv(C.tensor_tensor(out=u2[:], in0=xi, in1=bc_g(yi2), op=ALU.mult))
            dv(C.tensor_tensor(out=outr, in0=u1[:], in1=u2[:], op=ALU.subtract))
            dv(C.tensor_tensor(out=u1[:], in0=xr, in1=bc_g(yi2), op=ALU.mult))
            dv(C.tensor_tensor(out=u2[:], in0=xi, in1=bc_g(yr2), op=ALU.mult))
            dv(C.tensor_tensor(out=outi, in0=u1[:], in1=u2[:], op=ALU.add))
        cmul_gc(Bbr[:], Bbi[:], Br[:], Bi[:], fr[:], fi[:], u1f, u2f)
        Cr = sbt([64, G, 16], "Cr"); Ci = sbt([64, G, 16], "Ci")
        cst = sbt([128, 4, 64], "cst"); cst2 = sbt([128, 4, 64], "cst2")
        load("sp", cst[:], c_re.rearrange("(a q) p -> q a p", q=128), t_ssm)
        load("sp", cst2[:], c_im.rearrange("(a q) p -> q a p", q=128), t_ssm)
        win_v0 = w_in.rearrange("(k p) n -> p k n", p=128); wglu_v0 = w_glu.rearrange("(k p) n -> p k n", p=128)
        wout_v0 = w_out.rearrange("(k p) n -> p k n", p=128)
        wfi_v0 = w_ffn_in.rearrange("(k p) n -> p k n", p=128); wfo_v0 = w_ffn_out.rearrange("(k p) n -> p k n", p=128)
        def mk_cols0(wv, K, cols):
            return lambda view: [(view[:, 0:K, ci * 128:(ci + 1) * 128], wv[:, :, c0_:c0_ + 128]) for ci, c0_ in enumerate(cols)]
        for i0 in range(0, 8, 2):
            precast(("u", i0), mk_cols0(win_v0, 8, [128 * i0, 128 * i0 + 128]), "w")
        for m in range(8):
            precast(("g", m), mk_cols0(win_v0, 8, [1024 + 128 * m, 2048 + 128 * m]), "w")
            precast(("glu", m), mk_cols0(wglu_v0, 4, [128 * m, 1024 + 128 * m]), "w")
        for i0 in range(0, 8, 2):
            precast(("out", i0), mk_cols0(wout_v0, 8, [128 * i0, 128 * i0 + 128]), "w")
        for fh in range(2):
            for fl in range(11):
                ff = fh * 11 + fl
                precast(("ffi", ff), mk_cols0(wfi_v0, 8, [128 * ff, DFF + 128 * ff]), "w")
            for m in range(8):
                precast(("ffo", fh, m), (lambda view, fh=fh, m=m: [(view[:, :, :], wfo_v0[:, fh * 11:(fh + 1) * 11, 128 * m:128 * m + 128])]), "w2")
        for (src, dst) in ((cst, Cr), (cst2, Ci)):
            bk, tb = nb()
            for a in range(4):
                P.op("pe", C.transpose(out=bk[0:64, a * 128:(a + 1) * 128], in_=src[:, a, :], identity=ident[:]), reads=[t_ssm, t_ident], writes=[tb], sig=(a == 3))
            P.op("dve", C.tensor_copy(out=dst[:].rearrange("p g c -> p (g c)"), in_=bk[0:64, :]), reads=[tb], writes=[t_ssm])
        t_W = T()
        Er = sbt([128, 8, 8, 16], "Er"); Ei = sbt([128, 8, 8, 16], "Ei")
        Rr = sbt([128, 8, 8, 16], "Rr"); Ri = sbt([128, 8, 8, 16], "Ri")
        Qr = sbt([128, 8, 9, 16], "Qr"); Qi = sbt([128, 8, 9, 16], "Qi"); nQi = sbt([128, 8, 9, 16], "nQi")
        t_D8 = T()
        for i in range(8):
            load("sp", D8[16 * i:16 * i + 16, :], ssm_d.rearrange("(g c) -> c g", c=16), t_D8)
        kt = sbt([128, 128], "kt"); t_kt = T()
        v1 = sbt([64, 8, 9, 16], "v1"); v2 = sbt([64, 8, 9, 16], "v2")
        PWrev = sbt([64, 2, G, 8], "PWrev")
        for j in range(8):
            dv(C.tensor_copy(out=PWrev[:, :, :, j], in_=PW[:, :, :, 7 - j]))
        for tt in (Er, Ei, Rr, Ri, Qr, Qi, nQi):
            dv(C.memset(tt[:].rearrange("p a b c -> p (a b c)"), 0.0))
        adaln_A(A1, g1, 1)
        DBLK = int(dbg[0][1:]) if (dbg and dbg[0].startswith('@')) else -1
        def mk_cols(wv, K, cols):
            return lambda view: [(view[:, 0:K, ci * 128:(ci + 1) * 128], wv[:, :, c0_:c0_ + 128]) for ci, c0_ in enumerate(cols)]
        specs = []
        for i0 in range(0, 8, 2):
            specs.append((("u", i0), mk_cols(win_v, 8, [128 * i0, 128 * i0 + 128]), "w"))
        for m in range(8):
            specs.append((("g", m), mk_cols(win_v, 8, [1024 + 128 * m, 2048 + 128 * m]), "w"))
            specs.append((("glu", m), mk_cols(wglu_v, 4, [128 * m, 1024 + 128 * m]), "w"))
        for i0 in range(0, 8, 2):
            specs.append((("out", i0), mk_cols(wout_v, 8, [128 * i0, 128 * i0 + 128]), "w"))
        for fh in range(2):
            for fl in range(11):
                ff = fh * 11 + fl
                specs.append((("ffi", ff), mk_cols(wfi_v, 8, [128 * ff, DFF + 128 * ff]), "w"))
            for m in range(8):
                specs.append((("ffo", fh, m), (lambda view, fh=fh, m=m: [(view[:, :, :], wfo_v[:, fh * 11:(fh + 1) * 11, 128 * m:128 * m + 128])]), "w2"))
        spec = {key: (mk, kind) for (key, mk, kind) in specs}
        def wl(key):
            mk, kind = spec[key]
            return wload(key, mk, kind)
        def early(blk):
            dq = "sp" if blk == 0 else "pool"
            has_s = (blk == SBLK)
            parts = [(0, 512, False)] + ([(512, 64, True)] if has_s else [])
            def V(ap, smp):
                return ap.rearrange("p (t s) -> p t s", t=4) if smp else ap
            def MB(tab, idx, smp):
                if smp:
                    return tab[:, idx, 1:17].unsqueeze(1).to_broadcast([128, 4, 16])
                return tab[:, idx, 0:1].to_broadcast([128, 512])
            tiles = [(jj, False) for jj in range(4)] + ([(0, True)] if has_s else [])
            def linear(kname, K, evac, src, t_src, gen=False):
                for i0 in range(0, 8, 2):
                    sl, tsl = wl((kname, i0))
                    for ci in range(2):
                        for (c0, NT, smp) in parts:
                            bk, tb = nb_e()
                            for k in range(K):
                                P.op("pe", C.matmul(bk[:, 0:NT], lhsT=sl[:, k, ci * 128:(ci + 1) * 128], rhs=src[:, k, c0:c0 + NT], start=(k == 0), stop=(k == K - 1)),
                                     reads=[tsl, t_src], writes=[tb], sig=(k == K - 1))
                            evac(i0 + ci, bk, tb, c0, NT, smp)
                    if gen:
                        yield

            def linear_now(kname, K, evac, src, t_src):
                for _ in linear(kname, K, evac, src, t_src, False):
                    pass
            for (jj, smp) in tiles:
                stg, tstg = stage[st_i[0] % 2]; st_i[0] += 1
                rows = 64 if smp else 128
                c0 = 512 if smp else jj * 128
                if smp:
                    for t in range(4):
                        load(dq, stg[16 * t:16 * t + 16, :], xsv[t, :, :], tstg)
                else:
                    for j2 in range(2):
                        load(dq, stg[64 * j2:64 * j2 + 64, :], xpv[2 * jj + j2, 64 * blk:64 * blk + 64, :], tstg)
                yield
                yield
                sb_, tsb = ssb[ss_i[0] % 2]; ss_i[0] += 1
                junk = usm[:].rearrange("p a b -> p (a b)")[:, 0:1024]
                P.op("act", C.activation(out=junk, in_=stg[:, :], func=AF.Square, accum_out=sb_[:, 0:1]), reads=[tstg], writes=[t_usm, tsb])
                P.op("act", C.activation(out=sb_[:, 1:2], in_=sb_[:, 0:1], func=AF.Sqrt, scale=1.0 / D, bias=epsc[:, 0:1]), reads=[tsb, t_eps], writes=[tsb])
                P.op("dve", C.reciprocal(out=sb_[:, 2:3], in_=sb_[:, 1:2]), reads=[tsb], writes=[tsb])
                P.op("act", C.activation(out=stg[:, :], in_=stg[:, :], func=AF.Identity, scale=sb_[:, 2:3]), reads=[tsb, tstg], writes=[tstg])
                yield
                bks = []
                for half in range(2):
                    bk, tb = nb_e()
                    bks.append((bk, tb))
                    for f4 in range(4):
                        fc = half * 4 + f4
                        P.op("pe", C.transpose(out=bk[:, f4 * 128:(f4 + 1) * 128], in_=stg[:, fc * 128:(fc + 1) * 128], identity=ident[:]),
                             reads=[tstg, t_ident], writes=[tb], sig=(f4 == 3))
                yield
                for half in range(2):
                    bk, tb = bks[half]
                    for f4 in range(4):
                        fc = half * 4 + f4
                        if not smp:
                            P.op("act", C.activation(out=hB[:, fc, c0:c0 + 128], in_=bk[:, f4 * 128:(f4 + 1) * 128], func=AF.Identity, scale=A1[:, fc, 0:1], bias=mod[:, fc, 0:1]),
                                 reads=[tb, t_A, t_mod], writes=[t_hB])
                        else:
                            tm, ttm = tq_e()
                            P.op("dve", C.tensor_tensor(out=V(tm[:, 0:64], True), in0=V(bk[:, f4 * 128:f4 * 128 + 64], True), in1=MB(A1, fc, True), op=ALU.mult), reads=[tb, t_A], writes=[ttm])
                            P.op("dve", C.tensor_tensor(out=V(hB[:, fc, 512:576], True), in0=V(tm[:, 0:64], True), in1=MB(mod, fc, True), op=ALU.add), reads=[ttm, t_mod], writes=[t_hB])
                yield
            if blk == DBLK: tap('h1', hB[:], t_hB)
            def pool_pc(pc):
                lv = pc + 1
                w = float(1 << lv)
                if has_s:
                    cur = Es[:, pc, :, :]
                    bufs = [s1[:].rearrange("p a b -> p (a b)")[:, 0:304].rearrange("p (s r) -> p s r", r=19), s2[:].rearrange("p a b -> p (a b)")[:, 0:304].rearrange("p (s r) -> p s r", r=19)]
                    for l in range(lv):
                        d = 1 << l
                        dst = bufs[l % 2]
                        P.op("dve", C.tensor_tensor(out=dst[:, :, d:19], in0=cur[:, :, d:19], in1=cur[:, :, 0:19 - d], op=ALU.add), reads=[t_Es, t_s], writes=[t_s])
                        cur = dst
                    P.op("dve", C.scalar_tensor_tensor(out=pooled[:, pc, 512:576].rearrange("p (t s) -> p s t", t=4), in0=cur[:, :, 15:19], scalar=1.0 / w, in1=Es[:, pc, :, 15:19], op0=ALU.mult, op1=ALU.subtract), reads=[t_s, t_Es], writes=[t_pooled])
                cur = E1[:, :, :]
                bufs = [s1, s2]
                for l in range(lv):
                    d = 1 << l
                    dst = bufs[l % 2]
                    if d < 8:
                        P.op("dve", C.tensor_tensor(out=dst[:, d:8, :], in0=cur[:, d:8, :], in1=cur[:, 0:8 - d, :], op=ALU.add), reads=[t_E1, t_s], writes=[t_s])
                        P.op("dve", C.tensor_tensor(out=dst[:, 0:d, 1:66], in0=cur[:, 0:d, 1:66], in1=cur[:, 8 - d:8, 0:65], op=ALU.add), reads=[t_E1, t_s], writes=[t_s])
                    else:
                        P.op("dve", C.tensor_tensor(out=dst[:, :, 1:66], in0=cur[:, :, 1:66], in1=cur[:, :, 0:65], op=ALU.add), reads=[t_E1, t_s], writes=[t_s])
                    cur = dst
                P.op("dve", C.scalar_tensor_tensor(out=pooled[:, pc, 0:512].rearrange("p (j n) -> p j n", j=8), in0=cur[:, :, 2:66], scalar=1.0 / w, in1=E1[:, :, 2:66], op0=ALU.mult, op1=ALU.subtract), reads=[t_s, t_E1], writes=[t_pooled])
                if blk == 0:
                    for t in range(int(w) - 1):
                        j, n = t % 8, t // 8
                        P.op("dve", C.scalar_tensor_tensor(out=pooled[:, pc, j * 64 + n:j * 64 + n + 1], in0=cur[:, j, 2 + n:3 + n], scalar=1.0 / (t + 1), in1=E1[:, j, 2 + n:3 + n], op0=ALU.mult, op1=ALU.subtract), reads=[t_s, t_E1], writes=[t_pooled])
                if blk == 3:
                    P.op("dve", C.tensor_copy(out=PPt[:, pc * 16:(pc + 1) * 16].rearrange("p (n j) -> p n j", n=2), in_=E1[:, :, 64:66].rearrange("p j n -> p n j")), reads=[t_E1], writes=[t_PPt])
                P.op("dve", C.tensor_copy(out=Ehist[:, pc, :, :], in_=E1[:, :, 64:66]), reads=[t_E1, t_pooled], writes=[t_Eh])
            def evac_u(mc, bk, tb, c0, NT, smp):
                if mc < 4:
                    P.op("act", C.activation(out=usm[:, mc, c0:c0 + NT], in_=bk[:, 0:NT], func=AF.Copy), reads=[tb], writes=[t_usm])
                else:
                    pc = mc - 4
                    if smp:
                        P.op("act", C.activation(out=Es[:, pc, :, 15:19], in_=bk[:, 0:64].rearrange("p (t s) -> p s t", t=4), func=AF.Copy), reads=[tb], writes=[t_Es])
                    else:
                        P.op("act", C.activation(out=E1[:, :, 2:66], in_=bk[:, 0:512].rearrange("p (j n) -> p j n", j=8), func=AF.Copy), reads=[tb], writes=[t_E1])
                        P.op("dve", C.tensor_copy(out=E1[:, :, 0:2], in_=Ehist[:, pc, :, :]), reads=[t_Eh], writes=[t_E1])
                    if smp or not has_s:
                        pool_pc(pc)
            if has_s:
                for half in range(2):
                    stg, tstg = stage[st_i[0] % 2]; st_i[0] += 1
                    load(dq, stg[0:120, 0:512], st_pool[120 * half:120 * half + 120, :], tstg)
                    bk, tb = nb_e()
                    for pc in range(4):
                        P.op("pe", C.transpose(out=bk[:, pc * 128:(pc + 1) * 128], in_=stg[:, pc * 128:(pc + 1) * 128], identity=ident[:]), reads=[tstg, t_ident], writes=[tb], sig=(pc == 3))
                    P.op("act", C.activation(out=Es[:, :, 8 * half:8 * half + 8, 0:15], in_=bk[:, :].rearrange("p (a b) -> p a b", a=4)[:, :, 0:120].rearrange("p a (s r) -> p a s r", s=8), func=AF.Copy), reads=[tb], writes=[t_Es])
            yield from linear("u", 8, evac_u, hB, t_hB, True)
            if blk == DBLK: tap('usm', usm[:], t_usm)
            ckpt('u%d' % blk)
            P.op(dq, C.dma_start(out=Ud.rearrange("(a p) j n -> p a (j n)", p=128), in_=usm[:, :, 0:512]), reads=[t_usm, t_U8], writes=[t_Ud], dma=psem_q(dq))
            Udv = Ud.rearrange("(g c) j n -> j c g n", c=16)
            us = psem_q(dq)
            for j in range(8):
                P.op(dq, C.dma_start(out=U8[16 * j:16 * j + 16, :, :], in_=Udv[j, :, :, :]), reads=[t_Ud], writes=[t_U8], dma=us)
            if has_s:
                uds = psem_q(dq)
                for a in range(4):
                    P.op(dq, C.dma_start(out=Uds.rearrange("(a p) j n -> p a j n", p=128)[:, a, 4:8, :], in_=usm[:, a, 512:576].rearrange("p (t s) -> p t s", t=4)), reads=[t_usm, t_U8s], writes=[t_Uds], dma=uds)
                P.op("dve", C.memset(U8s[:].rearrange("p a b -> p (a b)"), 0.0), reads=[], writes=[t_U8s])
                Udsv = Uds.rearrange("(g c) j n -> j c g n", c=16)
                us2 = psem_q(dq)
                for j in range(4, 8):
                    P.op(dq, C.dma_start(out=U8s[16 * j:16 * j + 16, :, :], in_=Udsv[j, :, :, :]), reads=[t_Uds], writes=[t_U8s], dma=us2)
            if blk == DBLK: tap('U8', U8[:], t_U8)
            ckpt('im%d' % blk)

            if blk == DBLK: tap('usm', usm[:], t_usm)
            if blk == 3:
                tm, ttm = PPt, t_PPt
                tm2, ttm2 = tq_e()
                for pc in range(4):
                    bk, tb = nb_e()
                    P.op("pe", C.transpose(out=bk[0:16, 0:128], in_=tm[:, pc * 16:(pc + 1) * 16], identity=ident[:]), reads=[ttm, t_ident], writes=[tb])
                    P.op("dve", C.tensor_copy(out=tm2[0:16, pc * 128:(pc + 1) * 128], in_=bk[0:16, 0:128]), reads=[tb], writes=[ttm2])
                out_events.append(store("pool", p_pool[:, :], tm2[1:16, 0:512], ttm2))
            if has_s:
                for half in range(2):
                    stg, tstg = stage[st_i[0] % 2]; st_i[0] += 1
                    tm, ttm = tq_e()
                    P.op("dve", C.tensor_copy(out=tm[:, 0:480].rearrange("p (a s r) -> p a s r", a=4, s=8), in_=Es[:, :, 8 * half:8 * half + 8, 4:19]), reads=[t_Es], writes=[ttm])
                    for pc in range(4):
                        bk, tb = nb_e()
                        P.op("pe", C.transpose(out=bk[0:120, 0:128], in_=tm[:, pc * 120:(pc + 1) * 120], identity=ident[:]), reads=[ttm, t_ident], writes=[tb])
                        P.op("dve", C.tensor_copy(out=stg[0:120, pc * 128:(pc + 1) * 128], in_=bk[0:120, 0:128]), reads=[tb], writes=[tstg])
                    out_events.append(store("pool", s_pool[120 * half:120 * half + 120, :], stg[0:120, 0:512], tstg))
            if blk == DBLK: tap('pooled', pooled[:], t_pooled)
            ckpt('pool%d' % blk)

            yield "ssm"
            P.op("dve", C.tensor_tensor(out=S1[:], in0=S[:], in1=AR[:], op=ALU.mult), reads=[t_S, t_ssm], writes=[t_ssc])
            P.op("dve", C.tensor_tensor(out=S2[:, 0, :], in0=S[:, 1, :], in1=AIs[:, 0, :], op=ALU.mult), reads=[t_S], writes=[t_ssc])
            P.op("dve", C.tensor_tensor(out=S2[:, 1, :], in0=S[:, 0, :], in1=AIs[:, 1, :], op=ALU.mult), reads=[t_S], writes=[t_ssc])
            P.op("dve", C.tensor_tensor(out=Zm1[:], in0=S1[:], in1=S2[:], op=ALU.add), reads=[t_ssc], writes=[t_Zm1])
            for o in range(4):
                g0 = 8 * o
                zb = ZB[0]
                zT1, tzT1 = tq_e(); zT2, tzT2 = tq_e()
                zb['T1'] = zT1[0:64, 0:512].rearrange("p (a b) -> p a b", a=8); zb['T2'] = zT2[0:64, 0:512].rearrange("p (a b) -> p a b", a=8)
                bka, tba = nb_e(); bkb, tbb = nb_e()
                for gg in range(8):
                    g = g0 + gg
                    P.op("pe", C.matmul(bka[0:64, gg * 64:(gg + 1) * 64], lhsT=Wend[:, g, 0:64], rhs=U8[:, g, :], start=True, stop=True), reads=[t_W, t_U8], writes=[tba], sig=False)
                    P.op("pe", C.matmul(bkb[0:64, gg * 64:(gg + 1) * 64], lhsT=Wend[:, g, 64:128], rhs=U8[:, g, :], start=True, stop=True), reads=[t_W, t_U8], writes=[tbb], sig=(gg == 7))
                XA = bka[0:64, :].rearrange("p (a b) -> p a b", a=8); XB = bkb[0:64, :].rearrange("p (a b) -> p a b", a=8)
                CO = ROT[:, 0, g0:g0 + 8, :]; SI = ROT[:, 1, g0:g0 + 8, :]
                tzall = [zb['tz'][r][gl] for r in range(2) for gl in range(8)]
                yield
                P.op("dve", C.memset(zT1[0:64, 0:1], 0.0), reads=[zb['t']], writes=[tzT1, tzT2, zb['t']])
                P.op("dve", C.tensor_tensor(out=zb['T1'], in0=XA, in1=CO, op=ALU.mult), reads=[tba, t_ssm], writes=[zb['t']])
                P.op("dve", C.tensor_tensor(out=zb['T2'], in0=XB, in1=SI, op=ALU.mult), reads=[tbb], writes=[zb['t']])
                P.op("dve", C.tensor_tensor(out=zb['Zin'][:, 0], in0=zb['T1'], in1=zb['T2'], op=ALU.add), reads=[zb['t']], writes=[zb['t']])
                P.op("dve", C.tensor_tensor(out=zb['T1'], in0=XB, in1=CO, op=ALU.mult), reads=[tbb, zb['t']], writes=[zb['t']])
                P.op("dve", C.tensor_tensor(out=zb['T2'], in0=XA, in1=SI, op=ALU.mult), reads=[tba], writes=[zb['t']])
                P.op("dve", C.tensor_tensor(out=zb['Zin'][:, 1], in0=zb['T1'], in1=zb['T2'], op=ALU.subtract), reads=[zb['t']], writes=[zb['t']])
                for r in range(2):
                    for gl in range(8):
                        g = g0 + gl
                        P.op("dve", C.tensor_tensor_scan(out=zb['Z'][:, r, gl, :], data0=RHO[:, g:g + 1].to_broadcast([64, 64]), data1=zb['Zin'][:, r, gl, :], initial=Zm1[:, r, g:g + 1], op0=ALU.mult, op1=ALU.add),
                             reads=[zb['t'], t_Zm1], writes=[zb['tz'][r][gl]])
                P.op("dve", C.tensor_tensor(out=zb['T1'], in0=zb['Z'][:, 0], in1=CO, op=ALU.mult), reads=tzall + [zb['t']], writes=[zb['t']])
                P.op("dve", C.tensor_tensor(out=zb['T2'], in0=zb['Z'][:, 1], in1=SI, op=ALU.mult), reads=[zb['t']], writes=[zb['t']])
                P.op("dve", C.tensor_tensor(out=zb['Sf'][:, 0], in0=zb['T1'], in1=zb['T2'], op=ALU.subtract), reads=[zb['t']], writes=[zb['t']])
                P.op("dve", C.tensor_tensor(out=zb['T1'], in0=zb['Z'][:, 1], in1=CO, op=ALU.mult), reads=[zb['t']], writes=[zb['t']])
                P.op("dve", C.tensor_tensor(out=zb['T2'], in0=zb['Z'][:, 0], in1=SI, op=ALU.mult), reads=[zb['t']], writes=[zb['t']])
                P.op("dve", C.tensor_tensor(out=zb['Sf'][:, 1], in0=zb['T1'], in1=zb['T2'], op=ALU.add), reads=[zb['t']], writes=[zb['t']] + tzall)
                for r in range(2):
                    P.op("dve", C.tensor_copy(out=Spv[64 * r:64 * r + 64, g0:g0 + 8, 1:64], in_=zb['Sf'][:, r, :, 0:63]), reads=[zb['t']], writes=[t_Spv[o]])
                    P.op("dve", C.tensor_copy(out=Spv[64 * r:64 * r + 64, g0:g0 + 8, 0], in_=S[:, r, g0:g0 + 8]), reads=[t_S], writes=[t_Spv[o]])
                P.op("dve", C.tensor_copy(out=S[:, :, g0:g0 + 8], in_=zb['Sf'][:, :, :, 63]), reads=[zb['t'], t_Zm1], writes=[t_S, zb['t'], tzT1, tzT2])
                yield
            if blk == 3:
                out_events.append(store("pool", p_re.rearrange("(g p) -> p g", p=64), S[:, 0, :], t_S))
                out_events.append(store("pool", p_im.rearrange("(g p) -> p g", p=64), S[:, 1, :], t_S))
            if has_s:
                for g0 in range(0, G, 16):
                    bka, tba = nb_e(); bkb, tbb = nb_e()
                    for gg in range(16):
                        g = g0 + gg
                        P.op("pe", C.matmul(bka[0:64, gg * 16:(gg + 1) * 16], lhsT=Wend[:, g, 0:64], rhs=U8s[:, g, :], start=True, stop=True), reads=[t_W, t_U8s], writes=[tba], sig=False)
                        P.op("pe", C.matmul(bkb[0:64, gg * 16:(gg + 1) * 16], lhsT=Wend[:, g, 64:128], rhs=U8s[:, g, :], start=True, stop=True), reads=[t_W, t_U8s], writes=[tbb], sig=(gg == 15))
                    P.op("act", C.activation(out=Xs[:, 0, g0:g0 + 16, :], in_=bka[0:64, 0:256].rearrange("p (a b) -> p a b", a=16), func=AF.Copy), reads=[tba], writes=[t_Xs])
                    P.op("act", C.activation(out=Xs[:, 1, g0:g0 + 16, :], in_=bkb[0:64, 0:256].rearrange("p (a b) -> p a b", a=16), func=AF.Copy), reads=[tbb], writes=[t_Xs])
                for ri, src in enumerate((st_re, st_im)):
                    srcv = src.rearrange("s (g p) -> (s g) p", p=64)
                    for q in range(4):
                        stg, tstg = stage[st_i[0] % 2]; st_i[0] += 1
                        load(dq, stg[:, 0:64], srcv[128 * q:128 * q + 128, :], tstg)
                        bk, tb = nb_e()
                        P.op("pe", C.transpose(out=bk[0:64, 0:128], in_=stg[:, 0:64], identity=ident[:]), reads=[tstg, t_ident], writes=[tb])
                        P.op("dve", C.tensor_copy(out=H0[:, ri, 4 * q:4 * q + 4, :].rearrange("p s g -> p (s g)"), in_=bk[0:64, 0:128]), reads=[tb], writes=[t_H0])
                def bc_s(ap2): return ap2.unsqueeze(1).to_broadcast([64, 16, G])
                t_hh = ZB[0]['t']
                def cm(outt, pr, pi_):
                    P.op("dve", C.tensor_tensor(out=Ht[:, 0, :, :], in0=H0[:, 0, :, :], in1=bc_s(pr), op=ALU.mult), reads=[t_H0, t_ssm, t_hh], writes=[t_hh])
                    P.op("dve", C.tensor_tensor(out=Ht[:, 1, :, :], in0=H0[:, 1, :, :], in1=bc_s(pi_), op=ALU.mult), reads=[t_H0], writes=[t_hh])
                    P.op("dve", C.tensor_tensor(out=outt[:, 0, :, :], in0=Ht[:, 0, :, :], in1=Ht[:, 1, :, :], op=ALU.subtract), reads=[t_hh], writes=[t_hh])
                    P.op("dve", C.tensor_tensor(out=Ht[:, 0, :, :], in0=H0[:, 0, :, :], in1=bc_s(pi_), op=ALU.mult), reads=[t_H0, t_hh], writes=[t_hh])
                    P.op("dve", C.tensor_tensor(out=Ht[:, 1, :, :], in0=H0[:, 1, :, :], in1=bc_s(pr), op=ALU.mult), reads=[t_H0], writes=[t_hh])
                    P.op("dve", C.tensor_tensor(out=outt[:, 1, :, :], in0=Ht[:, 0, :, :], in1=Ht[:, 1, :, :], op=ALU.add), reads=[t_hh], writes=[t_hh])
                cm(Hu[0:64], NPW[:, 0, :, 4], NPW[:, 1, :, 4])
                for r in range(2):
                    P.op("dve", C.tensor_copy(out=Spvs[64 * r:64 * r + 64, :, :], in_=Hu[0:64, r].rearrange("p s g -> p g s")), reads=[t_hh], writes=[t_Spvs])
                cm(Hu[0:64], PW[:, 0, :, 4], PW[:, 1, :, 4])
                P.op("dve", C.tensor_tensor(out=Hu[0:64], in0=Hu[0:64], in1=Xs[:, :, :, :].rearrange("p r g s -> p r s g"), op=ALU.add), reads=[t_hh, t_Xs], writes=[t_hh])
                for ri, dst in enumerate((s_re, s_im)):
                    dstv = dst.rearrange("s (g p) -> (s g) p", p=64)
                    for q in range(4):
                        stg, tstg = stage[st_i[0] % 2]; st_i[0] += 1
                        bk, tb = nb_e()
                        P.op("pe", C.transpose(out=bk[:, 0:128], in_=Hu[:, ri, 4 * q:4 * q + 4, :].rearrange("p s g -> p (s g)"), identity=ident[:]), reads=[t_hh, t_ident], writes=[tb])
                        P.op("dve", C.tensor_copy(out=stg[:, 0:64], in_=bk[:, 0:64]), reads=[tb], writes=[tstg])
                        out_events.append(store("pool", dstv[128 * q:128 * q + 128, :], stg[:, 0:64], tstg))
            ckpt('rec%d' % blk)
            yield
            def ssm_out(NC, U8_, tU8_, Spv_, tSpv_fn, yg_, tyg_, tYd_):
                for o in range(4):
                    yield
                    g0 = 8 * o
                    bk, tb = nb_e()
                    for gg in range(8):
                        g = g0 + gg
                        P.op("pe", C.matmul(bk[:, gg * NC:(gg + 1) * NC], lhsT=Wst[:, g, :], rhs=Spv_[:, g, :], start=True, stop=False), reads=[t_W, tSpv_fn(o)], writes=[tb], sig=False)
                        P.op("pe", C.matmul(bk[:, gg * NC:(gg + 1) * NC], lhsT=Kloc[:, g, :], rhs=U8_[:, g, :], start=False, stop=True), reads=[t_W, tU8_], writes=[tb], sig=(gg == 7))
                    yield
                    W_ = 8 * NC
                    ta, tta = tq_e(); tb2, ttb2 = tq_e()
                    def v3(ap): return ap[:, 0:W_].rearrange("p (a b) -> p a b", a=8)
                    P.op("act", C.activation(out=ta[:, 0:W_], in_=bk[:, 0:W_], func=AF.Copy), reads=[tb], writes=[tta])
                    P.op("act", C.activation(out=tb2[:, 0:W_], in_=bk[:, 0:W_], func=AF.Square), reads=[tb], writes=[ttb2])
                    P.op("act", C.activation(out=tb2[:, 0:W_], in_=tb2[:, 0:W_], func=AF.Identity, scale=0.044715, bias=onesf[:, 0:1]), reads=[ttb2, t_ones], writes=[ttb2])
                    yield
                    P.op("dve", C.tensor_tensor(out=tb2[:, 0:W_], in0=tb2[:, 0:W_], in1=ta[:, 0:W_], op=ALU.mult), reads=[tta, ttb2], writes=[ttb2])
                    yield
                    P.op("act", C.activation(out=tb2[:, 0:W_], in_=tb2[:, 0:W_], func=AF.Sigmoid, scale=2.0 * math.sqrt(2.0 / math.pi)), reads=[ttb2], writes=[ttb2])
                    yield
                    P.op("dve", C.tensor_tensor(out=yg_[:, g0:g0 + 8, :], in0=v3(ta), in1=v3(tb2), op=ALU.mult), reads=[tta, ttb2, tYd_], writes=[tyg_])
            yield from ssm_out(64, U8, t_U8, Spv, lambda o: t_Spv[o], yg8, t_yg8, t_Yd)
            yield
            Ydv = Yd.rearrange("(g c) i n -> i c g n", c=16)
            ysm = psem_q(dq)
            for i in range(8):
                P.op(dq, C.dma_start(out=Ydv[i, :, :, :], in_=yg8[16 * i:16 * i + 16, :, :]), reads=[t_yg8, t_yfe], writes=[t_Yd], dma=ysm)
            P.op(dq, C.dma_start(out=yfe[:, :, 0:512], in_=Yd.rearrange("(a p) i n -> p a (i n)", p=128)), reads=[t_Yd], writes=[t_yfe], dma=psem_q(dq))
            if has_s:
                yield from ssm_out(16, U8s, t_U8s, Spvs, lambda o: t_Spvs, yg8s, t_yg8s, t_Yds)
                Ydsv = Yds.rearrange("(g c) i n -> i c g n", c=16)
                ysm2 = psem_q(dq)
                for i in range(4, 8):
                    P.op(dq, C.dma_start(out=Ydsv[i, :, :, :], in_=yg8s[16 * i:16 * i + 16, :, :]), reads=[t_yg8s, t_yfe], writes=[t_Yds], dma=ysm2)
                yfs = psem_q(dq)
                for a in range(4):
                    P.op(dq, C.dma_start(out=yfe[:, a, 512:576].rearrange("p (t s) -> p t s", t=4), in_=Yds.rearrange("(a p) i n -> p a i n", p=128)[:, a, 4:8, :]), reads=[t_Yds], writes=[t_yfe], dma=yfs)
            if blk == DBLK: tap('yg8', yg8[:], t_yg8); tap('yfe', yfe[:], t_yfe)
            ckpt('ssm%d' % blk)

            yield
        def run_main(blk, nxt, prev_out):
            dq = "sp" if blk == 0 else "pool"
            def step_next():
                if nxt is not None:
                    next(nxt, None)
                    next(nxt, None)
            has_s = (blk == SBLK)
            parts = [(0, 512, False)] + ([(512, 64, True)] if has_s else [])
            def V(ap, smp):
                return ap.rearrange("p (t s) -> p t s", t=4) if smp else ap
            def MB(tab, idx, smp):
                if smp:
                    return tab[:, idx, 1:17].unsqueeze(1).to_broadcast([128, 4, 16])
                return tab[:, idx, 0:1].to_broadcast([128, 512])
            tiles = [(jj, False) for jj in range(4)] + ([(0, True)] if has_s else [])
            def rmsnorm(Atab, kind_shift, final=False, after=None):
                for (c0, NT, smp) in parts:
                    cs_ = slice(c0, c0 + NT)
                    for fc in range(8):
                        P.op("act", C.activation(out=sq[:, fc, cs_], in_=xT[:, fc, cs_], func=AF.Square), reads=[t_xT[fc]], writes=[t_sq])
                    bk, tb = nb()
                    for fc in range(8):
                        P.op("pe", C.matmul(bk[:, 0:NT], lhsT=onesb[:], rhs=sq[:, fc, cs_], start=(fc == 0), stop=(fc == 7)), reads=[t_sq, t_ones], writes=[tb], sig=(fc == 7))
                    tm, ttm = tq()
                    P.op("act", C.activation(out=tm[:, 0:NT], in_=bk[:, 0:NT], func=AF.Sqrt, scale=1.0 / D, bias=epsc[:, 0:1]), reads=[tb, t_eps], writes=[ttm])
                    P.op("dve", C.reciprocal(out=rstd[:, cs_], in_=tm[:, 0:NT]), reads=[ttm], writes=[t_rstd])
                    for fc in range(8):
                        tm, ttm = tq()
                        P.op("dve", C.tensor_tensor(out=tm[:, 0:NT], in0=xT[:, fc, cs_], in1=rstd[:, cs_], op=ALU.mult), reads=[t_xT[fc], t_rstd], writes=[ttm])
                        if not final:
                            P.op("dve", C.tensor_tensor(out=V(tm[:, 0:NT], smp), in0=V(tm[:, 0:NT], smp), in1=MB(Atab, fc, smp), op=ALU.mult), reads=[t_A, ttm], writes=[ttm])
                            P.op("dve", C.tensor_tensor(out=V(h[:, fc, cs_], smp), in0=V(tm[:, 0:NT], smp), in1=MB(mod, kind_shift * 8 + fc, smp), op=ALU.add), reads=[t_mod, ttm], writes=[t_h])
                        else:
                            P.op("dve", C.tensor_scalar(out=xT[:, fc, cs_], in0=tm[:, 0:NT], scalar1=gf[:, fc:fc + 1], scalar2=None, op0=ALU.mult), reads=[ttm, t_g], writes=[t_xT[fc]])
                        if after is not None:
                            after()

            def linear(kname, K, evac, src, t_src, gen=False):
                for i0 in range(0, 8, 2):
                    sl, tsl = wl((kname, i0))
                    for ci in range(2):
                        for (c0, NT, smp) in parts:
                            bk, tb = nb()
                            for k in range(K):
                                P.op("pe", C.matmul(bk[:, 0:NT], lhsT=sl[:, k, ci * 128:(ci + 1) * 128], rhs=src[:, k, c0:c0 + NT], start=(k == 0), stop=(k == K - 1)),
                                     reads=[tsl, t_src], writes=[tb], sig=(k == K - 1))
                            evac(i0 + ci, bk, tb, c0, NT, smp)
                    if gen:
                        yield

            def linear_now(kname, K, evac, src, t_src, after=None):
                for _ in linear(kname, K, evac, src, t_src, True):
                    if after is not None:
                        after()
            for m in range(8):
                slg, tslg = wl(("g", m))
                sll, tsll = wl(("glu", m))
                for (c0, NT, smp) in parts:
                    cs_ = slice(c0, c0 + NT)
                    bgs, tgs = nb(); bgp, tgp = nb(); bla, tla = nb(); blb, tlb = nb(); bz, tz = nb()
                    for k in range(8):
                        P.op("pe", C.matmul(bgs[:, 0:NT], lhsT=slg[:, k, 0:128], rhs=hB[:, k, cs_], start=(k == 0), stop=(k == 7)), reads=[tslg, t_hB], writes=[tgs], sig=(k == 7))
                    for k in range(8):
                        P.op("pe", C.matmul(bgp[:, 0:NT], lhsT=slg[:, k, 128:256], rhs=hB[:, k, cs_], start=(k == 0), stop=(k == 7)), reads=[tslg, t_hB], writes=[tgp], sig=(k == 7))
                    for k in range(4):
                        P.op("pe", C.matmul(bla[:, 0:NT], lhsT=sll[:, k, 0:128], rhs=yfe[:, k, cs_], start=(k == 0), stop=(k == 3)), reads=[tsll, t_yfe], writes=[tla], sig=(k == 3))
                    for k in range(4):
                        P.op("pe", C.matmul(blb[:, 0:NT], lhsT=sll[:, k, 128:256], rhs=yfe[:, k, cs_], start=(k == 0), stop=(k == 3)), reads=[tsll, t_yfe], writes=[tlb], sig=(k == 3))
                    P.op("pe", C.matmul(bz[:, 0:NT], lhsT=wpl[:, m // 2, (m % 2) * 128:(m % 2) * 128 + 128], rhs=pooled[:, m // 2, cs_], start=True, stop=True), reads=[t_wpl, t_pooled], writes=[tz])
                    t1, tt1 = tq(); t2, tt2 = tq(); t3, tt3 = tq()
                    P.op("act", C.activation(out=t1[:, 0:NT], in_=bgs[:, 0:NT], func=AF.Sigmoid), reads=[tgs], writes=[tt1])
                    P.op("act", C.activation(out=t2[:, 0:NT], in_=bgp[:, 0:NT], func=AF.Sigmoid), reads=[tgp], writes=[tt2])
                    P.op("act", C.activation(out=t3[:, 0:NT], in_=blb[:, 0:NT], func=AF.Sigmoid), reads=[tlb], writes=[tt3])
                    P.op("dve", C.tensor_tensor(out=t3[:, 0:NT], in0=t3[:, 0:NT], in1=bla[:, 0:NT], op=ALU.mult), reads=[tla, tt3], writes=[tt3])
                    P.op("dve", C.tensor_tensor(out=t3[:, 0:NT], in0=t3[:, 0:NT], in1=t1[:, 0:NT], op=ALU.mult), reads=[tt1, tt3], writes=[tt3])
                    P.op("dve", C.scalar_tensor_tensor(out=t2[:, 0:NT], in0=bz[:, 0:NT], scalar=psc[:, m:m + 1], in1=t2[:, 0:NT], op0=ALU.mult, op1=ALU.mult), reads=[tz, tt2, t_g], writes=[tt2])
                    P.op("dve", C.tensor_tensor(out=merged[:, m, cs_], in0=t3[:, 0:NT], in1=t2[:, 0:NT], op=ALU.add), reads=[tt2, tt3], writes=[t_merged])
            if blk == DBLK: tap('merged', merged[:], t_merged)
            ckpt('mrg%d' % blk)

            if prev_out is not None:
                prev_out()
            tiles = [(jj, False) for jj in range(4)] + ([(0, True)] if has_s else [])
            for (jj, smp) in tiles:
                stg, tstg = stage[st_i[0] % 2]; st_i[0] += 1
                rows = 64 if smp else 128
                c0 = 512 if smp else jj * 128
                if smp:
                    for t in range(4):
                        load(dq, stg[16 * t:16 * t + 16, :], xsv[t, :, :], tstg)
                else:
                    for j2 in range(2):
                        load(dq, stg[64 * j2:64 * j2 + 64, :], xpv[2 * jj + j2, 64 * blk:64 * blk + 64, :], tstg)
                for half in range(2):
                    bk, tb = nb()
                    for f4 in range(4):
                        fc = half * 4 + f4
                        P.op("pe", C.transpose(out=bk[:, f4 * 128:(f4 + 1) * 128], in_=stg[:, fc * 128:(fc + 1) * 128], identity=ident[:]),
                             reads=[tstg, t_ident], writes=[tb], sig=(f4 == 3))
                    P.op("act", C.activation(out=xT[:, half * 4:half * 4 + 4, c0:c0 + rows], in_=bk[:, :].rearrange("p (a b) -> p a b", a=4)[:, :, 0:rows], func=AF.Copy),
                         reads=[tb], writes=t_xT[half * 4:half * 4 + 4])
            def evac_res(kind_gate):
                def f(mc, bk, tb, c0, NT, smp):
                    tm, ttm = tq()
                    P.op("dve", C.tensor_tensor(out=V(tm[:, 0:NT], smp), in0=V(bk[:, 0:NT], smp), in1=MB(mod, kind_gate * 8 + mc, smp), op=ALU.mult), reads=[tb, t_mod], writes=[ttm])
                    P.op("dve", C.tensor_tensor(out=xT[:, mc, c0:c0 + NT], in0=xT[:, mc, c0:c0 + NT], in1=tm[:, 0:NT], op=ALU.add), reads=[ttm, t_xT[mc]], writes=[t_xT[mc]])
                return f
            linear_now("out", 8, evac_res(2), merged, t_merged, step_next)
            if blk == DBLK: tap('x1', xT[:], t_xT[7])
            ckpt('wout%d' % blk)

            rmsnorm(A2, 3, after=step_next)

            for fh in range(2):
                for fl in range(11):
                    ff = fh * 11 + fl
                    sl, tsl = wl(("ffi", ff))
                    for (c0, NT, smp) in parts:
                        cs_ = slice(c0, c0 + NT)
                        ba, ta = nb(); bb, tbb_ = nb()
                        for k in range(8):
                            P.op("pe", C.matmul(ba[:, 0:NT], lhsT=sl[:, k, 0:128], rhs=h[:, k, cs_], start=(k == 0), stop=(k == 7)), reads=[tsl, t_h], writes=[ta], sig=(k == 7))
                        for k in range(8):
                            P.op("pe", C.matmul(bb[:, 0:NT], lhsT=sl[:, k, 128:256], rhs=h[:, k, cs_], start=(k == 0), stop=(k == 7)), reads=[tsl, t_h], writes=[tbb_], sig=(k == 7))
                        tm, ttm = tq()
                        P.op("act", C.activation(out=tm[:, 0:NT], in_=ba[:, 0:NT], func=AF.Silu), reads=[ta], writes=[ttm])
                        P.op("dve", C.tensor_tensor(out=actb[:, fl, cs_], in0=tm[:, 0:NT], in1=bb[:, 0:NT], op=ALU.mult), reads=[tbb_, ttm], writes=[t_act])
                    step_next()
                if blk == DBLK and fh == 0: tap('act', actb[:], t_act)
                for m in range(8):
                    sl, tsl = wl(("ffo", fh, m))
                    for (c0, NT, smp) in parts:
                        bk, tb = nb()
                        for k in range(11):
                            P.op("pe", C.matmul(bk[:, 0:NT], lhsT=sl[:, k, :], rhs=actb[:, k, c0:c0 + NT], start=(k == 0), stop=(k == 10)), reads=[tsl, t_act], writes=[tb], sig=(k == 10))
                        evac_res(5)(m, bk, tb, c0, NT, smp)
                    step_next()
            if nxt is not None:
                for _ in nxt:
                    pass
            if blk == DBLK: tap('x2', xT[:], t_xT[7])
            ckpt('ffo%d' % blk)

            rmsnorm(None, 0, final=True)
            ckpt('fn%d' % blk)
            def out_fn():
                for (jj, smp) in tiles:
                    rows = 64 if smp else 128
                    c0 = 512 if smp else jj * 128
                    stg, tstg = stage[st_i[0] % 2]; st_i[0] += 1
                    for half in range(2):
                        bk, tb = nb()
                        for f4 in range(4):
                            fc = half * 4 + f4
                            cw = 64 if smp else 128
                            P.op("pe", C.transpose(out=bk[0:cw, f4 * 128:(f4 + 1) * 128], in_=xT[:, fc, c0:c0 + cw], identity=ident[:]), reads=[t_xT[fc], t_ident], writes=[tb], sig=(f4 == 3))
                        P.op("act", C.activation(out=stg[0:rows, half * 512:(half + 1) * 512], in_=bk[0:rows, :], func=AF.Copy), reads=[tb], writes=[tstg])
                    if smp:
                        for t in range(4):
                            out_events.append(store("pool", ysv[t, :, :], stg[16 * t:16 * t + 16, :], tstg))
                    else:
                        for j2 in range(2):
                            out_events.append(store("pool", ypv[2 * jj + j2, 64 * blk:64 * blk + 64, :], stg[64 * j2:64 * j2 + 64, :], tstg))

            return out_fn
        NBM[0] = 6
        g0 = early(0)
        g0_state = [None]
        def step_g0(n):
            for _ in range(n):
                if g0_state[0] in ("ssm", "end"):
                    return
                g0_state[0] = next(g0, "end")
        for go in range(4):
            gs = slice(8 * go, 8 * go + 8)
            def cmul4(outr, outi, xr, xi, yr, yi, nk):
                def bx(a): return a.unsqueeze(2).to_broadcast([64, 8, nk, 16])
                def by(a): return a.unsqueeze(3).to_broadcast([64, 8, nk, 16])
                dv(C.tensor_tensor(out=v1[:, :, 0:nk, :], in0=bx(xr), in1=by(yr), op=ALU.mult))
                dv(C.tensor_tensor(out=v2[:, :, 0:nk, :], in0=bx(xi), in1=by(yi), op=ALU.mult))
                dv(C.tensor_tensor(out=outr, in0=v1[:, :, 0:nk, :], in1=v2[:, :, 0:nk, :], op=ALU.subtract))
                dv(C.tensor_tensor(out=v1[:, :, 0:nk, :], in0=bx(xr), in1=by(yi), op=ALU.mult))
                dv(C.tensor_tensor(out=v2[:, :, 0:nk, :], in0=bx(xi), in1=by(yr), op=ALU.mult))
                dv(C.tensor_tensor(out=outi, in0=v1[:, :, 0:nk, :], in1=v2[:, :, 0:nk, :], op=ALU.add))
            cmul4(Er[0:64], Ei[0:64], Bbr[:, gs, :], Bbi[:, gs, :], PWrev[:, 0, gs, :], PWrev[:, 1, gs, :], 8)
            cmul4(Rr[0:64], Ri[0:64], Bbr[:, gs, :], Bbi[:, gs, :], NPW[:, 0, gs, :], NPW[:, 1, gs, :], 8)
            cmul4(Qr[0:64], Qi[0:64], Cr[:, gs, :], Ci[:, gs, :], PW[:, 0, gs, :], PW[:, 1, gs, :], 9)
            dv(C.tensor_scalar(out=nQi[0:64].rearrange("p g k c -> p (g k c)"), in0=Qi[0:64].rearrange("p g k c -> p (g k c)"), scalar1=-1.0, scalar2=None, op0=ALU.mult))
            step_g0(10)
            for gl in range(8):
                g = 8 * go + gl
                P.op("dve", C.tensor_copy(out=Wst[0:64, g, :].rearrange("p (k c) -> p k c", k=8), in_=Qr[0:64, gl, 1:9, :]), reads=[t_ssm], writes=[t_W])
                P.op("act", C.activation(out=Wst[64:128, g, :].rearrange("p (k c) -> p k c", k=8), in_=nQi[0:64, gl, 1:9, :], func=AF.Copy), reads=[t_ssm], writes=[t_W])
            for gl in range(8):
                g = 8 * go + gl
                bk, tb = nb()
                P.op("pe", C.transpose(out=bk[:, 0:128], in_=Er[:, gl, :, :].rearrange("p j c -> p (j c)"), identity=ident[:]), reads=[t_ssm, t_ident], writes=[tb], sig=False)
                P.op("pe", C.transpose(out=bk[:, 128:256], in_=Ei[:, gl, :, :].rearrange("p j c -> p (j c)"), identity=ident[:]), reads=[t_ssm, t_ident], writes=[tb], sig=False)
                P.op("pe", C.matmul(bk[:, 256:384], lhsT=Rr[:, gl, :, :].rearrange("p j c -> p (j c)"), rhs=Qr[:, gl, 0:8, :].rearrange("p k c -> p (k c)"), start=True, stop=False), reads=[t_ssm], writes=[tb], sig=False)
                P.op("pe", C.matmul(bk[:, 256:384], lhsT=Ri[:, gl, :, :].rearrange("p j c -> p (j c)"), rhs=nQi[:, gl, 0:8, :].rearrange("p k c -> p (k c)"), start=False, stop=True), reads=[t_ssm], writes=[tb], sig=True)
                P.op("act", C.activation(out=Wend[:, g, :].rearrange("p (a b) -> p a b", a=2), in_=bk[:, 0:256].rearrange("p (a b) -> p a b", a=2)[:, :, 0:64], func=AF.Copy), reads=[tb], writes=[t_W])
                P.op("dve", C.tensor_tensor(out=kt[:], in0=bk[:, 256:384], in1=mask[:], op=ALU.mult), reads=[tb, t_mask], writes=[t_kt])
                P.op("dve", C.scalar_tensor_tensor(out=Kloc[:, g, :], in0=ident[:], scalar=D8[:, g:g + 1], in1=kt[:], op0=ALU.mult, op1=ALU.add), reads=[t_kt, t_ident, t_D8], writes=[t_W])
        dv(C.tensor_tensor(out=tA[:], in0=PW[:, 0, :, 8], in1=PW[:, 0, :, 8], op=ALU.mult))
        dv(C.tensor_tensor(out=tB[:], in0=PW[:, 1, :, 8], in1=PW[:, 1, :, 8], op=ALU.mult))
        dv(C.tensor_tensor(out=tA[:], in0=tA[:], in1=tB[:], op=ALU.add))
        ac(C.activation(out=RHO[:], in_=tA[:], func=AF.Sqrt))
        dv(C.reciprocal(out=tC[:], in_=RHO[:]))
        ur = sbt([64, G], "ur"); ui = sbt([64, G], "ui")
        dv(C.tensor_tensor(out=ur[:], in0=PW[:, 0, :, 8], in1=tC[:], op=ALU.mult))
        dv(C.tensor_tensor(out=ui[:], in0=PW[:, 1, :, 8], in1=tC[:], op=ALU.mult))
        dv(C.memset(ROT[:, 0, :, 0], 1.0)); dv(C.memset(ROT[:, 1, :, 0], 0.0))
        w1 = sbt([64, G, 32], "w1"); w2 = sbt([64, G, 32], "w2")
        for lvl in range(6):
            n0 = 1 << lvl
            def bu(a): return a.unsqueeze(2).to_broadcast([64, G, n0])
            src_r = ROT[:, 0, :, 0:n0]; src_i = ROT[:, 1, :, 0:n0]
            dv(C.tensor_tensor(out=w1[:, :, 0:n0], in0=src_r, in1=bu(ur[:]), op=ALU.mult))
            dv(C.tensor_tensor(out=w2[:, :, 0:n0], in0=src_i, in1=bu(ui[:]), op=ALU.mult))
            dv(C.tensor_tensor(out=ROT[:, 0, :, n0:2 * n0], in0=w1[:, :, 0:n0], in1=w2[:, :, 0:n0], op=ALU.subtract))
            dv(C.tensor_tensor(out=w1[:, :, 0:n0], in0=src_r, in1=bu(ui[:]), op=ALU.mult))
            dv(C.tensor_tensor(out=w2[:, :, 0:n0], in0=src_i, in1=bu(ur[:]), op=ALU.mult))
            dv(C.tensor_tensor(out=ROT[:, 1, :, n0:2 * n0], in0=w1[:, :, 0:n0], in1=w2[:, :, 0:n0], op=ALU.add))
            if lvl < 5:
                dv(C.tensor_tensor(out=tA[:], in0=ur[:], in1=ur[:], op=ALU.mult))
                dv(C.tensor_tensor(out=tB[:], in0=ui[:], in1=ui[:], op=ALU.mult))
                dv(C.tensor_tensor(out=tC[:], in0=ur[:], in1=ui[:], op=ALU.mult))
                dv(C.tensor_tensor(out=ur[:], in0=tA[:], in1=tB[:], op=ALU.subtract))
                dv(C.tensor_scalar(out=ui[:], in0=tC[:], scalar1=2.0, scalar2=None, op0=ALU.mult))
        dv(C.tensor_copy(out=AR[:, 0, :], in_=ROT[:, 0, :, 1])); dv(C.tensor_copy(out=AR[:, 1, :], in_=ROT[:, 0, :, 1]))
        dv(C.tensor_scalar(out=AIs[:, 0, :], in0=ROT[:, 1, :, 1], scalar1=-1.0, scalar2=None, op0=ALU.mult)); dv(C.tensor_copy(out=AIs[:, 1, :], in_=ROT[:, 1, :, 1]))

        adaln_A(A2, g2, 4)
        tap('mod', mod[:], t_mod); tap('A1', A1[:], t_A)
        ckpt('adaln')
        tap('PW', PW[:], t_ssm); tap('NPW', NPW[:], t_ssm); tap('Wend', Wend[:], t_W); tap('Kloc', Kloc[:], t_W); tap('Wst', Wst[:], t_W); tap('AR', AR[:], t_ssm); tap('D8', D8[:], t_D8)
        ckpt('ssmpre')
        step_g0(1000)
        bar = P.op("dve", C.memset(tA[:], 0.0), reads=[], writes=[t_ssm, t_c, t_sg, t_silu, t_W])
        bar2 = P.op("dve", C.memset(tB[:], 0.0), reads=[], writes=[t_ssm])
        P.es2 = None
        es2.close()
        _DEFW[0] = bar2
        S = sbt([64, 2, G], "S"); t_S = T()
        P.op("dve", C.memset(S[:], 0.0), writes=[t_S])
        wpl = P.sb([128, 4, 256], BF16, "wpl"); t_wpl = T()
        wps = P.newsem("wpsem")
        P.op("pool", C.dma_start(out=wpl[:], in_=w_pool.rearrange("g c o -> c g o")), writes=[t_wpl], dma=wps)

        xT = sbt([128, 8, NTT], "xT"); t_xT = [T() for _ in range(8)]
        h = P.sb([128, 8, NTT], BF16, "h"); t_h = T()
        rstd = sbt([128, NTT], "rstd"); t_rstd = T()
        tq_bufs = [(sbt([128, NTT], "tq%d" % i), T()) for i in range(3)]
        tq_i = [0]
        def tq():
            b = tq_bufs[tq_i[0] % len(tq_bufs)]; tq_i[0] += 1
            return b
        Spv = P.sb([128, G, 64], BF16, "Spv"); t_Spv = [T() for _ in range(4)]
        Spvs = P.sb([128, G, 16], BF16, "Spvs"); t_Spvs = T()
        yg8 = P.sb([128, G, 64], BF16, "yg8"); t_yg8 = T()
        yg8s = P.sb([128, G, 16], BF16, "yg8s"); t_yg8s = T()
        yfe = usm; t_yfe = t_usm
        merged = P.sb([128, 8, NTT], BF16, "merged"); t_merged = T()
        sq = merged; t_sq = t_merged
        actb = P.sb([128, 11, NTT], BF16, "actb"); t_act = T()
        ZB = [dict(Zin=sbt([64, 2, 8, 64], "zin%d" % i),
                   Z=sbt([64, 2, 8, 64], "zz%d" % i), Sf=sbt([64, 2, 8, 64], "zsf%d" % i),
                   t=T(), tz=[[T() for _ in range(8)] for _ in range(2)]) for i in range(1)]
        Zm1 = sbt([64, 2, G], "Zm1"); t_Zm1 = T()
        S1 = sbt([64, 2, G], "S1"); S2 = sbt([64, 2, G], "S2"); t_ssc = T()
        Xs = ZB[0]["Zin"][:].rearrange("p a b c -> p (a b c)").rearrange("p (r g s) -> p r g s", r=2, g=G); t_Xs = ZB[0]["t"]
        H0 = ZB[0]["Z"][:].rearrange("p a b c -> p (a b c)").rearrange("p (r s g) -> p r s g", r=2, g=G); t_H0 = ZB[0]["t"]
        Ht = ZB[0]["Sf"][:].rearrange("p a b c -> p (a b c)").rearrange("p (r s g) -> p r s g", r=2, g=G); Hu = sbt([128, 2, 16, G], "Hu"); t_ssc0 = T()
        P.op("dve", C.memset(Hu[:].rearrange("p a b c -> p (a b c)"), 0.0), writes=[t_ssc0])


        for _ in g0:
            pass
        prev_out = None
        for blk in range(4):
            nxt = early(blk + 1) if blk < 3 else None
            prev_out = run_main(blk, nxt, prev_out)
        prev_out()
        fw = {}
        for (s, v) in out_events:
            fw[s] = max(fw.get(s, 0), v)
        P.emit([(s, v) for s, v in fw.items()])
    return nc


_NC = None


def kernel(**inputs):
    global _NC
    f = lambda k: np.ascontiguousarray(inputs[k], dtype=np.float32)
    x_prompt = f("x_prompt"); x_sample = f("x_sample"); c_prompt = f("c_prompt"); c_sample = f("c_sample")
    sre = f("state_ssm_re")[0].reshape(128, 2048); sim = f("state_ssm_im")[0].reshape(128, 2048); spool = f("state_pool")[0].reshape(128, 15 * 512)
    shared = dict(
        norm1_g=f("norm1_g")[0], norm2_g=f("norm2_g")[0], normf_g=f("normf_g"), w_ada=f("w_ada")[0], b_ada=f("b_ada")[0], w_in=f("w_in")[0],
        lam_re=f("ssm_lam_re")[0], lam_im=f("ssm_lam_im")[0], log_dt=f("ssm_log_dt"), b_re=f("ssm_b_re")[0], b_im=f("ssm_b_im")[0],
        c_re=f("ssm_c_re")[0].reshape(512, 64), c_im=f("ssm_c_im")[0].reshape(512, 64), ssm_d=f("ssm_d")[0], w_glu=f("w_glu")[0],
        w_pool=f("w_pool")[0], pool_scale=f("pool_scale")[0], w_out=f("w_out")[0], w_ffn_in=f("w_ffn_in")[0], w_ffn_out=f("w_ffn_out")[0])
    in_maps = []
    for i in range(8):
        m = dict(shared)
        m["xp"] = x_prompt[i]
        m["xs"] = x_sample[16 * i:16 * i + 16].reshape(64, D)
        m["cc"] = np.concatenate([c_prompt[i:i + 1], c_sample[16 * i:16 * i + 16]], 0)
        m["st_re"] = sre[16 * i:16 * i + 16]; m["st_im"] = sim[16 * i:16 * i + 16]
        m["st_pool"] = spool[16 * i:16 * i + 16].reshape(240, 512)
        in_maps.append(m)
    if _NC is None:
        _NC = build_nc()
    res = run_bass_kernel_spmd(_NC, in_maps, core_ids=list(range(8)))
    R = res.results
    y_prompt = np.stack([r["yp"] for r in R], 0)
    y_sample = np.concatenate([r["ys"].reshape(16, 4, D) for r in R], 0)
    p_re = np.stack([r["p_re"].reshape(G, 64) for r in R], 0)[None]
    p_im = np.stack([r["p_im"].reshape(G, 64) for r in R], 0)[None]
    p_pool = np.stack([r["p_pool"] for r in R], 0)[None]
    s_re = np.concatenate([r["s_re"].reshape(16, G, 64) for r in R], 0)[None]
    s_im = np.concatenate([r["s_im"].reshape(16, G, 64) for r in R], 0)[None]
    s_pool = np.concatenate([r["s_pool"].reshape(16, 15, 512) for r in R], 0)[None]
    return (y_prompt.astype(np.float32), y_sample.astype(np.float32), p_re.astype(np.float32), p_im.astype(np.float32),
            p_pool.astype(np.float32), s_re.astype(np.float32), s_im.astype(np.float32), s_pool.astype(np.float32))
```

```python
import math
import numpy as np
from contextlib import ExitStack
import concourse.bass as bass
import concourse.mybir as mybir
from concourse.bass_utils import run_bass_kernel_spmd

F32 = mybir.dt.float32
BF16 = mybir.dt.bfloat16
AF = mybir.ActivationFunctionType
ALU = mybir.AluOpType
D = 1024; DFF = 2816; G = 32; NPST = 64; EPS = 1e-6
SELFSYNC = True
RAW_ONLY = True
DBG = []


class _Rec:
    def __getattr__(self, name):
        return lambda *a, **k: (name, a, k)


C = _Rec()


class Sem:
    def __init__(self, h):
        self.h = h; self.v = 0


_DEFW = [None]


class T:
    def __init__(self, name=""):
        self.w = _DEFW[0]; self.r = []; self.name = name


class Prog:
    EMAP = dict(pe="tensor", act="scalar", dve="vector", pool="gpsimd", sp="sync")

    def __init__(self, nc, es):
        self.nc = nc; self.es = es
        self.q = {e: [] for e in self.EMAP}
        self.cnt = {e: 0 for e in self.EMAP}
        self.sem = {e: Sem(es.enter_context(nc.semaphore("s_" + e))) for e in self.EMAP}
        self.waited = {e: {} for e in self.EMAP}
        self.nsb = 0; self.es2 = None; self.stopped = False

    def sb(self, shape, dt, name=None):
        self.nsb += 1
        es = self.es2 if (self.es2 is not None) else self.es
        return es.enter_context(self.nc.sbuf_tensor(name or ("sb%d" % self.nsb), list(shape), dt))

    def ps(self, shape, dt, name):
        return self.es.enter_context(self.nc.psum_tensor(name, list(shape), dt))

    def newsem(self, name):
        return Sem(self.es.enter_context(self.nc.semaphore(name)))

    def op(self, eng, fn, reads=(), writes=(), dma=None, sig=True):
        if self.stopped:
            return (self.sem['dve'], 0)
        deps = set()
        own = self.sem[eng]
        if dma is not None and getattr(dma, 'guard', 0) > 0:
            deps.add((dma, dma.guard))
        for t in reads:
            if t.w is not None: deps.add(t.w)
        for t in writes:
            if t.w is not None and not (RAW_ONLY and t.w[0] is own and dma is None): deps.add(t.w)
            for ev_ in t.r:
                if not (RAW_ONLY and ev_[0] is own and dma is None): deps.add(ev_)
        if dma is not None:
            dma.v += 16; ev = (dma, dma.v); inc = (dma, 16)
        elif sig:
            self.cnt[eng] += 1; ev = (own, self.cnt[eng]); inc = (own, 1)
        else:
            ev = (own, self.cnt[eng] + 1); inc = None
        waits = []
        for (s, v) in sorted(deps, key=lambda sv: id(sv[0])):
            if s is own and (eng == "pe" or not SELFSYNC or dma is not None and False):
                continue
            if s is own and dma is None and v >= ev[1]:
                continue
            if self.waited[eng].get(s, 0) >= v: continue
            self.waited[eng][s] = v; waits.append((s, v))
        self.q[eng].append((fn, waits, inc))
        for t in reads: t.r.append(ev)
        for t in writes:
            t.w = ev; t.r = []
        return ev

    def emit(self, final_waits):
        with self.nc.Block() as block:
            for eng, ename in self.EMAP.items():
                ops = self.q[eng]
                extra = final_waits if eng == "sp" else []
                if not ops and not extra: continue
                def body(e, ops=ops, extra=extra):
                    for fn, waits, inc in ops:
                        for (s, v) in waits:
                            e.wait_ge(s.h, v)
                        ins = getattr(e, fn[0])(*fn[1], **fn[2])
                        if inc is not None:
                            ins.then_inc(inc[0].h, inc[1])
                    for (s, v) in extra:
                        e.wait_ge(s.h, v)
                getattr(block, ename)(body)


def build_nc(stop_at=None, dbg=()):
    nc = bass.Bass("TRN2", target_bir_lowering=False)
    def din(name, shape): return nc.dram_tensor(name, list(shape), F32, kind="ExternalInput").ap()
    def dout(name, shape): return nc.dram_tensor(name, list(shape), F32, kind="ExternalOutput").ap()
    xp = din("xp", [2048, D]); xs = din("xs", [64, D]); cc = din("cc", [17, D])
    st_re = din("st_re", [16, 2048]); st_im = din("st_im", [16, 2048]); st_pool = din("st_pool", [240, 512])
    norm1_g = din("norm1_g", [D]); norm2_g = din("norm2_g", [D]); normf_g = din("normf_g", [D])
    w_ada = din("w_ada", [D, 6 * D]); b_ada = din("b_ada", [6 * D]); w_in = din("w_in", [D, 3072])
    lam_re = din("lam_re", [G, 64]); lam_im = din("lam_im", [G, 64]); log_dt = din("log_dt", [1, G])
    b_re = din("b_re", [G, 64, 16]); b_im = din("b_im", [G, 64, 16]); c_re = din("c_re", [512, 64]); c_im = din("c_im", [512, 64])
    ssm_d = din("ssm_d", [512]); w_glu = din("w_glu", [512, 2048]); w_pool = din("w_pool", [4, 128, 256])
    pool_scale = din("pool_scale", [D]); w_out = din("w_out", [D, D]); w_ffn_in = din("w_ffn_in", [D, 2 * DFF]); w_ffn_out = din("w_ffn_out", [DFF, D])
    yp = dout("yp", [2048, D]); ys = dout("ys", [64, D]); p_re = dout("p_re", [2048]); p_im = dout("p_im", [2048])
    p_pool = dout("p_pool", [15, 512]); s_re = dout("s_re", [16, 2048]); s_im = dout("s_im", [16, 2048]); s_pool = dout("s_pool", [240, 512])
    Ud = nc.dram_tensor("Ud", [512, 8, 64], BF16, kind="Internal").ap()
    Yd = nc.dram_tensor("Yd", [512, 8, 64], BF16, kind="Internal").ap()
    dbg_outs = {}

    with ExitStack() as es:
        es.enter_context(nc.allow_non_contiguous_dma(reason="small strided parameter loads"))
        P = Prog(nc, es)
        out_events = []
        def ckpt(name):
            if stop_at == name:
                P.stopped = True
        def tap(name, ap, t):
            if name in dbg and not P.stopped:
                d = nc.dram_tensor("dbg_" + name, list(ap.shape), ap.dtype, kind="ExternalOutput").ap()
                out_events.append(store("pool", d, ap, t))
        banks = [(P.ps([128, 512], F32, "bank%d" % i), T("bank%d" % i)) for i in range(8)]
        bank_i = [0]
        NBM = [8]
        def nb():
            b = banks[bank_i[0] % NBM[0]]; bank_i[0] += 1
            return b
        banke_i = [0]
        def nb_e():
            b = banks[6 + banke_i[0] % 2]; banke_i[0] += 1
            return b
        dpool = [P.newsem("dp%d" % i) for i in range(20)]
        lsem = [0]
        def psem():
            lsem[0] += 1
            sm = dpool[lsem[0] % len(dpool)]
            sm.guard = sm.v
            return sm
        dpool2 = [P.newsem("dq%d" % i) for i in range(8)]
        lsem2 = [0]
        def psem2():
            lsem2[0] += 1
            sm = dpool2[lsem2[0] % len(dpool2)]
            sm.guard = sm.v
            return sm
        def psem_q(q):
            return psem2() if q == "sp" else psem()
        def load(eng, out_ap, in_ap, t, reads=()):
            s = psem_q(eng)
            return P.op(eng, C.dma_start(out=out_ap, in_=in_ap), reads=reads, writes=[t], dma=s)
        stsem = [0]
        def store(eng, out_ap, in_ap, t):
            s = psem()
            ev = P.op(eng, C.dma_start(out=out_ap, in_=in_ap), reads=[t], dma=s)
            return ev
        ident = P.sb([128, 128], F32, "ident"); t_ident = T()
        onesf = P.sb([128, 128], F32, "onesf"); t_ones = T()
        onesb = P.sb([128, 128], BF16, "onesb")
        mask = P.sb([128, 128], F32, "mask"); t_mask = T()
        P.op("dve", C.memset(onesf[:], 1.0), writes=[t_ones])
        P.op("dve", C.memset(onesb[:], 1.0), writes=[t_ones])
        P.op("pool", C.affine_select(out=ident[:], in_=onesf[:], pattern=[[-1, 128]], compare_op=ALU.is_equal, fill=0.0, base=0, channel_multiplier=1), reads=[t_ones], writes=[t_ident])
        P.op("pool", C.affine_select(out=mask[:].rearrange("p (i c) -> p i c", i=8), in_=onesf[:].rearrange("p (i c) -> p i c", i=8), pattern=[[16, 8], [0, 16]], compare_op=ALU.is_ge, fill=0.0, base=15, channel_multiplier=-1), reads=[t_ones], writes=[t_mask])

        NSLOT = 3
        wslots = [(P.sb([128, 8, 256], BF16, "wslot%d" % i), T(), P.newsem("wsem%d" % i)) for i in range(NSLOT)]
        ws_i = [0]
        wsems_sw = [P.newsem("wsemsw%d" % i) for i in range(NSLOT)]
        def wload_direct(mk):
            sl, t, s_ = wslots[ws_i[0] % NSLOT]; s_ = wsems_sw[ws_i[0] % NSLOT]; ws_i[0] += 1
            for (o, i_) in mk(sl):
                P.op("pool", C.dma_start(out=o, in_=i_), writes=[t], dma=s_)
            return sl, t
        w2slots = [(P.sb([128, 11, 128], BF16, "w2slot%d" % i), T(), P.newsem("w2sem%d" % i)) for i in range(2)]
        w2_i = [0]
        NSL = 46
        Wd = nc.dram_tensor("Wd", [NSL, 128, 2048], BF16, kind="Internal").ap()
        Wd2 = nc.dram_tensor("Wd2", [16, 128, 1408], BF16, kind="Internal").ap()
        wreg = {}
        def precast(key, mk, kind="w"):
            idx = sum(1 for v_ in wreg.values() if v_[0] == kind)
            tw = T()
            view = Wd[idx].rearrange("p (k c) -> p k c", k=8) if kind == "w" else Wd2[idx].rearrange("p (k c) -> p k c", k=11)
            sm = psem()
            for (o, i_) in mk(view):
                P.op("pool", C.dma_start(out=o, in_=i_), writes=[tw], dma=sm)
            wreg[key] = (kind, idx, tw)
        def wload(key, mk, kind="w"):
            if kind == "w":
                sl, t, s_ = wslots[ws_i[0] % NSLOT]; ws_i[0] += 1
            else:
                sl, t, s_ = w2slots[w2_i[0] % 2]; w2_i[0] += 1
            flat = sl[:].rearrange("p k c -> p (k c)")
            if key not in wreg:
                idx = sum(1 for v_ in wreg.values() if v_[0] == kind)
                tw = T()
                dst = Wd[idx] if kind == "w" else Wd2[idx]
                for (o, i_) in mk(sl):
                    P.op("pool", C.dma_start(out=o, in_=i_), writes=[t], dma=psem())
                P.op("sp", C.dma_start(out=dst, in_=flat), reads=[t], writes=[tw], dma=psem2())
                wreg[key] = (kind, idx, tw)
            else:
                kind_, idx, tw = wreg[key]
                src = Wd[idx] if kind == "w" else Wd2[idx]
                if key[0] == "glu":
                    P.op("sp", C.dma_start(out=flat[:, 0:1024], in_=src[:, 0:1024]), reads=[tw], writes=[t], dma=s_)
                else:
                    P.op("sp", C.dma_start(out=flat, in_=src), reads=[tw], writes=[t], dma=s_)
            return sl, t
        PW = P.sb([64, 2, G, 9], F32, "PW"); NPW = P.sb([64, 2, G, 8], F32, "NPW")
        Wend = P.sb([128, G, 128], BF16, "Wend"); Kloc = P.sb([128, G, 128], BF16, "Kloc")
        Wst = P.sb([128, G, 128], BF16, "Wst")
        AR = P.sb([64, 2, G], F32, "AR"); AIs = P.sb([64, 2, G], F32, "AIs")
        RHO = P.sb([64, G], F32, "RHO"); ROT = P.sb([64, 2, G, 64], F32, "ROT")
        D8 = P.sb([128, G], F32, "D8")
        mod = P.sb([128, 48, 17], F32, "mod")
        A1 = P.sb([128, 8, 17], F32, "A1"); A2 = P.sb([128, 8, 17], F32, "A2")
        siluT = P.sb([128, 8, 17], BF16, "siluT")
        bada = P.sb([128, 48], F32, "bada")
        g1 = P.sb([128, 8], F32, "g1"); g2 = P.sb([128, 8], F32, "g2"); gf = P.sb([128, 8], F32, "gf")
        psc = P.sb([128, 8], F32, "psc"); dsk = P.sb([128, 4], F32, "dsk")
        NTT = 576
        SBLK = 1
        hB = P.sb([128, 8, NTT], BF16, "hB"); t_hB = T()
        tqe_bufs = [(P.sb([128, NTT], F32, "tqe%d" % i), T()) for i in range(2)]
        tqe_i = [0]
        def tq_e():
            b = tqe_bufs[tqe_i[0] % 2]; tqe_i[0] += 1
            return b
        usm = P.sb([128, 4, NTT], BF16, "usm"); t_usm = T()
        pooled = P.sb([128, 4, NTT], BF16, "pooled"); t_pooled = T()
        U8 = P.sb([128, G, 64], BF16, "U8"); t_U8 = T()
        U8s = P.sb([128, G, 16], BF16, "U8s"); t_U8s = T()
        stage = [(P.sb([128, D], F32, "stage%d" % i), T()) for i in range(2)]
        st_i = [0]
        s1 = P.sb([128, 8, 66], F32, "s1"); s2 = P.sb([128, 8, 66], F32, "s2"); t_s = T()
        P.op("dve", C.memset(s1[:].rearrange("p a b -> p (a b)"), 0.0), writes=[t_s])
        P.op("dve", C.memset(s2[:].rearrange("p a b -> p (a b)"), 0.0), writes=[t_s])
        E1 = P.sb([128, 8, 66], F32, "E1"); t_E1 = T()
        Ehist = P.sb([128, 4, 8, 2], F32, "Ehist"); t_Eh = T()
        P.op("dve", C.memset(Ehist[:].rearrange("p a b c -> p (a b c)"), 0.0), writes=[t_Eh])
        PPt = P.sb([128, 64], F32, "PPt"); t_PPt = T()
        ssb = [(P.sb([128, 4], F32, "ssb%d" % i), T()) for i in range(2)]
        ss_i = [0]
        t_Ud = T(); t_Yd = T(); t_Uds = T(); t_Yds = T()
        Uds = nc.dram_tensor("Uds", [512, 8, 16], BF16, kind="Internal").ap()
        Yds = nc.dram_tensor("Yds", [512, 8, 16], BF16, kind="Internal").ap()
        Es = P.sb([128, 4, 16, 19], F32, "Es"); t_Es = T()
        epsc = P.sb([128, 1], F32, "epsc"); t_eps = T()
        P.op("dve", C.memset(epsc[:], EPS), writes=[t_eps])
        xpv = xp.rearrange("(n j) f -> j n f", j=8)
        xsv = xs.rearrange("(s t) f -> t s f", t=4)
        ypv = yp.rearrange("(n j) f -> j n f", j=8)
        ysv = ys.rearrange("(s t) f -> t s f", t=4)
        win_v = w_in.rearrange("(k p) n -> p k n", p=128)
        wglu_v = w_glu.rearrange("(k p) n -> p k n", p=128)
        wout_v = w_out.rearrange("(k p) n -> p k n", p=128)
        wfi_v = w_ffn_in.rearrange("(k p) n -> p k n", p=128)
        wfo_v = w_ffn_out.rearrange("(k p) n -> p k n", p=128)
        es2 = ExitStack(); P.es2 = es2
        t_mod = T(); t_A = T(); t_silu = T(); t_bada = T(); t_g = T(); t_c = T(); t_sg = T()
        csb = P.sb([128, D], F32, "csb")
        P.op("dve", C.memset(csb[:], 0.0), writes=[t_c])
        load("sp", csb[0:17, :], cc[:, :], t_c)
        load("sp", bada[:], b_ada.rearrange("(m p) -> p m", p=128), t_bada)
        load("sp", g1[:], norm1_g.rearrange("(m p) -> p m", p=128), t_g)
        load("sp", g2[:], norm2_g.rearrange("(m p) -> p m", p=128), t_g)
        load("sp", gf[:], normf_g.rearrange("(m p) -> p m", p=128), t_g)
        load("sp", psc[:], pool_scale.rearrange("(m p) -> p m", p=128), t_g)
        sg = P.sb([128, 8, 17], F32, "sg")
        cT = P.sb([128, 8, 17], F32, "cT")
        for half in range(2):
            bk, tb = nb()
            for f4 in range(4):
                fc = half * 4 + f4
                P.op("pe", C.transpose(out=bk[:, f4 * 128:(f4 + 1) * 128], in_=csb[:, fc * 128:(fc + 1) * 128], identity=ident[:]), reads=[t_c, t_ident], writes=[tb], sig=(f4 == 3))
            P.op("act", C.activation(out=sg[:, half * 4:half * 4 + 4, :], in_=bk[:, :].rearrange("p (a b) -> p a b", a=4)[:, :, 0:17], func=AF.Sigmoid), reads=[tb], writes=[t_sg])
            P.op("dve", C.tensor_copy(out=cT[:, half * 4:half * 4 + 4, :], in_=bk[:, :].rearrange("p (a b) -> p a b", a=4)[:, :, 0:17]), reads=[tb], writes=[t_sg])
        P.op("dve", C.tensor_tensor(out=siluT[:], in0=sg[:], in1=cT[:], op=ALU.mult), reads=[t_sg], writes=[t_silu])
        wada_v = w_ada.rearrange("(k p) n -> p k n", p=128)
        def adaln_slices(lo, hi):
            for sl_i in range(lo, hi):
                sl, tsl = wload_direct(lambda sl, sl_i=sl_i: [(sl[:, :, :], wada_v[:, :, sl_i * 256:(sl_i + 1) * 256])])
                bk, tb = nb()
                for sub in range(2):
                    for k in range(8):
                        P.op("pe", C.matmul(bk[:, sub * 17:(sub + 1) * 17], lhsT=sl[:, k, sub * 128:(sub + 1) * 128], rhs=siluT[:, k, :], start=(k == 0), stop=(k == 7)),
                             reads=[tsl, t_silu], writes=[tb], sig=(k == 7))
                for sub in range(2):
                    mc = sl_i * 2 + sub
                    P.op("act", C.activation(out=mod[:, mc, :], in_=bk[:, sub * 17:(sub + 1) * 17], func=AF.Identity, bias=bada[:, mc:mc + 1]), reads=[tb, t_bada], writes=[t_mod])
        def adaln_A(A, g_, kind):
            P.op("dve", C.tensor_scalar(out=A[:], in0=mod[:, kind * 8:(kind + 1) * 8, :], scalar1=1.0, scalar2=None, op0=ALU.add), reads=[t_mod], writes=[t_A])
            P.op("dve", C.tensor_tensor(out=A[:], in0=A[:], in1=g_[:].unsqueeze(2).to_broadcast([128, 8, 17]), op=ALU.mult), reads=[t_g, t_A], writes=[t_A])
        adaln_slices(0, 24)
        t_ssm = T()
        lr = P.sb([64, G], F32, "lr"); li = P.sb([64, G], F32, "li"); dtr = P.sb([128, G], F32, "dtr"); dtb = P.sb([64, G], F32, "dtb")
        load("sp", lr[:], lam_re.rearrange("g p -> p g"), t_ssm)
        load("sp", li[:], lam_im.rearrange("g p -> p g"), t_ssm)
        P.op("dve", C.memset(dtr[:], 0.0), writes=[t_ssm])
        load("sp", dtr[0:1, :], log_dt[:, :], t_ssm)
        load("sp", dsk[:], ssm_d.rearrange("(m p) -> p m", p=128), t_g)
        P.op("act", C.activation(out=dtr[0:1, :], in_=dtr[0:1, :], func=AF.Exp), reads=[t_ssm], writes=[t_ssm])
        bk, tb = nb()
        P.op("pe", C.matmul(bk[0:64, 0:G], lhsT=onesf[:, 0:64], rhs=dtr[:, :], start=True, stop=True), reads=[t_ssm, t_ones], writes=[tb])
        P.op("dve", C.tensor_copy(out=dtb[:], in_=bk[0:64, 0:G]), reads=[tb], writes=[t_ssm])
        def sbt(shape, name): return P.sb(shape, F32, name)
        lrdt = sbt([64, G], "lrdt"); ang = sbt([64, G], "ang"); mag = sbt([64, G], "mag"); cs = sbt([64, G], "cs"); sn = sbt([64, G], "sn")
        tA = sbt([64, G], "tA"); tB = sbt([64, G], "tB"); tC = sbt([64, G], "tC")
        def dv(fn): return P.op("dve", fn, reads=[t_ssm], writes=[t_ssm])
        def ac(fn): return P.op("act", fn, reads=[t_ssm], writes=[t_ssm])
        dv(C.tensor_tensor(out=lrdt[:], in0=lr[:], in1=dtb[:], op=ALU.mult))
        dv(C.tensor_tensor(out=ang[:], in0=li[:], in1=dtb[:], op=ALU.mult))
        ac(C.activation(out=mag[:], in_=lrdt[:], func=AF.Exp))
        halfpi = sbt([64, 1], "halfpi")
        dv(C.memset(halfpi[:], math.pi / 2))
        ac(C.activation(out=sn[:], in_=ang[:], func=AF.Sin, scale=1.0 / 32))
        ac(C.activation(out=cs[:], in_=ang[:], func=AF.Sin, scale=1.0 / 32, bias=halfpi[:, 0:1]))
        for _ in range(5):
            dv(C.tensor_tensor(out=tA[:], in0=cs[:], in1=cs[:], op=ALU.mult))
            dv(C.tensor_tensor(out=tB[:], in0=sn[:], in1=sn[:], op=ALU.mult))
            dv(C.tensor_tensor(out=tC[:], in0=cs[:], in1=sn[:], op=ALU.mult))
            dv(C.tensor_tensor(out=cs[:], in0=tA[:], in1=tB[:], op=ALU.subtract))
            dv(C.tensor_scalar(out=sn[:], in0=tC[:], scalar1=2.0, scalar2=None, op0=ALU.mult))

        dv(C.memset(PW[:, 0, :, 0], 1.0)); dv(C.memset(PW[:, 1, :, 0], 0.0))
        dv(C.memset(NPW[:, 0, :, 0], 1.0)); dv(C.memset(NPW[:, 1, :, 0], 0.0))
        dv(C.tensor_tensor(out=PW[:, 0, :, 1], in0=mag[:], in1=cs[:], op=ALU.mult))
        dv(C.tensor_tensor(out=PW[:, 1, :, 1], in0=mag[:], in1=sn[:], op=ALU.mult))
        m2i = sbt([64, G], "m2i")
        ac(C.activation(out=m2i[:], in_=lrdt[:], func=AF.Exp, scale=-2.0))
        dv(C.tensor_tensor(out=NPW[:, 0, :, 1], in0=PW[:, 0, :, 1], in1=m2i[:], op=ALU.mult))
        dv(C.tensor_tensor(out=tA[:], in0=PW[:, 1, :, 1], in1=m2i[:], op=ALU.mult))
        dv(C.tensor_scalar(out=NPW[:, 1, :, 1], in0=tA[:], scalar1=-1.0, scalar2=None, op0=ALU.mult))
        def cmul_small(outr, outi, ar_, ai_, br_, bi_):
            dv(C.tensor_tensor(out=tA[:], in0=ar_, in1=br_, op=ALU.mult))
            dv(C.tensor_tensor(out=tB[:], in0=ai_, in1=bi_, op=ALU.mult))
            dv(C.tensor_tensor(out=outr, in0=tA[:], in1=tB[:], op=ALU.subtract))
            dv(C.tensor_tensor(out=tA[:], in0=ar_, in1=bi_, op=ALU.mult))
            dv(C.tensor_tensor(out=tB[:], in0=ai_, in1=br_, op=ALU.mult))
            dv(C.tensor_tensor(out=outi, in0=tA[:], in1=tB[:], op=ALU.add))
        for k in range(2, 9):
            cmul_small(PW[:, 0, :, k], PW[:, 1, :, k], PW[:, 0, :, k - 1], PW[:, 1, :, k - 1], PW[:, 0, :, 1], PW[:, 1, :, 1])
        for k in range(2, 8):
            cmul_small(NPW[:, 0, :, k], NPW[:, 1, :, k], NPW[:, 0, :, k - 1], NPW[:, 1, :, k - 1], NPW[:, 0, :, 1], NPW[:, 1, :, 1])
        fr = sbt([64, G], "fr"); fi = sbt([64, G], "fi"); den = sbt([64, G], "den"); nr = sbt([64, G], "nr")
        dv(C.tensor_tensor(out=tA[:], in0=lr[:], in1=lr[:], op=ALU.mult))
        dv(C.tensor_tensor(out=tB[:], in0=li[:], in1=li[:], op=ALU.mult))
        dv(C.tensor_tensor(out=den[:], in0=tA[:], in1=tB[:], op=ALU.add))
        dv(C.reciprocal(out=den[:], in_=den[:]))
        dv(C.tensor_scalar(out=nr[:], in0=PW[:, 0, :, 1], scalar1=-1.0, scalar2=None, op0=ALU.add))
        dv(C.tensor_tensor(out=tA[:], in0=nr[:], in1=lr[:], op=ALU.mult))
        dv(C.tensor_tensor(out=tB[:], in0=PW[:, 1, :, 1], in1=li[:], op=ALU.mult))
        dv(C.tensor_tensor(out=tA[:], in0=tA[:], in1=tB[:], op=ALU.add))
        dv(C.tensor_tensor(out=fr[:], in0=tA[:], in1=den[:], op=ALU.mult))
        dv(C.tensor_tensor(out=tA[:], in0=PW[:, 1, :, 1], in1=lr[:], op=ALU.mult))
        dv(C.tensor_tensor(out=tB[:], in0=nr[:], in1=li[:], op=ALU.mult))
        dv(C.tensor_tensor(out=tA[:], in0=tA[:], in1=tB[:], op=ALU.subtract))
        dv(C.tensor_tensor(out=fi[:], in0=tA[:], in1=den[:], op=ALU.mult))
        Br = sbt([64, G, 16], "Br"); Bi = sbt([64, G, 16], "Bi"); Bbr = sbt([64, G, 16], "Bbr"); Bbi = sbt([64, G, 16], "Bbi")
        u1_ = sbt([64, 8, 16], "u1"); u2_ = sbt([64, 8, 16], "u2"); u1f = sbt([64, G, 16], "u1f"); u2f = sbt([64, G, 16], "u2f")
        load("sp", Br[:], b_re.rearrange("g p c -> p g c"), t_ssm)
        load("sp", Bi[:], b_im.rearrange("g p c -> p g c"), t_ssm)
        def bc_g(ap2):
            return ap2.unsqueeze(2).to_broadcast([64, ap2.shape[1], 16])
        def cmul_gc(outr, outi, xr, xi, yr2, yi2, u1=None, u2=None):
            u1 = u1_ if u1 is None else u1; u2 = u2_ if u2 is None else u2
            dv(C.tensor_tensor(out=u1[:], in0=xr, in1=bc_g(yr2), op=ALU.mult))
            dv(C.tensor_tensor(out=u2[:], in0=xi, in1=bc_g(yi2), op=ALU.mult))
            dv(C.tensor_tensor(out=outr, in0=u1[:], in1=u2[:], op=ALU.subtract))
            dv(C.tensor_tensor(out=u1[:], in0=xr, in1=bc_g(yi2), op=ALU.mult))
            dv(C.tensor_tensor(out=u2[:], in0=xi, in1=bc_g(yr2), op=ALU.mult))
            dv(C.tensor_tensor(out=outi, in0=u1[:], in1=u2[:], op=ALU.add))
        cmul_gc(Bbr[:], Bbi[:], Br[:], Bi[:], fr[:], fi[:], u1f, u2f)
        Cr = sbt([64, G, 16], "Cr"); Ci = sbt([64, G, 16], "Ci")
        cst = sbt([128, 4, 64], "cst"); cst2 = sbt([128, 4, 64], "cst2")
        load("sp", cst[:], c_re.rearrange("(a q) p -> q a p", q=128), t_ssm)
        load("sp", cst2[:], c_im.rearrange("(a q) p -> q a p", q=128), t_ssm)
        win_v0 = w_in.rearrange("(k p) n -> p k n", p=128); wglu_v0 = w_glu.rearrange("(k p) n -> p k n", p=128)
        wout_v0 = w_out.rearrange("(k p) n -> p k n", p=128)
        wfi_v0 = w_ffn_in.rearrange("(k p) n -> p k n", p=128); wfo_v0 = w_ffn_out.rearrange("(k p) n -> p k n", p=128)
        def mk_cols0(wv, K, cols):
            return lambda view: [(view[:, 0:K, ci * 128:(ci + 1) * 128], wv[:, :, c0_:c0_ + 128]) for ci, c0_ in enumerate(cols)]
        for i0 in range(0, 8, 2):
            precast(("u", i0), mk_cols0(win_v0, 8, [128 * i0, 128 * i0 + 128]), "w")
        for m in range(8):
            precast(("g", m), mk_cols0(win_v0, 8, [1024 + 128 * m, 2048 + 128 * m]), "w")
            precast(("glu", m), mk_cols0(wglu_v0, 4, [128 * m, 1024 + 128 * m]), "w")
        for i0 in range(0, 8, 2):
            precast(("out", i0), mk_cols0(wout_v0, 8, [128 * i0, 128 * i0 + 128]), "w")
        for fh in range(2):
            for fl in range(11):
                ff = fh * 11 + fl
                precast(("ffi", ff), mk_cols0(wfi_v0, 8, [128 * ff, DFF + 128 * ff]), "w")
            for m in range(8):
                precast(("ffo", fh, m), (lambda view, fh=fh, m=m: [(view[:, :, :], wfo_v0[:, fh * 11:(fh + 1) * 11, 128 * m:128 * m + 128])]), "w2")
        for (src, dst) in ((cst, Cr), (cst2, Ci)):
            bk, tb = nb()
            for a in range(4):
                P.op("pe", C.transpose(out=bk[0:64, a * 128:(a + 1) * 128], in_=src[:, a, :], identity=ident[:]), reads=[t_ssm, t_ident], writes=[tb], sig=(a == 3))
            P.op("dve", C.tensor_copy(out=dst[:].rearrange("p g c -> p (g c)"), in_=bk[0:64, :]), reads=[tb], writes=[t_ssm])
        t_W = T()
        Er = sbt([128, 8, 8, 16], "Er"); Ei = sbt([128, 8, 8, 16], "Ei")
        Rr = sbt([128, 8, 8, 16], "Rr"); Ri = sbt([128, 8, 8, 16], "Ri")
        Qr = sbt([128, 8, 9, 16], "Qr"); Qi = sbt([128, 8, 9, 16], "Qi"); nQi = sbt([128, 8, 9, 16], "nQi")
        t_D8 = T()
        for i in range(8):
            load("sp", D8[16 * i:16 * i + 16, :], ssm_d.rearrange("(g c) -> c g", c=16), t_D8)
        kt = sbt([128, 128], "kt"); t_kt = T()
        v1 = sbt([64, 8, 9, 16], "v1"); v2 = sbt([64, 8, 9, 16], "v2")
        PWrev = sbt([64, 2, G, 8], "PWrev")
        for j in range(8):
            dv(C.tensor_copy(out=PWrev[:, :, :, j], in_=PW[:, :, :, 7 - j]))
        for tt in (Er, Ei, Rr, Ri, Qr, Qi, nQi):
            dv(C.memset(tt[:].rearrange("p a b c -> p (a b c)"), 0.0))
        adaln_A(A1, g1, 1)
        DBLK = int(dbg[0][1:]) if (dbg and dbg[0].startswith('@')) else -1
        def mk_cols(wv, K, cols):
            return lambda view: [(view[:, 0:K, ci * 128:(ci + 1) * 128], wv[:, :, c0_:c0_ + 128]) for ci, c0_ in enumerate(cols)]
        specs = []
        for i0 in range(0, 8, 2):
            specs.append((("u", i0), mk_cols(win_v, 8, [128 * i0, 128 * i0 + 128]), "w"))
        for m in range(8):
            specs.append((("g", m), mk_cols(win_v, 8, [1024 + 128 * m, 2048 + 128 * m]), "w"))
            specs.append((("glu", m), mk_cols(wglu_v, 4, [128 * m, 1024 + 128 * m]), "w"))
        for i0 in range(0, 8, 2):
            specs.append((("out", i0), mk_cols(wout_v, 8, [128 * i0, 128 * i0 + 128]), "w"))
        for fh in range(2):
            for fl in range(11):
                ff = fh * 11 + fl
                specs.append((("ffi", ff), mk_cols(wfi_v, 8, [128 * ff, DFF + 128 * ff]), "w"))
            for m in range(8):
                specs.append((("ffo", fh, m), (lambda view, fh=fh, m=m: [(view[:, :, :], wfo_v[:, fh * 11:(fh + 1) * 11, 128 * m:128 * m + 128])]), "w2"))
        spec = {key: (mk, kind) for (key, mk, kind) in specs}
        def wl(key):
            mk, kind = spec[key]
            return wload(key, mk, kind)
        def early(blk):
            dq = "sp" if blk == 0 else "pool"
            has_s = (blk == SBLK)
            parts = [(0, 512, False)] + ([(512, 64, True)] if has_s else [])
            def V(ap, smp):
                return ap.rearrange("p (t s) -> p t s", t=4) if smp else ap
            def MB(tab, idx, smp):
                if smp:
                    return tab[:, idx, 1:17].unsqueeze(1).to_broadcast([128, 4, 16])
                return tab[:, idx, 0:1].to_broadcast([128, 512])
            tiles = [(jj, False) for jj in range(4)] + ([(0, True)] if has_s else [])
            def linear(kname, K, evac, src, t_src, gen=False):
                for i0 in range(0, 8, 2):
                    sl, tsl = wl((kname, i0))
                    for ci in range(2):
                        for (c0, NT, smp) in parts:
                            bk, tb = nb_e()
                            for k in range(K):
                                P.op("pe", C.matmul(bk[:, 0:NT], lhsT=sl[:, k, ci * 128:(ci + 1) * 128], rhs=src[:, k, c0:c0 + NT], start=(k == 0), stop=(k == K - 1)),
                                     reads=[tsl, t_src], writes=[tb], sig=(k == K - 1))
                            evac(i0 + ci, bk, tb, c0, NT, smp)
                    if gen:
                        yield

            def linear_now(kname, K, evac, src, t_src):
                for _ in linear(kname, K, evac, src, t_src, False):
                    pass
            for (jj, smp) in tiles:
                stg, tstg = stage[st_i[0] % 2]; st_i[0] += 1
                rows = 64 if smp else 128
                c0 = 512 if smp else jj * 128
                if smp:
                    for t in range(4):
                        load(dq, stg[16 * t:16 * t + 16, :], xsv[t, :, :], tstg)
                else:
                    for j2 in range(2):
                        load(dq, stg[64 * j2:64 * j2 + 64, :], xpv[2 * jj + j2, 64 * blk:64 * blk + 64, :], tstg)
                yield
                yield
                sb_, tsb = ssb[ss_i[0] % 2]; ss_i[0] += 1
                junk = usm[:].rearrange("p a b -> p (a b)")[:, 0:1024]
                P.op("act", C.activation(out=junk, in_=stg[:, :], func=AF.Square, accum_out=sb_[:, 0:1]), reads=[tstg], writes=[t_usm, tsb])
                P.op("act", C.activation(out=sb_[:, 1:2], in_=sb_[:, 0:1], func=AF.Sqrt, scale=1.0 / D, bias=epsc[:, 0:1]), reads=[tsb, t_eps], writes=[tsb])
                P.op("dve", C.reciprocal(out=sb_[:, 2:3], in_=sb_[:, 1:2]), reads=[tsb], writes=[tsb])
                P.op("act", C.activation(out=stg[:, :], in_=stg[:, :], func=AF.Identity, scale=sb_[:, 2:3]), reads=[tsb, tstg], writes=[tstg])
                yield
                bks = []
                for half in range(2):
                    bk, tb = nb_e()
                    bks.append((bk, tb))
                    for f4 in range(4):
                        fc = half * 4 + f4
                        P.op("pe", C.transpose(out=bk[:, f4 * 128:(f4 + 1) * 128], in_=stg[:, fc * 128:(fc + 1) * 128], identity=ident[:]),
                             reads=[tstg, t_ident], writes=[tb], sig=(f4 == 3))
                yield
                for half in range(2):
                    bk, tb = bks[half]
                    for f4 in range(4):
                        fc = half * 4 + f4
                        if not smp:
                            P.op("act", C.activation(out=hB[:, fc, c0:c0 + 128], in_=bk[:, f4 * 128:(f4 + 1) * 128], func=AF.Identity, scale=A1[:, fc, 0:1], bias=mod[:, fc, 0:1]),
                                 reads=[tb, t_A, t_mod], writes=[t_hB])
                        else:
                            tm, ttm = tq_e()
                            P.op("dve", C.tensor_tensor(out=V(tm[:, 0:64], True), in0=V(bk[:, f4 * 128:f4 * 128 + 64], True), in1=MB(A1, fc, True), op=ALU.mult), reads=[tb, t_A], writes=[ttm])
                            P.op("dve", C.tensor_tensor(out=V(hB[:, fc, 512:576], True), in0=V(tm[:, 0:64], True), in1=MB(mod, fc, True), op=ALU.add), reads=[ttm, t_mod], writes=[t_hB])
                yield
            if blk == DBLK: tap('h1', hB[:], t_hB)
            def pool_pc(pc):
                lv = pc + 1
                w = float(1 << lv)
                if has_s:
                    cur = Es[:, pc, :, :]
                    bufs = [s1[:].rearrange("p a b -> p (a b)")[:, 0:304].rearrange("p (s r) -> p s r", r=19), s2[:].rearrange("p a b -> p (a b)")[:, 0:304].rearrange("p (s r) -> p s r", r=19)]
                    for l in range(lv):
                        d = 1 << l
                        dst = bufs[l % 2]
                        P.op("dve", C.tensor_tensor(out=dst[:, :, d:19], in0=cur[:, :, d:19], in1=cur[:, :, 0:19 - d], op=ALU.add), reads=[t_Es, t_s], writes=[t_s])
                        cur = dst
                    P.op("dve", C.scalar_tensor_tensor(out=pooled[:, pc, 512:576].rearrange("p (t s) -> p s t", t=4), in0=cur[:, :, 15:19], scalar=1.0 / w, in1=Es[:, pc, :, 15:19], op0=ALU.mult, op1=ALU.subtract), reads=[t_s, t_Es], writes=[t_pooled])
                cur = E1[:, :, :]
                bufs = [s1, s2]
                for l in range(lv):
                    d = 1 << l
                    dst = bufs[l % 2]
                    if d < 8:
                        P.op("dve", C.tensor_tensor(out=dst[:, d:8, :], in0=cur[:, d:8, :], in1=cur[:, 0:8 - d, :], op=ALU.add), reads=[t_E1, t_s], writes=[t_s])
                        P.op("dve", C.tensor_tensor(out=dst[:, 0:d, 1:66], in0=cur[:, 0:d, 1:66], in1=cur[:, 8 - d:8, 0:65], op=ALU.add), reads=[t_E1, t_s], writes=[t_s])
                    else:
                        P.op("dve", C.tensor_tensor(out=dst[:, :, 1:66], in0=cur[:, :, 1:66], in1=cur[:, :, 0:65], op=ALU.add), reads=[t_E1, t_s], writes=[t_s])
                    cur = dst
                P.op("dve", C.scalar_tensor_tensor(out=pooled[:, pc, 0:512].rearrange("p (j n) -> p j n", j=8), in0=cur[:, :, 2:66], scalar=1.0 / w, in1=E1[:, :, 2:66], op0=ALU.mult, op1=ALU.subtract), reads=[t_s, t_E1], writes=[t_pooled])
                if blk == 0:
                    for t in range(int(w) - 1):
                        j, n = t % 8, t // 8
                        P.op("dve", C.scalar_tensor_tensor(out=pooled[:, pc, j * 64 + n:j * 64 + n + 1], in0=cur[:, j, 2 + n:3 + n], scalar=1.0 / (t + 1), in1=E1[:, j, 2 + n:3 + n], op0=ALU.mult, op1=ALU.subtract), reads=[t_s, t_E1], writes=[t_pooled])
                if blk == 3:
                    P.op("dve", C.tensor_copy(out=PPt[:, pc * 16:(pc + 1) * 16].rearrange("p (n j) -> p n j", n=2), in_=E1[:, :, 64:66].rearrange("p j n -> p n j")), reads=[t_E1], writes=[t_PPt])
                P.op("dve", C.tensor_copy(out=Ehist[:, pc, :, :], in_=E1[:, :, 64:66]), reads=[t_E1, t_pooled], writes=[t_Eh])
            def evac_u(mc, bk, tb, c0, NT, smp):
                if mc < 4:
                    P.op("act", C.activation(out=usm[:, mc, c0:c0 + NT], in_=bk[:, 0:NT], func=AF.Copy), reads=[tb], writes=[t_usm])
                else:
                    pc = mc - 4
                    if smp:
                        P.op("act", C.activation(out=Es[:, pc, :, 15:19], in_=bk[:, 0:64].rearrange("p (t s) -> p s t", t=4), func=AF.Copy), reads=[tb], writes=[t_Es])
                    else:
                        P.op("act", C.activation(out=E1[:, :, 2:66], in_=bk[:, 0:512].rearrange("p (j n) -> p j n", j=8), func=AF.Copy), reads=[tb], writes=[t_E1])
                        P.op("dve", C.tensor_copy(out=E1[:, :, 0:2], in_=Ehist[:, pc, :, :]), reads=[t_Eh], writes=[t_E1])
                    if smp or not has_s:
                        pool_pc(pc)
            if has_s:
                for half in range(2):
                    stg, tstg = stage[st_i[0] % 2]; st_i[0] += 1
                    load(dq, stg[0:120, 0:512], st_pool[120 * half:120 * half + 120, :], tstg)
                    bk, tb = nb_e()
                    for pc in range(4):
                        P.op("pe", C.transpose(out=bk[:, pc * 128:(pc + 1) * 128], in_=stg[:, pc * 128:(pc + 1) * 128], identity=ident[:]), reads=[tstg, t_ident], writes=[tb], sig=(pc == 3))
                    P.op("act", C.activation(out=Es[:, :, 8 * half:8 * half + 8, 0:15], in_=bk[:, :].rearrange("p (a b) -> p a b", a=4)[:, :, 0:120].rearrange("p a (s r) -> p a s r", s=8), func=AF.Copy), reads=[tb], writes=[t_Es])
            yield from linear("u", 8, evac_u, hB, t_hB, True)
            if blk == DBLK: tap('usm', usm[:], t_usm)
            ckpt('u%d' % blk)
            P.op(dq, C.dma_start(out=Ud.rearrange("(a p) j n -> p a (j n)", p=128), in_=usm[:, :, 0:512]), reads=[t_usm, t_U8], writes=[t_Ud], dma=psem_q(dq))
            Udv = Ud.rearrange("(g c) j n -> j c g n", c=16)
            us = psem_q(dq)
            for j in range(8):
                P.op(dq, C.dma_start(out=U8[16 * j:16 * j + 16, :, :], in_=Udv[j, :, :, :]), reads=[t_Ud], writes=[t_U8], dma=us)
            if has_s:
                uds = psem_q(dq)
                for a in range(4):
                    P.op(dq, C.dma_start(out=Uds.rearrange("(a p) j n -> p a j n", p=128)[:, a, 4:8, :], in_=usm[:, a, 512:576].rearrange("p (t s) -> p t s", t=4)), reads=[t_usm, t_U8s], writes=[t_Uds], dma=uds)
                P.op("dve", C.memset(U8s[:].rearrange("p a b -> p (a b)"), 0.0), reads=[], writes=[t_U8s])
                Udsv = Uds.rearrange("(g c) j n -> j c g n", c=16)
                us2 = psem_q(dq)
                for j in range(4, 8):
                    P.op(dq, C.dma_start(out=U8s[16 * j:16 * j + 16, :, :], in_=Udsv[j, :, :, :]), reads=[t_Uds], writes=[t_U8s], dma=us2)
            if blk == DBLK: tap('U8', U8[:], t_U8)
            ckpt('im%d' % blk)

            if blk == DBLK: tap('usm', usm[:], t_usm)
            if blk == 3:
                tm, ttm = PPt, t_PPt
                tm2, ttm2 = tq_e()
                for pc in range(4):
                    bk, tb = nb_e()
                    P.op("pe", C.transpose(out=bk[0:16, 0:128], in_=tm[:, pc * 16:(pc + 1) * 16], identity=ident[:]), reads=[ttm, t_ident], writes=[tb])
                    P.op("dve", C.tensor_copy(out=tm2[0:16, pc * 128:(pc + 1) * 128], in_=bk[0:16, 0:128]), reads=[tb], writes=[ttm2])
                out_events.append(store("pool", p_pool[:, :], tm2[1:16, 0:512], ttm2))
            if has_s:
                for half in range(2):
                    stg, tstg = stage[st_i[0] % 2]; st_i[0] += 1
                    tm, ttm = tq_e()
                    P.op("dve", C.tensor_copy(out=tm[:, 0:480].rearrange("p (a s r) -> p a s r", a=4, s=8), in_=Es[:, :, 8 * half:8 * half + 8, 4:19]), reads=[t_Es], writes=[ttm])
                    for pc in range(4):
                        bk, tb = nb_e()
                        P.op("pe", C.transpose(out=bk[0:120, 0:128], in_=tm[:, pc * 120:(pc + 1) * 120], identity=ident[:]), reads=[ttm, t_ident], writes=[tb])
                        P.op("dve", C.tensor_copy(out=stg[0:120, pc * 128:(pc + 1) * 128], in_=bk[0:120, 0:128]), reads=[tb], writes=[tstg])
                    out_events.append(store("pool", s_pool[120 * half:120 * half + 120, :], stg[0:120, 0:512], tstg))
            if blk == DBLK: tap('pooled', pooled[:], t_pooled)
            ckpt('pool%d' % blk)

            yield "ssm"
            P.op("dve", C.tensor_tensor(out=S1[:], in0=S[:], in1=AR[:], op=ALU.mult), reads=[t_S, t_ssm], writes=[t_ssc])
            P.op("dve", C.tensor_tensor(out=S2[:, 0, :], in0=S[:, 1, :], in1=AIs[:, 0, :], op=ALU.mult), reads=[t_S], writes=[t_ssc])
            P.op("dve", C.tensor_tensor(out=S2[:, 1, :], in0=S[:, 0, :], in1=AIs[:, 1, :], op=ALU.mult), reads=[t_S], writes=[t_ssc])
            P.op("dve", C.tensor_tensor(out=Zm1[:], in0=S1[:], in1=S2[:], op=ALU.add), reads=[t_ssc], writes=[t_Zm1])
            for o in range(4):
                g0 = 8 * o
                zb = ZB[0]
                zT1, tzT1 = tq_e(); zT2, tzT2 = tq_e()
                zb['T1'] = zT1[0:64, 0:512].rearrange("p (a b) -> p a b", a=8); zb['T2'] = zT2[0:64, 0:512].rearrange("p (a b) -> p a b", a=8)
                bka, tba = nb_e(); bkb, tbb = nb_e()
                for gg in range(8):
                    g = g0 + gg
                    P.op("pe", C.matmul(bka[0:64, gg * 64:(gg + 1) * 64], lhsT=Wend[:, g, 0:64], rhs=U8[:, g, :], start=True, stop=True), reads=[t_W, t_U8], writes=[tba], sig=False)
                    P.op("pe", C.matmul(bkb[0:64, gg * 64:(gg + 1) * 64], lhsT=Wend[:, g, 64:128], rhs=U8[:, g, :], start=True, stop=True), reads=[t_W, t_U8], writes=[tbb], sig=(gg == 7))
                XA = bka[0:64, :].rearrange("p (a b) -> p a b", a=8); XB = bkb[0:64, :].rearrange("p (a b) -> p a b", a=8)
                CO = ROT[:, 0, g0:g0 + 8, :]; SI = ROT[:, 1, g0:g0 + 8, :]
                tzall = [zb['tz'][r][gl] for r in range(2) for gl in range(8)]
                yield
                P.op("dve", C.memset(zT1[0:64, 0:1], 0.0), reads=[zb['t']], writes=[tzT1, tzT2, zb['t']])
                P.op("dve", C.tensor_tensor(out=zb['T1'], in0=XA, in1=CO, op=ALU.mult), reads=[tba, t_ssm], writes=[zb['t']])
                P.op("dve", C.tensor_tensor(out=zb['T2'], in0=XB, in1=SI, op=ALU.mult), reads=[tbb], writes=[zb['t']])
                P.op("dve", C.tensor_tensor(out=zb['Zin'][:, 0], in0=zb['T1'], in1=zb['T2'], op=ALU.add), reads=[zb['t']], writes=[zb['t']])
                P.op("dve", C.tensor_tensor(out=zb['T1'], in0=XB, in1=CO, op=ALU.mult), reads=[tbb, zb['t']], writes=[zb['t']])
                P.op("dve", C.tensor_tensor(out=zb['T2'], in0=XA, in1=SI, op=ALU.mult), reads=[tba], writes=[zb['t']])
                P.op("dve", C.tensor_tensor(out=zb['Zin'][:, 1], in0=zb['T1'], in1=zb['T2'], op=ALU.subtract), reads=[zb['t']], writes=[zb['t']])
                for r in range(2):
                    for gl in range(8):
                        g = g0 + gl
                        P.op("dve", C.tensor_tensor_scan(out=zb['Z'][:, r, gl, :], data0=RHO[:, g:g + 1].to_broadcast([64, 64]), data1=zb['Zin'][:, r, gl, :], initial=Zm1[:, r, g:g + 1], op0=ALU.mult, op1=ALU.add),
                             reads=[zb['t'], t_Zm1], writes=[zb['tz'][r][gl]])
                P.op("dve", C.tensor_tensor(out=zb['T1'], in0=zb['Z'][:, 0], in1=CO, op=ALU.mult), reads=tzall + [zb['t']], writes=[zb['t']])
                P.op("dve", C.tensor_tensor(out=zb['T2'], in0=zb['Z'][:, 1], in1=SI, op=ALU.mult), reads=[zb['t']], writes=[zb['t']])
                P.op("dve", C.tensor_tensor(out=zb['Sf'][:, 0], in0=zb['T1'], in1=zb['T2'], op=ALU.subtract), reads=[zb['t']], writes=[zb['t']])
                P.op("dve", C.tensor_tensor(out=zb['T1'], in0=zb['Z'][:, 1], in1=CO, op=ALU.mult), reads=[zb['t']], writes=[zb['t']])
                P.op("dve", C.tensor_tensor(out=zb['T2'], in0=zb['Z'][:, 0], in1=SI, op=ALU.mult), reads=[zb['t']], writes=[zb['t']])
                P.op("dve", C.tensor_tensor(out=zb['Sf'][:, 1], in0=zb['T1'], in1=zb['T2'], op=ALU.add), reads=[zb['t']], writes=[zb['t']] + tzall)
                for r in range(2):
                    P.op("dve", C.tensor_copy(out=Spv[64 * r:64 * r + 64, g0:g0 + 8, 1:64], in_=zb['Sf'][:, r, :, 0:63]), reads=[zb['t']], writes=[t_Spv[o]])
                    P.op("dve", C.tensor_copy(out=Spv[64 * r:64 * r + 64, g0:g0 + 8, 0], in_=S[:, r, g0:g0 + 8]), reads=[t_S], writes=[t_Spv[o]])
                P.op("dve", C.tensor_copy(out=S[:, :, g0:g0 + 8], in_=zb['Sf'][:, :, :, 63]), reads=[zb['t'], t_Zm1], writes=[t_S, zb['t'], tzT1, tzT2])
                yield
            if blk == 3:
                out_events.append(store("pool", p_re.rearrange("(g p) -> p g", p=64), S[:, 0, :], t_S))
                out_events.append(store("pool", p_im.rearrange("(g p) -> p g", p=64), S[:, 1, :], t_S))
            if has_s:
                for g0 in range(0, G, 16):
                    bka, tba = nb_e(); bkb, tbb = nb_e()
                    for gg in range(16):
                        g = g0 + gg
                        P.op("pe", C.matmul(bka[0:64, gg * 16:(gg + 1) * 16], lhsT=Wend[:, g, 0:64], rhs=U8s[:, g, :], start=True, stop=True), reads=[t_W, t_U8s], writes=[tba], sig=False)
                        P.op("pe", C.matmul(bkb[0:64, gg * 16:(gg + 1) * 16], lhsT=Wend[:, g, 64:128], rhs=U8s[:, g, :], start=True, stop=True), reads=[t_W, t_U8s], writes=[tbb], sig=(gg == 15))
                    P.op("act", C.activation(out=Xs[:, 0, g0:g0 + 16, :], in_=bka[0:64, 0:256].rearrange("p (a b) -> p a b", a=16), func=AF.Copy), reads=[tba], writes=[t_Xs])
                    P.op("act", C.activation(out=Xs[:, 1, g0:g0 + 16, :], in_=bkb[0:64, 0:256].rearrange("p (a b) -> p a b", a=16), func=AF.Copy), reads=[tbb], writes=[t_Xs])
                for ri, src in enumerate((st_re, st_im)):
                    srcv = src.rearrange("s (g p) -> (s g) p", p=64)
                    for q in range(4):
                        stg, tstg = stage[st_i[0] % 2]; st_i[0] += 1
                        load(dq, stg[:, 0:64], srcv[128 * q:128 * q + 128, :], tstg)
                        bk, tb = nb_e()
                        P.op("pe", C.transpose(out=bk[0:64, 0:128], in_=stg[:, 0:64], identity=ident[:]), reads=[tstg, t_ident], writes=[tb])
                        P.op("dve", C.tensor_copy(out=H0[:, ri, 4 * q:4 * q + 4, :].rearrange("p s g -> p (s g)"), in_=bk[0:64, 0:128]), reads=[tb], writes=[t_H0])
                def bc_s(ap2): return ap2.unsqueeze(1).to_broadcast([64, 16, G])
                t_hh = ZB[0]['t']
                def cm(outt, pr, pi_):
                    P.op("dve", C.tensor_tensor(out=Ht[:, 0, :, :], in0=H0[:, 0, :, :], in1=bc_s(pr), op=ALU.mult), reads=[t_H0, t_ssm, t_hh], writes=[t_hh])
                    P.op("dve", C.tensor_tensor(out=Ht[:, 1, :, :], in0=H0[:, 1, :, :], in1=bc_s(pi_), op=ALU.mult), reads=[t_H0], writes=[t_hh])
                    P.op("dve", C.tensor_tensor(out=outt[:, 0, :, :], in0=Ht[:, 0, :, :], in1=Ht[:, 1, :, :], op=ALU.subtract), reads=[t_hh], writes=[t_hh])
                    P.op("dve", C.tensor_tensor(out=Ht[:, 0, :, :], in0=H0[:, 0, :, :], in1=bc_s(pi_), op=ALU.mult), reads=[t_H0, t_hh], writes=[t_hh])
                    P.op("dve", C.tensor_tensor(out=Ht[:, 1, :, :], in0=H0[:, 1, :, :], in1=bc_s(pr), op=ALU.mult), reads=[t_H0], writes=[t_hh])
                    P.op("dve", C.tensor_tensor(out=outt[:, 1, :, :], in0=Ht[:, 0, :, :], in1=Ht[:, 1, :, :], op=ALU.add), reads=[t_hh], writes=[t_hh])
                cm(Hu[0:64], NPW[:, 0, :, 4], NPW[:, 1, :, 4])
                for r in range(2):
                    P.op("dve", C.tensor_copy(out=Spvs[64 * r:64 * r + 64, :, :], in_=Hu[0:64, r].rearrange("p s g -> p g s")), reads=[t_hh], writes=[t_Spvs])
                cm(Hu[0:64], PW[:, 0, :, 4], PW[:, 1, :, 4])
                P.op("dve", C.tensor_tensor(out=Hu[0:64], in0=Hu[0:64], in1=Xs[:, :, :, :].rearrange("p r g s -> p r s g"), op=ALU.add), reads=[t_hh, t_Xs], writes=[t_hh])
                for ri, dst in enumerate((s_re, s_im)):
                    dstv = dst.rearrange("s (g p) -> (s g) p", p=64)
                    for q in range(4):
                        stg, tstg = stage[st_i[0] % 2]; st_i[0] += 1
                        bk, tb = nb_e()
                        P.op("pe", C.transpose(out=bk[:, 0:128], in_=Hu[:, ri, 4 * q:4 * q + 4, :].rearrange("p s g -> p (s g)"), identity=ident[:]), reads=[t_hh, t_ident], writes=[tb])
                        P.op("dve", C.tensor_copy(out=stg[:, 0:64], in_=bk[:, 0:64]), reads=[tb], writes=[tstg])
                        out_events.append(store("pool", dstv[128 * q:128 * q + 128, :], stg[:, 0:64], tstg))
            ckpt('rec%d' % blk)
            yield
            def ssm_out(NC, U8_, tU8_, Spv_, tSpv_fn, yg_, tyg_, tYd_):
                for o in range(4):
                    yield
                    g0 = 8 * o
                    bk, tb = nb_e()
                    for gg in range(8):
                        g = g0 + gg
                        P.op("pe", C.matmul(bk[:, gg * NC:(gg + 1) * NC], lhsT=Wst[:, g, :], rhs=Spv_[:, g, :], start=True, stop=False), reads=[t_W, tSpv_fn(o)], writes=[tb], sig=False)
                        P.op("pe", C.matmul(bk[:, gg * NC:(gg + 1) * NC], lhsT=Kloc[:, g, :], rhs=U8_[:, g, :], start=False, stop=True), reads=[t_W, tU8_], writes=[tb], sig=(gg == 7))
                    yield
                    W_ = 8 * NC
                    ta, tta = tq_e(); tb2, ttb2 = tq_e()
                    def v3(ap): return ap[:, 0:W_].rearrange("p (a b) -> p a b", a=8)
                    P.op("act", C.activation(out=ta[:, 0:W_], in_=bk[:, 0:W_], func=AF.Copy), reads=[tb], writes=[tta])
                    P.op("act", C.activation(out=tb2[:, 0:W_], in_=bk[:, 0:W_], func=AF.Square), reads=[tb], writes=[ttb2])
                    P.op("act", C.activation(out=tb2[:, 0:W_], in_=tb2[:, 0:W_], func=AF.Identity, scale=0.044715, bias=onesf[:, 0:1]), reads=[ttb2, t_ones], writes=[ttb2])
                    yield
                    P.op("dve", C.tensor_tensor(out=tb2[:, 0:W_], in0=tb2[:, 0:W_], in1=ta[:, 0:W_], op=ALU.mult), reads=[tta, ttb2], writes=[ttb2])
                    yield
                    P.op("act", C.activation(out=tb2[:, 0:W_], in_=tb2[:, 0:W_], func=AF.Sigmoid, scale=2.0 * math.sqrt(2.0 / math.pi)), reads=[ttb2], writes=[ttb2])
                    yield
                    P.op("dve", C.tensor_tensor(out=yg_[:, g0:g0 + 8, :], in0=v3(ta), in1=v3(tb2), op=ALU.mult), reads=[tta, ttb2, tYd_], writes=[tyg_])
            yield from ssm_out(64, U8, t_U8, Spv, lambda o: t_Spv[o], yg8, t_yg8, t_Yd)
            yield
            Ydv = Yd.rearrange("(g c) i n -> i c g n", c=16)
            ysm = psem_q(dq)
            for i in range(8):
                P.op(dq, C.dma_start(out=Ydv[i, :, :, :], in_=yg8[16 * i:16 * i + 16, :, :]), reads=[t_yg8, t_yfe], writes=[t_Yd], dma=ysm)
            P.op(dq, C.dma_start(out=yfe[:, :, 0:512], in_=Yd.rearrange("(a p) i n -> p a (i n)", p=128)), reads=[t_Yd], writes=[t_yfe], dma=psem_q(dq))
            if has_s:
                yield from ssm_out(16, U8s, t_U8s, Spvs, lambda o: t_Spvs, yg8s, t_yg8s, t_Yds)
                Ydsv = Yds.rearrange("(g c) i n -> i c g n", c=16)
                ysm2 = psem_q(dq)
                for i in range(4, 8):
                    P.op(dq, C.dma_start(out=Ydsv[i, :, :, :], in_=yg8s[16 * i:16 * i + 16, :, :]), reads=[t_yg8s, t_yfe], writes=[t_Yds], dma=ysm2)
                yfs = psem_q(dq)
                for a in range(4):
                    P.op(dq, C.dma_start(out=yfe[:, a, 512:576].rearrange("p (t s) -> p t s", t=4), in_=Yds.rearrange("(a p) i n -> p a i n", p=128)[:, a, 4:8, :]), reads=[t_Yds], writes=[t_yfe], dma=yfs)
            if blk == DBLK: tap('yg8', yg8[:], t_yg8); tap('yfe', yfe[:], t_yfe)
            ckpt('ssm%d' % blk)

            yield
        def run_main(blk, nxt, prev_out):
            dq = "sp" if blk == 0 else "pool"
            def step_next():
                if nxt is not None:
                    next(nxt, None)
                    next(nxt, None)
            has_s = (blk == SBLK)
            parts = [(0, 512, False)] + ([(512, 64, True)] if has_s else [])
            def V(ap, smp):
                return ap.rearrange("p (t s) -> p t s", t=4) if smp else ap
            def MB(tab, idx, smp):
                if smp:
                    return tab[:, idx, 1:17].unsqueeze(1).to_broadcast([128, 4, 16])
                return tab[:, idx, 0:1].to_broadcast([128, 512])
            tiles = [(jj, False) for jj in range(4)] + ([(0, True)] if has_s else [])
            def rmsnorm(Atab, kind_shift, final=False):
                for (c0, NT, smp) in parts:
                    cs_ = slice(c0, c0 + NT)
                    for fc in range(8):
                        P.op("act", C.activation(out=sq[:, fc, cs_], in_=xT[:, fc, cs_], func=AF.Square), reads=[t_xT[fc]], writes=[t_sq])
                    bk, tb = nb()
                    for fc in range(8):
                        P.op("pe", C.matmul(bk[:, 0:NT], lhsT=onesb[:], rhs=sq[:, fc, cs_], start=(fc == 0), stop=(fc == 7)), reads=[t_sq, t_ones], writes=[tb], sig=(fc == 7))
                    tm, ttm = tq()
                    P.op("act", C.activation(out=tm[:, 0:NT], in_=bk[:, 0:NT], func=AF.Sqrt, scale=1.0 / D, bias=epsc[:, 0:1]), reads=[tb, t_eps], writes=[ttm])
                    P.op("dve", C.reciprocal(out=rstd[:, cs_], in_=tm[:, 0:NT]), reads=[ttm], writes=[t_rstd])
                    for fc in range(8):
                        tm, ttm = tq()
                        P.op("dve", C.tensor_tensor(out=tm[:, 0:NT], in0=xT[:, fc, cs_], in1=rstd[:, cs_], op=ALU.mult), reads=[t_xT[fc], t_rstd], writes=[ttm])
                        if not final:
                            P.op("dve", C.tensor_tensor(out=V(tm[:, 0:NT], smp), in0=V(tm[:, 0:NT], smp), in1=MB(Atab, fc, smp), op=ALU.mult), reads=[t_A, ttm], writes=[ttm])
                            P.op("dve", C.tensor_tensor(out=V(h[:, fc, cs_], smp), in0=V(tm[:, 0:NT], smp), in1=MB(mod, kind_shift * 8 + fc, smp), op=ALU.add), reads=[t_mod, ttm], writes=[t_h])
                        else:
                            P.op("dve", C.tensor_scalar(out=xT[:, fc, cs_], in0=tm[:, 0:NT], scalar1=gf[:, fc:fc + 1], scalar2=None, op0=ALU.mult), reads=[ttm, t_g], writes=[t_xT[fc]])

            def linear(kname, K, evac, src, t_src, gen=False):
                for i0 in range(0, 8, 2):
                    sl, tsl = wl((kname, i0))
                    for ci in range(2):
                        for (c0, NT, smp) in parts:
                            bk, tb = nb()
                            for k in range(K):
                                P.op("pe", C.matmul(bk[:, 0:NT], lhsT=sl[:, k, ci * 128:(ci + 1) * 128], rhs=src[:, k, c0:c0 + NT], start=(k == 0), stop=(k == K - 1)),
                                     reads=[tsl, t_src], writes=[tb], sig=(k == K - 1))
                            evac(i0 + ci, bk, tb, c0, NT, smp)
                    if gen:
                        yield

            def linear_now(kname, K, evac, src, t_src, after=None):
                for _ in linear(kname, K, evac, src, t_src, True):
                    if after is not None:
                        after()
            for m in range(8):
                slg, tslg = wl(("g", m))
                sll, tsll = wl(("glu", m))
                for (c0, NT, smp) in parts:
                    cs_ = slice(c0, c0 + NT)
                    bgs, tgs = nb(); bgp, tgp = nb(); bla, tla = nb(); blb, tlb = nb(); bz, tz = nb()
                    for k in range(8):
                        P.op("pe", C.matmul(bgs[:, 0:NT], lhsT=slg[:, k, 0:128], rhs=hB[:, k, cs_], start=(k == 0), stop=(k == 7)), reads=[tslg, t_hB], writes=[tgs], sig=(k == 7))
                    for k in range(8):
                        P.op("pe", C.matmul(bgp[:, 0:NT], lhsT=slg[:, k, 128:256], rhs=hB[:, k, cs_], start=(k == 0), stop=(k == 7)), reads=[tslg, t_hB], writes=[tgp], sig=(k == 7))
                    for k in range(4):
                        P.op("pe", C.matmul(bla[:, 0:NT], lhsT=sll[:, k, 0:128], rhs=yfe[:, k, cs_], start=(k == 0), stop=(k == 3)), reads=[tsll, t_yfe], writes=[tla], sig=(k == 3))
                    for k in range(4):
                        P.op("pe", C.matmul(blb[:, 0:NT], lhsT=sll[:, k, 128:256], rhs=yfe[:, k, cs_], start=(k == 0), stop=(k == 3)), reads=[tsll, t_yfe], writes=[tlb], sig=(k == 3))
                    P.op("pe", C.matmul(bz[:, 0:NT], lhsT=wpl[:, m // 2, (m % 2) * 128:(m % 2) * 128 + 128], rhs=pooled[:, m // 2, cs_], start=True, stop=True), reads=[t_wpl, t_pooled], writes=[tz])
                    t1, tt1 = tq(); t2, tt2 = tq(); t3, tt3 = tq()
                    P.op("act", C.activation(out=t1[:, 0:NT], in_=bgs[:, 0:NT], func=AF.Sigmoid), reads=[tgs], writes=[tt1])
                    P.op("act", C.activation(out=t2[:, 0:NT], in_=bgp[:, 0:NT], func=AF.Sigmoid), reads=[tgp], writes=[tt2])
                    P.op("act", C.activation(out=t3[:, 0:NT], in_=blb[:, 0:NT], func=AF.Sigmoid), reads=[tlb], writes=[tt3])
                    P.op("dve", C.tensor_tensor(out=t3[:, 0:NT], in0=t3[:, 0:NT], in1=bla[:, 0:NT], op=ALU.mult), reads=[tla, tt3], writes=[tt3])
                    P.op("dve", C.tensor_tensor(out=t3[:, 0:NT], in0=t3[:, 0:NT], in1=t1[:, 0:NT], op=ALU.mult), reads=[tt1, tt3], writes=[tt3])
                    P.op("dve", C.scalar_tensor_tensor(out=t2[:, 0:NT], in0=bz[:, 0:NT], scalar=psc[:, m:m + 1], in1=t2[:, 0:NT], op0=ALU.mult, op1=ALU.mult), reads=[tz, tt2, t_g], writes=[tt2])
                    P.op("dve", C.tensor_tensor(out=merged[:, m, cs_], in0=t3[:, 0:NT], in1=t2[:, 0:NT], op=ALU.add), reads=[tt2, tt3], writes=[t_merged])
            if blk == DBLK: tap('merged', merged[:], t_merged)
            ckpt('mrg%d' % blk)

            if prev_out is not None:
                prev_out()
            tiles = [(jj, False) for jj in range(4)] + ([(0, True)] if has_s else [])
            for (jj, smp) in tiles:
                stg, tstg = stage[st_i[0] % 2]; st_i[0] += 1
                rows = 64 if smp else 128
                c0 = 512 if smp else jj * 128
                if smp:
                    for t in range(4):
                        load(dq, stg[16 * t:16 * t + 16, :], xsv[t, :, :], tstg)
                else:
                    for j2 in range(2):
                        load(dq, stg[64 * j2:64 * j2 + 64, :], xpv[2 * jj + j2, 64 * blk:64 * blk + 64, :], tstg)
                for half in range(2):
                    bk, tb = nb()
                    for f4 in range(4):
                        fc = half * 4 + f4
                        P.op("pe", C.transpose(out=bk[:, f4 * 128:(f4 + 1) * 128], in_=stg[:, fc * 128:(fc + 1) * 128], identity=ident[:]),
                             reads=[tstg, t_ident], writes=[tb], sig=(f4 == 3))
                    P.op("act", C.activation(out=xT[:, half * 4:half * 4 + 4, c0:c0 + rows], in_=bk[:, :].rearrange("p (a b) -> p a b", a=4)[:, :, 0:rows], func=AF.Copy),
                         reads=[tb], writes=t_xT[half * 4:half * 4 + 4])
            def evac_res(kind_gate):
                def f(mc, bk, tb, c0, NT, smp):
                    if not smp:
                        P.op("dve", C.scalar_tensor_tensor(out=xT[:, mc, c0:c0 + NT], in0=bk[:, 0:NT], scalar=mod[:, kind_gate * 8 + mc, 0:1], in1=xT[:, mc, c0:c0 + NT], op0=ALU.mult, op1=ALU.add), reads=[tb, t_mod, t_xT[mc]], writes=[t_xT[mc]])
                    else:
                        tm, ttm = tq()
                        P.op("dve", C.tensor_tensor(out=V(tm[:, 0:NT], smp), in0=V(bk[:, 0:NT], smp), in1=MB(mod, kind_gate * 8 + mc, smp), op=ALU.mult), reads=[tb, t_mod], writes=[ttm])
                        P.op("dve", C.tensor_tensor(out=xT[:, mc, c0:c0 + NT], in0=xT[:, mc, c0:c0 + NT], in1=tm[:, 0:NT], op=ALU.add), reads=[ttm, t_xT[mc]], writes=[t_xT[mc]])
                return f
            linear_now("out", 8, evac_res(2), merged, t_merged, step_next)
            if blk == DBLK: tap('x1', xT[:], t_xT[7])
            ckpt('wout%d' % blk)

            rmsnorm(A2, 3)

            for fh in range(2):
                for fl in range(11):
                    ff = fh * 11 + fl
                    sl, tsl = wl(("ffi", ff))
                    for (c0, NT, smp) in parts:
                        cs_ = slice(c0, c0 + NT)
                        ba, ta = nb(); bb, tbb_ = nb()
                        for k in range(8):
                            P.op("pe", C.matmul(ba[:, 0:NT], lhsT=sl[:, k, 0:128], rhs=h[:, k, cs_], start=(k == 0), stop=(k == 7)), reads=[tsl, t_h], writes=[ta], sig=(k == 7))
                        for k in range(8):
                            P.op("pe", C.matmul(bb[:, 0:NT], lhsT=sl[:, k, 128:256], rhs=h[:, k, cs_], start=(k == 0), stop=(k == 7)), reads=[tsl, t_h], writes=[tbb_], sig=(k == 7))
                        tm, ttm = tq()
                        P.op("act", C.activation(out=tm[:, 0:NT], in_=ba[:, 0:NT], func=AF.Silu), reads=[ta], writes=[ttm])
                        P.op("dve", C.tensor_tensor(out=actb[:, fl, cs_], in0=tm[:, 0:NT], in1=bb[:, 0:NT], op=ALU.mult), reads=[tbb_, ttm], writes=[t_act])
                    step_next()
                if blk == DBLK and fh == 0: tap('act', actb[:], t_act)
                for m in range(8):
                    sl, tsl = wl(("ffo", fh, m))
                    for (c0, NT, smp) in parts:
                        bk, tb = nb()
                        for k in range(11):
                            P.op("pe", C.matmul(bk[:, 0:NT], lhsT=sl[:, k, :], rhs=actb[:, k, c0:c0 + NT], start=(k == 0), stop=(k == 10)), reads=[tsl, t_act], writes=[tb], sig=(k == 10))
                        evac_res(5)(m, bk, tb, c0, NT, smp)
                    step_next()
            if nxt is not None:
                for _ in nxt:
                    pass
            if blk == DBLK: tap('x2', xT[:], t_xT[7])
            ckpt('ffo%d' % blk)

            rmsnorm(None, 0, final=True)
            ckpt('fn%d' % blk)
            def out_fn():
                for (jj, smp) in tiles:
                    rows = 64 if smp else 128
                    c0 = 512 if smp else jj * 128
                    stg, tstg = stage[st_i[0] % 2]; st_i[0] += 1
                    for half in range(2):
                        bk, tb = nb()
                        for f4 in range(4):
                            fc = half * 4 + f4
                            cw = 64 if smp else 128
                            P.op("pe", C.transpose(out=bk[0:cw, f4 * 128:(f4 + 1) * 128], in_=xT[:, fc, c0:c0 + cw], identity=ident[:]), reads=[t_xT[fc], t_ident], writes=[tb], sig=(f4 == 3))
                        P.op("act", C.activation(out=stg[0:rows, half * 512:(half + 1) * 512], in_=bk[0:rows, :], func=AF.Copy), reads=[tb], writes=[tstg])
                    if smp:
                        for t in range(4):
                            out_events.append(store("pool", ysv[t, :, :], stg[16 * t:16 * t + 16, :], tstg))
                    else:
                        for j2 in range(2):
                            out_events.append(store("pool", ypv[2 * jj + j2, 64 * blk:64 * blk + 64, :], stg[64 * j2:64 * j2 + 64, :], tstg))

            return out_fn
        NBM[0] = 6
        g0 = early(0)
        g0_state = [None]
        def step_g0(n):
            for _ in range(n):
                if g0_state[0] in ("ssm", "end"):
                    return
                g0_state[0] = next(g0, "end")
        for go in range(4):
            gs = slice(8 * go, 8 * go + 8)
            def cmul4(outr, outi, xr, xi, yr, yi, nk):
                def bx(a): return a.unsqueeze(2).to_broadcast([64, 8, nk, 16])
                def by(a): return a.unsqueeze(3).to_broadcast([64, 8, nk, 16])
                dv(C.tensor_tensor(out=v1[:, :, 0:nk, :], in0=bx(xr), in1=by(yr), op=ALU.mult))
                dv(C.tensor_tensor(out=v2[:, :, 0:nk, :], in0=bx(xi), in1=by(yi), op=ALU.mult))
                dv(C.tensor_tensor(out=outr, in0=v1[:, :, 0:nk, :], in1=v2[:, :, 0:nk, :], op=ALU.subtract))
                dv(C.tensor_tensor(out=v1[:, :, 0:nk, :], in0=bx(xr), in1=by(yi), op=ALU.mult))
                dv(C.tensor_tensor(out=v2[:, :, 0:nk, :], in0=bx(xi), in1=by(yr), op=ALU.mult))
                dv(C.tensor_tensor(out=outi, in0=v1[:, :, 0:nk, :], in1=v2[:, :, 0:nk, :], op=ALU.add))
            cmul4(Er[0:64, :, 0:7, :], Ei[0:64, :, 0:7, :], Bbr[:, gs, :], Bbi[:, gs, :], PWrev[:, 0, gs, 0:7], PWrev[:, 1, gs, 0:7], 7)
            dv(C.tensor_copy(out=Er[0:64, :, 7, :], in_=Bbr[:, gs, :])); dv(C.tensor_copy(out=Ei[0:64, :, 7, :], in_=Bbi[:, gs, :]))
            cmul4(Rr[0:64, :, 1:8, :], Ri[0:64, :, 1:8, :], Bbr[:, gs, :], Bbi[:, gs, :], NPW[:, 0, gs, 1:8], NPW[:, 1, gs, 1:8], 7)
            dv(C.tensor_copy(out=Rr[0:64, :, 0, :], in_=Bbr[:, gs, :])); dv(C.tensor_copy(out=Ri[0:64, :, 0, :], in_=Bbi[:, gs, :]))
            cmul4(Qr[0:64, :, 1:9, :], Qi[0:64, :, 1:9, :], Cr[:, gs, :], Ci[:, gs, :], PW[:, 0, gs, 1:9], PW[:, 1, gs, 1:9], 8)
            dv(C.tensor_copy(out=Qr[0:64, :, 0, :], in_=Cr[:, gs, :])); dv(C.tensor_copy(out=Qi[0:64, :, 0, :], in_=Ci[:, gs, :]))
            dv(C.tensor_scalar(out=nQi[0:64].rearrange("p g k c -> p (g k c)"), in0=Qi[0:64].rearrange("p g k c -> p (g k c)"), scalar1=-1.0, scalar2=None, op0=ALU.mult))
            step_g0(10)
            for gl in range(8):
                g = 8 * go + gl
                P.op("dve", C.tensor_copy(out=Wst[0:64, g, :].rearrange("p (k c) -> p k c", k=8), in_=Qr[0:64, gl, 1:9, :]), reads=[t_ssm], writes=[t_W])
                P.op("act", C.activation(out=Wst[64:128, g, :].rearrange("p (k c) -> p k c", k=8), in_=nQi[0:64, gl, 1:9, :], func=AF.Copy), reads=[t_ssm], writes=[t_W])
            for gl in range(8):
                g = 8 * go + gl
                bk, tb = nb()
                P.op("pe", C.transpose(out=bk[:, 0:128], in_=Er[:, gl, :, :].rearrange("p j c -> p (j c)"), identity=ident[:]), reads=[t_ssm, t_ident], writes=[tb], sig=False)
                P.op("pe", C.transpose(out=bk[:, 128:256], in_=Ei[:, gl, :, :].rearrange("p j c -> p (j c)"), identity=ident[:]), reads=[t_ssm, t_ident], writes=[tb], sig=False)
                P.op("pe", C.matmul(bk[:, 256:384], lhsT=Rr[:, gl, :, :].rearrange("p j c -> p (j c)"), rhs=Qr[:, gl, 0:8, :].rearrange("p k c -> p (k c)"), start=True, stop=False), reads=[t_ssm], writes=[tb], sig=False)
                P.op("pe", C.matmul(bk[:, 256:384], lhsT=Ri[:, gl, :, :].rearrange("p j c -> p (j c)"), rhs=nQi[:, gl, 0:8, :].rearrange("p k c -> p (k c)"), start=False, stop=True), reads=[t_ssm], writes=[tb], sig=True)
                P.op("act", C.activation(out=Wend[:, g, :].rearrange("p (a b) -> p a b", a=2), in_=bk[:, 0:256].rearrange("p (a b) -> p a b", a=2)[:, :, 0:64], func=AF.Copy), reads=[tb], writes=[t_W])
                P.op("dve", C.tensor_tensor(out=kt[:], in0=bk[:, 256:384], in1=mask[:], op=ALU.mult), reads=[tb, t_mask], writes=[t_kt])
                P.op("dve", C.scalar_tensor_tensor(out=Kloc[:, g, :], in0=ident[:], scalar=D8[:, g:g + 1], in1=kt[:], op0=ALU.mult, op1=ALU.add), reads=[t_kt, t_ident, t_D8], writes=[t_W])
        dv(C.tensor_tensor(out=tA[:], in0=PW[:, 0, :, 8], in1=PW[:, 0, :, 8], op=ALU.mult))
        dv(C.tensor_tensor(out=tB[:], in0=PW[:, 1, :, 8], in1=PW[:, 1, :, 8], op=ALU.mult))
        dv(C.tensor_tensor(out=tA[:], in0=tA[:], in1=tB[:], op=ALU.add))
        ac(C.activation(out=RHO[:], in_=tA[:], func=AF.Sqrt))
        dv(C.reciprocal(out=tC[:], in_=RHO[:]))
        ur = sbt([64, G], "ur"); ui = sbt([64, G], "ui")
        dv(C.tensor_tensor(out=ur[:], in0=PW[:, 0, :, 8], in1=tC[:], op=ALU.mult))
        dv(C.tensor_tensor(out=ui[:], in0=PW[:, 1, :, 8], in1=tC[:], op=ALU.mult))
        dv(C.memset(ROT[:, 0, :, 0], 1.0)); dv(C.memset(ROT[:, 1, :, 0], 0.0))
        w1 = sbt([64, G, 32], "w1"); w2 = sbt([64, G, 32], "w2")
        for lvl in range(6):
            n0 = 1 << lvl
            def bu(a): return a.unsqueeze(2).to_broadcast([64, G, n0])
            src_r = ROT[:, 0, :, 0:n0]; src_i = ROT[:, 1, :, 0:n0]
            dv(C.tensor_tensor(out=w1[:, :, 0:n0], in0=src_r, in1=bu(ur[:]), op=ALU.mult))
            dv(C.tensor_tensor(out=w2[:, :, 0:n0], in0=src_i, in1=bu(ui[:]), op=ALU.mult))
            dv(C.tensor_tensor(out=ROT[:, 0, :, n0:2 * n0], in0=w1[:, :, 0:n0], in1=w2[:, :, 0:n0], op=ALU.subtract))
            dv(C.tensor_tensor(out=w1[:, :, 0:n0], in0=src_r, in1=bu(ui[:]), op=ALU.mult))
            dv(C.tensor_tensor(out=w2[:, :, 0:n0], in0=src_i, in1=bu(ur[:]), op=ALU.mult))
            dv(C.tensor_tensor(out=ROT[:, 1, :, n0:2 * n0], in0=w1[:, :, 0:n0], in1=w2[:, :, 0:n0], op=ALU.add))
            if lvl < 5:
                dv(C.tensor_tensor(out=tA[:], in0=ur[:], in1=ur[:], op=ALU.mult))
                dv(C.tensor_tensor(out=tB[:], in0=ui[:], in1=ui[:], op=ALU.mult))
                dv(C.tensor_tensor(out=tC[:], in0=ur[:], in1=ui[:], op=ALU.mult))
                dv(C.tensor_tensor(out=ur[:], in0=tA[:], in1=tB[:], op=ALU.subtract))
                dv(C.tensor_scalar(out=ui[:], in0=tC[:], scalar1=2.0, scalar2=None, op0=ALU.mult))
        dv(C.tensor_copy(out=AR[:, 0, :], in_=ROT[:, 0, :, 1])); dv(C.tensor_copy(out=AR[:, 1, :], in_=ROT[:, 0, :, 1]))
        dv(C.tensor_scalar(out=AIs[:, 0, :], in0=ROT[:, 1, :, 1], scalar1=-1.0, scalar2=None, op0=ALU.mult)); dv(C.tensor_copy(out=AIs[:, 1, :], in_=ROT[:, 1, :, 1]))

        adaln_A(A2, g2, 4)
        tap('mod', mod[:], t_mod); tap('A1', A1[:], t_A)
        ckpt('adaln')
        tap('PW', PW[:], t_ssm); tap('NPW', NPW[:], t_ssm); tap('Wend', Wend[:], t_W); tap('Kloc', Kloc[:], t_W); tap('Wst', Wst[:], t_W); tap('AR', AR[:], t_ssm); tap('D8', D8[:], t_D8)
        ckpt('ssmpre')
        step_g0(1000)
        bar = P.op("dve", C.memset(tA[:], 0.0), reads=[], writes=[t_ssm, t_c, t_sg, t_silu, t_W])
        bar2 = P.op("dve", C.memset(tB[:], 0.0), reads=[], writes=[t_ssm])
        P.es2 = None
        es2.close()
        _DEFW[0] = bar2
        S = sbt([64, 2, G], "S"); t_S = T()
        P.op("dve", C.memset(S[:], 0.0), writes=[t_S])
        wpl = P.sb([128, 4, 256], BF16, "wpl"); t_wpl = T()
        wps = P.newsem("wpsem")
        P.op("pool", C.dma_start(out=wpl[:], in_=w_pool.rearrange("g c o -> c g o")), writes=[t_wpl], dma=wps)

        xT = sbt([128, 8, NTT], "xT"); t_xT = [T() for _ in range(8)]
        h = P.sb([128, 8, NTT], BF16, "h"); t_h = T()
        rstd = sbt([128, NTT], "rstd"); t_rstd = T()
        tq_bufs = [(sbt([128, NTT], "tq%d" % i), T()) for i in range(3)]
        tq_i = [0]
        def tq():
            b = tq_bufs[tq_i[0] % len(tq_bufs)]; tq_i[0] += 1
            return b
        Spv = P.sb([128, G, 64], BF16, "Spv"); t_Spv = [T() for _ in range(4)]
        Spvs = P.sb([128, G, 16], BF16, "Spvs"); t_Spvs = T()
        yg8 = P.sb([128, G, 64], BF16, "yg8"); t_yg8 = T()
        yg8s = P.sb([128, G, 16], BF16, "yg8s"); t_yg8s = T()
        yfe = usm; t_yfe = t_usm
        merged = P.sb([128, 8, NTT], BF16, "merged"); t_merged = T()
        sq = merged; t_sq = t_merged
        actb = P.sb([128, 11, NTT], BF16, "actb"); t_act = T()
        ZB = [dict(Zin=sbt([64, 2, 8, 64], "zin%d" % i),
                   Z=sbt([64, 2, 8, 64], "zz%d" % i), Sf=sbt([64, 2, 8, 64], "zsf%d" % i),
                   t=T(), tz=[[T() for _ in range(8)] for _ in range(2)]) for i in range(1)]
        Zm1 = sbt([64, 2, G], "Zm1"); t_Zm1 = T()
        S1 = sbt([64, 2, G], "S1"); S2 = sbt([64, 2, G], "S2"); t_ssc = T()
        Xs = ZB[0]["Zin"][:].rearrange("p a b c -> p (a b c)").rearrange("p (r g s) -> p r g s", r=2, g=G); t_Xs = ZB[0]["t"]
        H0 = ZB[0]["Z"][:].rearrange("p a b c -> p (a b c)").rearrange("p (r s g) -> p r s g", r=2, g=G); t_H0 = ZB[0]["t"]
        Ht = ZB[0]["Sf"][:].rearrange("p a b c -> p (a b c)").rearrange("p (r s g) -> p r s g", r=2, g=G); Hu = sbt([128, 2, 16, G], "Hu"); t_ssc0 = T()
        P.op("dve", C.memset(Hu[:].rearrange("p a b c -> p (a b c)"), 0.0), writes=[t_ssc0])


        for _ in g0:
            pass
        prev_out = None
        for blk in range(4):
            nxt = early(blk + 1) if blk < 3 else None
            prev_out = run_main(blk, nxt, prev_out)
        prev_out()
        fw = {}
        for (s, v) in out_events:
            fw[s] = max(fw.get(s, 0), v)
        P.emit([(s, v) for s, v in fw.items()])
    return nc


_NC = None


def kernel(**inputs):
    global _NC
    f = lambda k: np.ascontiguousarray(inputs[k], dtype=np.float32)
    x_prompt = f("x_prompt"); x_sample = f("x_sample"); c_prompt = f("c_prompt"); c_sample = f("c_sample")
    sre = f("state_ssm_re")[0].reshape(128, 2048); sim = f("state_ssm_im")[0].reshape(128, 2048); spool = f("state_pool")[0].reshape(128, 15 * 512)
    shared = dict(
        norm1_g=f("norm1_g")[0], norm2_g=f("norm2_g")[0], normf_g=f("normf_g"), w_ada=f("w_ada")[0], b_ada=f("b_ada")[0], w_in=f("w_in")[0],
        lam_re=f("ssm_lam_re")[0], lam_im=f("ssm_lam_im")[0], log_dt=f("ssm_log_dt"), b_re=f("ssm_b_re")[0], b_im=f("ssm_b_im")[0],
        c_re=f("ssm_c_re")[0].reshape(512, 64), c_im=f("ssm_c_im")[0].reshape(512, 64), ssm_d=f("ssm_d")[0], w_glu=f("w_glu")[0],
        w_pool=f("w_pool")[0], pool_scale=f("pool_scale")[0], w_out=f("w_out")[0], w_ffn_in=f("w_ffn_in")[0], w_ffn_out=f("w_ffn_out")[0])
    in_maps = []
    for i in range(8):
        m = dict(shared)
        m["xp"] = x_prompt[i]
        m["xs"] = x_sample[16 * i:16 * i + 16].reshape(64, D)
        m["cc"] = np.concatenate([c_prompt[i:i + 1], c_sample[16 * i:16 * i + 16]], 0)
        m["st_re"] = sre[16 * i:16 * i + 16]; m["st_im"] = sim[16 * i:16 * i + 16]
        m["st_pool"] = spool[16 * i:16 * i + 16].reshape(240, 512)
        in_maps.append(m)
    if _NC is None:
        _NC = build_nc()
    res = run_bass_kernel_spmd(_NC, in_maps, core_ids=list(range(8)))
    R = res.results
    y_prompt = np.stack([r["yp"] for r in R], 0)
    y_sample = np.concatenate([r["ys"].reshape(16, 4, D) for r in R], 0)
    p_re = np.stack([r["p_re"].reshape(G, 64) for r in R], 0)[None]
    p_im = np.stack([r["p_im"].reshape(G, 64) for r in R], 0)[None]
    p_pool = np.stack([r["p_pool"] for r in R], 0)[None]
    s_re = np.concatenate([r["s_re"].reshape(16, G, 64) for r in R], 0)[None]
    s_im = np.concatenate([r["s_im"].reshape(16, G, 64) for r in R], 0)[None]
    s_pool = np.concatenate([r["s_pool"].reshape(16, 15, 512) for r in R], 0)[None]
    return (y_prompt.astype(np.float32), y_sample.astype(np.float32), p_re.astype(np.float32), p_im.astype(np.float32),
            p_pool.astype(np.float32), s_re.astype(np.float32), s_im.astype(np.float32), s_pool.astype(np.float32))
```
